# Optimizing a Trainium2 kernel written in Bass

```python
import jax, jax.numpy as jnp
from jax import lax
import numpy as np

D_MODEL = 1024
BATCH = 32
SEQ = 2048
DEPTH = 4

N_NSA_HEADS = 8
N_NSA_KV = 2
NSA_GROUP = N_NSA_HEADS // N_NSA_KV
N_SB_HEADS = 4
N_FOX_HEADS = 4
HEAD_DIM = D_MODEL // (N_NSA_HEADS + N_SB_HEADS + N_FOX_HEADS)
MIX_WIDTH = (N_NSA_HEADS + N_SB_HEADS + N_FOX_HEADS) * HEAD_DIM
ROPE_DIM = HEAD_DIM // 4
ROPE_THETA = 500000.0
CMP_BLOCK = 32
CMP_STRIDE = 16
CMP_HIDDEN = HEAD_DIM
SEL_BLOCK = 64
SEL_TOPK = 16
WINDOW = 512
Q_BLOCK = 128
SEL_Q_BLOCK = 32
MEM_LEN = 256
N_MEM_HEADS = 4
MEM_HEAD_DIM = 64
D_FF = 2816
CONV_WIDTH = 3
EPS = 1e-6

NSA_Q = N_NSA_HEADS * HEAD_DIM
NSA_KV = N_NSA_KV * HEAD_DIM
NSA_GATES = 3 * N_NSA_HEADS
SB_W = N_SB_HEADS * HEAD_DIM
FOX_W = N_FOX_HEADS * HEAD_DIM
IN_COLS = NSA_Q + 6 * NSA_KV + NSA_GATES + 3 * SB_W + 3 * FOX_W + N_FOX_HEADS

kernel_name = "hybrid_nsa_stickbreak_fox_block"


def rms_norm(x, g):
    xf = x.astype(jnp.float32)
    y = xf * lax.rsqrt(jnp.mean(xf * xf, axis=-1, keepdims=True) + EPS)
    return (y * g.astype(jnp.float32)).astype(x.dtype)


def partial_rope(x, pos):
    half = ROPE_DIM // 2
    inv = ROPE_THETA ** (-jnp.arange(half, dtype=jnp.float32) / half)
    ang = pos.astype(jnp.float32)[:, None] * inv[None, :]
    cos = jnp.cos(ang)[None, :, None, :]
    sin = jnp.sin(ang)[None, :, None, :]
    xf = x.astype(jnp.float32)
    x1, x2, rest = xf[..., :half], xf[..., half:ROPE_DIM], xf[..., ROPE_DIM:]
    out = jnp.concatenate([x1 * cos - x2 * sin, x2 * cos + x1 * sin, rest], axis=-1)
    return out.astype(x.dtype)


def masked_softmax(logits, mask):
    logits = jnp.where(mask, logits, -jnp.inf)
    m = jnp.max(logits, axis=-1, keepdims=True)
    m = jnp.where(jnp.isfinite(m), m, 0.0)
    p = jnp.exp(logits - m)
    return p / jnp.maximum(jnp.sum(p, axis=-1, keepdims=True), 1e-30)


def split_columns(proj):
    sizes = [NSA_Q] + [NSA_KV] * 6 + [NSA_GATES] + [SB_W] * 3 + [FOX_W] * 3 + [N_FOX_HEADS]
    out, off = [], 0
    for s in sizes:
        out.append(proj[..., off:off + s])
        off += s
    return out


def nsa_mixer(q, k_cmp, v_cmp, k_sel, v_sel, k_win, v_win, gates, pe_k, pe_v, wk1, wk2, wv1, wv2):
    B, S = q.shape[0], q.shape[1]
    G, Hg, Dh = N_NSA_KV, NSA_GROUP, HEAD_DIM
    f32 = jnp.float32
    scale = Dh ** -0.5
    qg = q.astype(f32).reshape(B, S, G, Hg, Dh).transpose(0, 2, 3, 1, 4)
    t = jnp.arange(S)

    n_cmp = (S - CMP_BLOCK) // CMP_STRIDE + 1
    starts = np.arange(n_cmp) * CMP_STRIDE
    blk = starts[:, None] + np.arange(CMP_BLOCK)[None, :]

    def compress(kv, pe, w1, w2):
        kb = kv.astype(f32)[:, blk] + pe.astype(f32)[None, None, :, None, :]
        hid = jax.nn.silu(jnp.einsum('bnlgd,ldc->bngc', kb, w1.astype(f32)))
        return jnp.einsum('bngc,ce->bgne', hid, w2.astype(f32))

    kc = compress(k_cmp, pe_k, wk1, wk2)
    vc = compress(v_cmp, pe_v, wv1, wv2)
    cmp_mask = (starts + CMP_BLOCK - 1)[None, :] <= t[:, None]
    p_cmp = masked_softmax(jnp.einsum('bghsd,bgnd->bghsn', qg, kc) * scale, cmp_mask)
    o_cmp = jnp.einsum('bghsn,bgnd->bghsd', p_cmp, vc)

    n_sel = S // SEL_BLOCK
    top_n = min(SEL_TOPK, n_sel)
    sel_starts = np.arange(n_sel) * SEL_BLOCK
    overlap = ((starts[:, None] < sel_starts[None, :] + SEL_BLOCK)
               & (starts[:, None] + CMP_BLOCK > sel_starts[None, :])).astype(np.float32)
    imp = jnp.einsum('bghsn,nj->bgsj', p_cmp, jnp.asarray(overlap))
    cur = (t // SEL_BLOCK)[:, None]
    bid = jnp.arange(n_sel)[None, :]
    forced = (bid == 0) | (bid == cur) | (bid == cur - 1)
    imp = jnp.where(bid <= cur, jnp.where(forced, jnp.inf, imp), -jnp.inf)
    top_val, top_idx = lax.top_k(imp, top_n)
    top_ok = top_val > -jnp.inf

    ks_b = k_sel.astype(f32).transpose(0, 2, 1, 3).reshape(B, G, n_sel, SEL_BLOCK, Dh)
    vs_b = v_sel.astype(f32).transpose(0, 2, 1, 3).reshape(B, G, n_sel, SEL_BLOCK, Dh)
    bi = jnp.arange(B)[:, None, None, None]
    gi = jnp.arange(G)[None, :, None, None]
    n_qc = S // SEL_Q_BLOCK

    def sel_chunk(args):
        q_c, idx_c, ok_c, t_c = args
        k_g = ks_b[bi, gi, idx_c]
        v_g = vs_b[bi, gi, idx_c]
        logits = jnp.einsum('bghqd,bgqnld->bghqnl', q_c, k_g) * scale
        tok = idx_c[..., None] * SEL_BLOCK + jnp.arange(SEL_BLOCK)
        mask = (tok <= t_c[None, None, :, None, None]) & ok_c[..., None]
        shp = logits.shape
        p = masked_softmax(logits.reshape(shp[:4] + (-1,)),
                           mask.reshape(B, G, 1, SEL_Q_BLOCK, -1))
        return jnp.einsum('bghqnl,bgqnld->bghqd', p.reshape(shp), v_g)

    q_chunks = jnp.moveaxis(qg.reshape(B, G, Hg, n_qc, SEL_Q_BLOCK, Dh), 3, 0)
    idx_chunks = jnp.moveaxis(top_idx.reshape(B, G, n_qc, SEL_Q_BLOCK, top_n), 2, 0)
    ok_chunks = jnp.moveaxis(top_ok.reshape(B, G, n_qc, SEL_Q_BLOCK, top_n), 2, 0)
    o_slc = lax.map(sel_chunk, (q_chunks, idx_chunks, ok_chunks, t.reshape(n_qc, SEL_Q_BLOCK)))
    o_slc = jnp.moveaxis(o_slc, 0, 3).reshape(B, G, Hg, S, Dh)

    pad = ((0, 0), (0, 0), (WINDOW, 0), (0, 0))
    kw_p = jnp.pad(k_win.astype(f32).transpose(0, 2, 1, 3), pad)
    vw_p = jnp.pad(v_win.astype(f32).transpose(0, 2, 1, 3), pad)
    span = WINDOW + Q_BLOCK
    n_qb = S // Q_BLOCK

    def win_block(args):
        q_b, i = args
        start = i * Q_BLOCK
        k_b = lax.dynamic_slice_in_dim(kw_p, start, span, axis=2)
        v_b = lax.dynamic_slice_in_dim(vw_p, start, span, axis=2)
        tq = start + jnp.arange(Q_BLOCK)
        kpos = start - WINDOW + jnp.arange(span)
        diff = tq[:, None] - kpos[None, :]
        mask = (diff >= 0) & (diff < WINDOW) & (kpos[None, :] >= 0)
        p = masked_softmax(jnp.einsum('bghqd,bgkd->bghqk', q_b, k_b) * scale, mask)
        return jnp.einsum('bghqk,bgkd->bghqd', p, v_b)

    qb = jnp.moveaxis(qg.reshape(B, G, Hg, n_qb, Q_BLOCK, Dh), 3, 0)
    o_win = lax.map(win_block, (qb, jnp.arange(n_qb)))
    o_win = jnp.moveaxis(o_win, 0, 3).reshape(B, G, Hg, S, Dh)

    g = gates.astype(f32).reshape(B, S, G, Hg, 3).transpose(0, 2, 3, 1, 4)
    o = g[..., 0:1] * o_cmp + g[..., 1:2] * o_slc + g[..., 2:3] * o_win
    return o.transpose(0, 3, 1, 2, 4).reshape(B, S, N_NSA_HEADS, Dh)


def stick_breaking_attention(q, k, v):
    B, S, H, Dh = q.shape
    f32 = jnp.float32
    scale = Dh ** -0.5
    qf = q.astype(f32).transpose(0, 2, 1, 3)
    kf = k.astype(f32).transpose(0, 2, 1, 3)
    vf = v.astype(f32).transpose(0, 2, 1, 3)
    n_qb = S // Q_BLOCK
    s_pos = jnp.arange(S)

    def blk(args):
        q_b, i = args
        tq = i * Q_BLOCK + jnp.arange(Q_BLOCK)
        z = jnp.einsum('bhqd,bhkd->bhqk', q_b, kf) * scale
        strict = s_pos[None, :] < tq[:, None]
        log_beta = jax.nn.log_sigmoid(z)
        log_1m = jnp.where(strict, log_beta - z, 0.0)
        excl = lax.cumsum(log_1m, axis=3, reverse=True) - log_1m
        a = jnp.where(strict, jnp.exp(log_beta + excl), 0.0)
        return jnp.einsum('bhqk,bhkd->bhqd', a, vf)

    qb = jnp.moveaxis(qf.reshape(B, H, n_qb, Q_BLOCK, Dh), 2, 0)
    o = lax.map(blk, (qb, jnp.arange(n_qb)))
    return jnp.moveaxis(o, 0, 2).reshape(B, H, S, Dh).transpose(0, 2, 1, 3)


def forgetting_attention(q, k, v, f_logit):
    B, S, H, Dh = q.shape
    f32 = jnp.float32
    scale = Dh ** -0.5
    qf = q.astype(f32).transpose(0, 2, 1, 3)
    kf = k.astype(f32).transpose(0, 2, 1, 3)
    vf = v.astype(f32).transpose(0, 2, 1, 3)
    c = lax.cumsum(jax.nn.log_sigmoid(f_logit.astype(f32)), axis=1).transpose(0, 2, 1)
    n_qb = S // Q_BLOCK
    s_pos = jnp.arange(S)

    def blk(args):
        q_b, c_b, i = args
        tq = i * Q_BLOCK + jnp.arange(Q_BLOCK)
        logits = (jnp.einsum('bhqd,bhkd->bhqk', q_b, kf) * scale
                  + c_b[..., :, None] - c[:, :, None, :])
        p = masked_softmax(logits, s_pos[None, :] <= tq[:, None])
        return jnp.einsum('bhqk,bhkd->bhqd', p, vf)

    qb = jnp.moveaxis(qf.reshape(B, H, n_qb, Q_BLOCK, Dh), 2, 0)
    cb = jnp.moveaxis(c.reshape(B, H, n_qb, Q_BLOCK), 2, 0)
    o = lax.map(blk, (qb, cb, jnp.arange(n_qb)))
    return jnp.moveaxis(o, 0, 2).reshape(B, H, S, Dh).transpose(0, 2, 1, 3)


def memory_cross_attention(h, m, wq, wk, wv, wo):
    B, S, _ = h.shape
    M = m.shape[1]
    f32 = jnp.float32
    q = (h @ wq).astype(f32).reshape(B, S, N_MEM_HEADS, MEM_HEAD_DIM)
    k = (m @ wk).astype(f32).reshape(B, M, N_MEM_HEADS, MEM_HEAD_DIM)
    v = (m @ wv).astype(f32).reshape(B, M, N_MEM_HEADS, MEM_HEAD_DIM)
    p = jax.nn.softmax(jnp.einsum('bshd,bmhd->bhsm', q, k) * MEM_HEAD_DIM ** -0.5, axis=-1)
    o = jnp.einsum('bhsm,bmhd->bshd', p, v).reshape(B, S, N_MEM_HEADS * MEM_HEAD_DIM)
    return o.astype(h.dtype) @ wo


def conv_gated_mlp(h, w_up, conv_w, conv_b, w_down):
    u = h @ w_up
    C = u.shape[-1]
    u = lax.conv_general_dilated(
        u, conv_w.astype(u.dtype).reshape(CONV_WIDTH, 1, C),
        window_strides=(1,), padding=[(CONV_WIDTH - 1, 0)],
        dimension_numbers=('NWC', 'WIO', 'NWC'), feature_group_count=C) + conv_b
    gate, val = u[..., :D_FF], u[..., D_FF:]
    return (jax.nn.silu(gate) * val) @ w_down


def setup_inputs(seed: int = 0) -> dict:
    key = jax.random.key(seed)
    ks = jax.random.split(key, 24)
    f32 = jnp.float32

    def nrm(k, shape, fan_in):
        return jax.random.normal(k, shape, f32) * fan_in ** -0.5

    def gain(k, shape):
        return 1.0 + 0.02 * jax.random.normal(k, shape, f32)

    L, Dh = CMP_BLOCK, HEAD_DIM
    MW = N_MEM_HEADS * MEM_HEAD_DIM
    return {
        "x": jax.random.normal(ks[0], (BATCH, SEQ, D_MODEL), f32),
        "mem": jax.random.normal(ks[1], (BATCH, MEM_LEN, D_MODEL), f32),
        "norm_mix": gain(ks[2], (DEPTH, D_MODEL)),
        "w_in": nrm(ks[3], (DEPTH, D_MODEL, IN_COLS), D_MODEL),
        "b_forget": 3.0 + 0.1 * jax.random.normal(ks[4], (DEPTH, N_FOX_HEADS), f32),
        "cmp_pe_k": 0.1 * jax.random.normal(ks[5], (DEPTH, L, Dh), f32),
        "cmp_pe_v": 0.1 * jax.random.normal(ks[6], (DEPTH, L, Dh), f32),
        "cmp_wk1": nrm(ks[7], (DEPTH, L, Dh, CMP_HIDDEN), L * Dh),
        "cmp_wk2": nrm(ks[8], (DEPTH, CMP_HIDDEN, Dh), CMP_HIDDEN),
        "cmp_wv1": nrm(ks[9], (DEPTH, L, Dh, CMP_HIDDEN), L * Dh),
        "cmp_wv2": nrm(ks[10], (DEPTH, CMP_HIDDEN, Dh), CMP_HIDDEN),
        "w_out": nrm(ks[11], (DEPTH, MIX_WIDTH, D_MODEL), MIX_WIDTH),
        "norm_cross": gain(ks[12], (DEPTH, D_MODEL)),
        "norm_mem": gain(ks[13], (DEPTH, D_MODEL)),
        "w_mq": nrm(ks[14], (DEPTH, D_MODEL, MW), D_MODEL),
        "w_mk": nrm(ks[15], (DEPTH, D_MODEL, MW), D_MODEL),
        "w_mv": nrm(ks[16], (DEPTH, D_MODEL, MW), D_MODEL),
        "w_mo": nrm(ks[17], (DEPTH, MW, D_MODEL), MW),
        "norm_ffn": gain(ks[18], (DEPTH, D_MODEL)),
        "w_up": nrm(ks[19], (DEPTH, D_MODEL, 2 * D_FF), D_MODEL),
        "conv_w": nrm(ks[20], (DEPTH, CONV_WIDTH, 2 * D_FF), CONV_WIDTH),
        "conv_b": 0.01 * jax.random.normal(ks[21], (DEPTH, 2 * D_FF), f32),
        "w_down": nrm(ks[22], (DEPTH, D_FF, D_MODEL), D_FF),
        "norm_final": gain(ks[23], (D_MODEL,)),
    }


def reference(x, mem, norm_mix, w_in, b_forget, cmp_pe_k, cmp_pe_v, cmp_wk1, cmp_wk2,
              cmp_wv1, cmp_wv2, w_out, norm_cross, norm_mem, w_mq, w_mk, w_mv, w_mo,
              norm_ffn, w_up, conv_w, conv_b, w_down, norm_final):
    B, S, _ = x.shape
    pos = jnp.arange(S)

    def heads(a, n):
        return a.reshape(B, S, n, HEAD_DIM)

    for i in range(DEPTH):
        h = rms_norm(x, norm_mix[i])
        (q_n, kc, vc, ks_, vs_, kw, vw, g_n,
         q_s, k_s, v_s, q_f, k_f, v_f, f_l) = split_columns(h @ w_in[i])
        q_n = partial_rope(heads(q_n, N_NSA_HEADS), pos)
        kc = partial_rope(heads(kc, N_NSA_KV), pos)
        ks_ = partial_rope(heads(ks_, N_NSA_KV), pos)
        kw = partial_rope(heads(kw, N_NSA_KV), pos)
        gates = jax.nn.sigmoid(g_n.astype(jnp.float32)).reshape(B, S, N_NSA_HEADS, 3)
        o_nsa = nsa_mixer(q_n, kc, heads(vc, N_NSA_KV), ks_, heads(vs_, N_NSA_KV),
                          kw, heads(vw, N_NSA_KV), gates,
                          cmp_pe_k[i], cmp_pe_v[i], cmp_wk1[i], cmp_wk2[i], cmp_wv1[i], cmp_wv2[i])
        o_sb = stick_breaking_attention(heads(q_s, N_SB_HEADS), heads(k_s, N_SB_HEADS),
                                        heads(v_s, N_SB_HEADS))
        o_fox = forgetting_attention(heads(q_f, N_FOX_HEADS), heads(k_f, N_FOX_HEADS),
                                     heads(v_f, N_FOX_HEADS), f_l + b_forget[i])
        o = jnp.concatenate([o_nsa, o_sb, o_fox], axis=2).reshape(B, S, MIX_WIDTH).astype(x.dtype)
        x = x + o @ w_out[i]
        m = rms_norm(mem, norm_mem[i])
        x = x + memory_cross_attention(rms_norm(x, norm_cross[i]), m,
                                       w_mq[i], w_mk[i], w_mv[i], w_mo[i])
        x = x + conv_gated_mlp(rms_norm(x, norm_ffn[i]), w_up[i], conv_w[i], conv_b[i], w_down[i])
    return rms_norm(x, norm_final)
```

```python
import numpy as np
import ml_dtypes
import concourse.bass as bass
import concourse.mybir as mybir
from concourse.bass_utils import run_bass_kernel_spmd

F32 = mybir.dt.float32
BF16 = mybir.dt.bfloat16
AF = mybir.ActivationFunctionType
ALU = mybir.AluOpType

SEQ, DM, KC, NT, TCH, NCH = 2048, 1024, 8, 16, 512, 4
DEPTH = 4
DFF = 2816
NEG = -30000.0
BIG = 1.0e30
EPS = 1e-6
C_QN, C_KC, C_VC, C_KS, C_VS, C_KW, C_VW, C_G = 0, 512, 640, 768, 896, 1024, 1152, 1280
C_QS, C_KSB, C_VSB, C_QF, C_KF, C_VF, C_FL = 1304, 1560, 1816, 2072, 2328, 2584, 2840
INC = 2844


class Buf:
    __slots__ = ("name", "t", "lw", "rd", "excl")

    def __init__(self, name, t, excl=False):
        self.name, self.t, self.lw, self.rd, self.excl = name, t, None, {}, excl

    def __getitem__(self, idx):
        return self.t[idx]


class Sched:
    NDSEM = 24

    def __init__(self, nc):
        self.nc = nc
        self.E = {"pe": nc.tensor, "act": nc.scalar, "dve": nc.vector, "pool": nc.gpsimd, "sp": nc.sync}
        self.sems, self.cnt = {}, {}
        for k in self.E:
            self.sems[k] = nc.alloc_semaphore("s_" + k)
            self.cnt[k] = 0
        for i in range(self.NDSEM):
            self.sems[("d", i)] = nc.alloc_semaphore("d%d" % i)
            self.cnt[("d", i)] = 0
        self.seen = {k: {} for k in self.E}
        self.dnext = 0
        self.ninstr = 0
        self.nwait = 0

    def _deps(self, reads, writes):
        deps = {}

        def add(kv):
            if kv is not None and deps.get(kv[0], 0) < kv[1]:
                deps[kv[0]] = kv[1]

        for b in reads:
            add(b.lw)
            if b.excl:
                for kv in b.rd.items():
                    add(kv)
        for b in writes:
            add(b.lw)
            for kv in b.rd.items():
                add(kv)
        return deps

    def _wait(self, eng, deps):
        seen, e = self.seen[eng], self.E[eng]
        for k, v in deps.items():
            if k == "pe" and eng == "pe":
                continue
            if seen.get(k, 0) >= v:
                continue
            e.wait_ge(self.sems[k], v)
            seen[k] = v
            self.nwait += 1

    def _mark(self, key, val, reads, writes):
        for b in reads:
            if b.excl:
                b.lw, b.rd = (key, val), {}
            else:
                b.rd[key] = val
        for b in writes:
            b.lw, b.rd = (key, val), {}

    def op(self, eng, fn, reads=(), writes=()):
        self._wait(eng, self._deps(reads, writes))
        ins = fn()
        self.cnt[eng] += 1
        ins.then_inc(self.sems[eng], 1)
        self._mark(eng, self.cnt[eng], reads, writes)
        self.ninstr += 1
        return ins

    def dma(self, out_ap, in_ap, reads=(), writes=(), q="sp", **kw):
        deps = self._deps(reads, writes)
        dk = ("d", self.dnext)
        self.dnext = (self.dnext + 1) % self.NDSEM
        if self.cnt[dk] > deps.get(dk, 0):
            deps[dk] = self.cnt[dk]
        self._wait(q, deps)
        ins = self.E[q].dma_start(out=out_ap, in_=in_ap, **kw)
        self.cnt[dk] += 16
        ins.then_inc(self.sems[dk], 16)
        self._mark(dk, self.cnt[dk], reads, writes)
        self.ninstr += 1
        return ins

    def barrier(self):
        deps = {k: v for k, v in self.cnt.items() if v > 0 and k != "sp"}
        for eng in self.E:
            self._wait(eng, dict(deps))


def _consts():
    bf = ml_dtypes.bfloat16
    c = {}
    half = 8
    inv = 500000.0 ** (-np.arange(half, dtype=np.float32) / half)
    ang = np.arange(SEQ, dtype=np.float32)[None, :] * inv[:, None]
    cs = np.concatenate([np.cos(ang), np.cos(ang)], 0).astype(np.float32)
    sn = np.concatenate([np.sin(ang), np.sin(ang)], 0).astype(np.float32)
    c["c_cos"], c["c_sin"] = cs.astype(bf), sn.astype(bf)
    j = np.arange(128)[:, None]
    t = np.arange(128)[None, :]
    c["c_ident"] = np.eye(128, dtype=np.float32).astype(bf)
    c["c_identf"] = np.eye(4, dtype=np.float32)
    c["c_ones"] = np.ones((128, 512), np.float32).astype(bf)
    c["c_tri"] = (j >= t).astype(np.float32).astype(bf)
    c["c_nm_incl"] = np.tile(np.where(j > t, NEG, 0.0), (1, 4)).astype(np.float32).astype(bf)
    c["c_nm_strict"] = np.tile(np.where(j >= t, NEG, 0.0), (1, 4)).astype(np.float32).astype(bf)
    c["c_nm_win"] = np.tile(np.where(j <= t, NEG, 0.0), (1, 4)).astype(np.float32).astype(bf)
    c["c_m01_strict"] = (j < t).astype(np.float32).astype(bf)
    n = np.arange(128)[:, None]
    tt = np.arange(SEQ)[None, :]
    c["c_nm_cmp"] = np.where(16 * n + 31 > tt, NEG, 0.0).astype(np.float32).astype(bf)
    starts = np.arange(127) * 16
    sel_starts = np.arange(32) * 64
    ovl = ((starts[:, None] < sel_starts[None, :] + 64) & (starts[:, None] + 32 > sel_starts[None, :]))
    vext = np.zeros((128, 33), np.float32)
    vext[:, 0] = 1.0
    vext[:127, 1:] = ovl
    c["c_vext"] = vext.astype(bf)
    onehot = (np.arange(SEQ)[None, :] // 64 == np.arange(32)[:, None]).astype(np.float32)
    c["c_blk1h"] = onehot.astype(bf)
    tq = np.arange(SEQ)
    cur = (tq // 64)[:, None]
    bid = np.arange(32)[None, :]
    forced = (bid == 0) | (bid == cur) | (bid == cur - 1)
    am = np.where(bid <= cur, np.where(forced, BIG, 0.0), -BIG).astype(np.float32)
    c["c_addmask"] = np.ascontiguousarray(am.reshape(16, 128, 32).transpose(1, 0, 2))
    pl = np.zeros((4, 6, 4, 68), np.float32)
    for h in range(4):
        pl[h, 0, h, 64] = 1.0
        pl[h, 1, h, 65] = 1.0
        pl[0, 2, h, 66] = 1.0
        pl[0, 2, h, 67] = 1.0
        pl[h, 3, h, 66] = 1.0
        pl[h, 4, h, 67] = 1.0
        pl[0, 5, h, 64] = 1.0
        pl[0, 5, h, 65] = 1.0
    c["c_place"] = pl.reshape(4, 6 * 4 * 68).astype(bf)
    return c


_CONST_SPECS = None


class K:
    def __init__(self, nseq, nlayers=DEPTH, phases=("nsa", "sb", "fox", "mem", "ffn"), final=True):
        self.nseq, self.nlayers, self.phases, self.final = nseq, nlayers, phases, final
        nc = self.nc = bass.Bass("TRN2", target_bir_lowering=False)
        self.S = Sched(nc)
        self._n = 0
        self.din = {}
        self.declare_io()
        self.alloc()
        self.load_consts()
        self.prep_all()
        print("sbuf bytes remaining", nc.sbuf_bytes_remaining)
        for s in range(nseq):
            self.run_seq(s)
            self.S.barrier()
        self.finish()

    def dram_in(self, name, shape, dt=F32):
        self.din[name] = self.nc.dram_tensor(name, list(shape), dt, kind="ExternalInput").ap()
        return self.din[name]

    def declare_io(self):
        nc, L = self.nc, DEPTH
        self.dram_in("x", [self.nseq, SEQ, DM])
        self.dram_in("mem", [self.nseq, 256, DM])
        for nm, shp in [("norm_mix", [L, DM]), ("w_in", [L, DM, INC]), ("b_forget", [L, 4]),
                        ("cmp_pe_k", [L, 32, 64]), ("cmp_pe_v", [L, 32, 64]),
                        ("cmp_wk1", [L, 32, 64, 64]), ("cmp_wk2", [L, 64, 64]),
                        ("cmp_wv1", [L, 32, 64, 64]), ("cmp_wv2", [L, 64, 64]),
                        ("w_out", [L, DM, DM]), ("norm_cross", [L, DM]), ("norm_mem", [L, DM]),
                        ("w_mq", [L, DM, 256]), ("w_mk", [L, DM, 256]), ("w_mv", [L, DM, 256]),
                        ("w_mo", [L, 256, DM]), ("norm_ffn", [L, DM]), ("w_up", [L, DM, 2 * DFF]),
                        ("conv_w", [L, 3, 2 * DFF]), ("conv_b", [L, 2 * DFF]), ("w_down", [L, DFF, DM]),
                        ("norm_final", [DM])]:
            self.dram_in(nm, shp)
        for nm, arr in _consts().items():
            self.dram_in(nm, arr.shape, BF16 if arr.dtype == ml_dtypes.bfloat16 else F32)
        self.y = nc.dram_tensor("y", [self.nseq, SEQ, DM], F32, kind="ExternalOutput").ap()
        d = lambda nm, shp: nc.dram_tensor(nm, shp, BF16).ap()
        self.W = []
        for l in range(self.nlayers):
            self.W.append(dict(
                win=d("b_win%d" % l, [DM, INC]), rot=d("b_rot%d" % l, [DM, 224]),
                wout=d("b_wout%d" % l, [DM, DM]), mq=d("b_mq%d" % l, [DM, 256]),
                mk=d("b_mk%d" % l, [DM, 256]), mv=d("b_mv%d" % l, [DM, 256]),
                mo=d("b_mo%d" % l, [256, DM]), up=d("b_up%d" % l, [DM, 2 * DFF]),
                down=d("b_down%d" % l, [DFF, DM]),
                w1k=d("b_w1k%d" % l, [64, 32 * 64]), w1v=d("b_w1v%d" % l, [64, 32 * 64]),
                w2k=d("b_w2k%d" % l, [64, 64]), w2v=d("b_w2v%d" % l, [64, 64]),
                pek=d("b_pek%d" % l, [32, 64]), pev=d("b_pev%d" % l, [32, 64])))
        self.hT_d = nc.dram_tensor("b_hT", [KC, 128, SEQ], BF16).ap()
        self.hT_dbuf = Buf("hT_d", None)
        self.wbuf_d = Buf("wdram", None)

    def sb(self, name, shape, dt=F32):
        return Buf(name, self.nc.alloc_sbuf_tensor(name, list(shape), dt))

    def alloc(self):
        nc = self.nc
        self.xres_t = nc.alloc_sbuf_tensor("xres", [128, NT, DM], F32)
        self.xt = [Buf("x%d" % i, self.xres_t) for i in range(NT)]
        self.ps = [Buf("ps%d" % i, nc.alloc_psum_tensor("ps%d" % i, [128, 512], F32), excl=True) for i in range(7)]
        self.pst = Buf("pst", nc.alloc_psum_tensor("pst", [128, 1024], BF16), excl=True)
        self._rot = 0
        self._rotbanks = [0, 1, 2]
        self._pending = None
        self._deferred = []
        self.wbuf = [self.sb("wbuf%d" % i, [128, 4096], BF16) for i in range(4)]
        self._wrot = 0
        self.hTb = [self.sb("hTb%d" % i, [128, KC, TCH], BF16) for i in range(2)]
        self._hrot = 0
        self.ARENA = 22528
        self.arena = nc.alloc_sbuf_tensor("arena", [128, self.ARENA], BF16)
        self.PT = [self.sb("PT%d" % i, [128, 512], BF16) for i in range(3)]
        self._prot = 0
        self.wf = [self.sb("wf%d" % i, [128, 514], F32) for i in range(5)]
        self._frot = 0
        self.h16 = [self.sb("h16_%d" % i, [128, DM], BF16) for i in range(2)]
        self._h16rot = 0
        self.small = [self.sb("sm%d" % i, [128, 64], F32) for i in range(6)]
        self._srot = 0
        self.o16 = [self.sb("o16_%d" % i, [128, 512], BF16) for i in range(2)]
        self._orot = 0
        self.oT = [self.sb("oT%d" % i, [128, 4, 128], BF16) for i in range(2)]
        self._otrot = 0
        self.oacc = self.sb("oacc", [128, 8, 64], F32)
        self.vcc = self.sb("vcc", [128, 2, 97], BF16)
        self.selpad = [self.sb("selpad%d" % g, [128, 96], BF16) for g in range(2)]

    def psum(self):
        rb = self._rotbanks
        self._rot = (self._rot + 1) % len(rb)
        return self.ps[rb[self._rot]]

    def pipe_unit(self, qk, pv):
        pt = qk()
        self.pipe_flush_pv()
        d, self._deferred = self._deferred, []
        for f in d:
            f()
        self._pending = (pv, pt)

    def pipe_flush_pv(self):
        if self._pending is not None:
            pv, pt = self._pending
            self._pending = None
            pv(pt)

    def pipe_defer(self, fn):
        self._deferred.append(fn)

    def pipe_drain(self):
        self.pipe_flush_pv()
        d, self._deferred = self._deferred, []
        for f in d:
            f()

    def getw(self):
        b = self.wbuf[self._wrot]
        self._wrot = (self._wrot + 1) % 4
        return b

    def gethT(self):
        b = self.hTb[self._hrot]
        self._hrot = (self._hrot + 1) % 2
        return b

    def getPT(self):
        b = self.PT[self._prot]
        self._prot = (self._prot + 1) % 3
        return b

    def getf(self):
        b = self.wf[self._frot]
        self._frot = (self._frot + 1) % 5
        return b

    def getsm(self):
        b = self.small[self._srot]
        self._srot = (self._srot + 1) % 6
        return b

    def mm(self, ob, out, lb, lhsT, rb, rhs, start, stop, extra=()):
        nc = self.nc
        self.S.op("pe", lambda: nc.tensor.matmul(out, lhsT=lhsT, rhs=rhs, start=start, stop=stop,
                                                 skip_group_check=True),
                  reads=[lb, rb] + list(extra), writes=[ob])

    def act(self, ob, out, ib, in_, func, reads=(), **kw):
        nc = self.nc
        self.S.op("act", lambda: nc.scalar.activation(out=out, in_=in_, func=func, **kw),
                  reads=[ib] + list(reads), writes=[ob] if not isinstance(ob, (list, tuple)) else list(ob))

    def ts(self, eng, ob, out, ib, in0, s1, s2, op0, op1=None, reads=()):
        e = self.S.E[eng]
        if op1 is None:
            fn = lambda: e.tensor_scalar(out=out, in0=in0, scalar1=s1, scalar2=None, op0=op0)
        else:
            fn = lambda: e.tensor_scalar(out=out, in0=in0, scalar1=s1, scalar2=s2, op0=op0, op1=op1)
        self.S.op(eng, fn, reads=[ib] + list(reads), writes=[ob])

    def tt(self, eng, ob, out, ab, a, bb, b, op):
        e = self.S.E[eng]
        self.S.op(eng, lambda: e.tensor_tensor(out=out, in0=a, in1=b, op=op), reads=[ab, bb], writes=[ob])

    def stt(self, ob, out, ab, in0, scalar, bb, in1, op0, op1, reads=()):
        nc = self.nc
        self.S.op("dve", lambda: nc.vector.scalar_tensor_tensor(out=out, in0=in0, scalar=scalar, in1=in1,
                                                                op0=op0, op1=op1),
                  reads=[ab, bb] + list(reads), writes=[ob])

    def cp(self, eng, ob, out, ib, in_):
        if eng == "act":
            nc = self.nc
            self.S.op("act", lambda: nc.scalar.copy(out=out, in_=in_), reads=[ib], writes=[ob])
        else:
            e = self.S.E[eng]
            self.S.op(eng, lambda: e.tensor_copy(out=out, in_=in_), reads=[ib], writes=[ob])

    def memset(self, eng, ob, ap, val):
        e = self.S.E[eng]
        self.S.op(eng, lambda: e.memset(ap, val), writes=[ob])

    def load_consts(self):
        S = self.S
        self.C = {}
        for nm, arr in _consts().items():
            shp = list(arr.shape)
            if nm == "c_blk1h":
                continue
            b = self.sb("k_" + nm, shp, BF16 if arr.dtype == ml_dtypes.bfloat16 else F32)
            S.dma(b[:], self.din[nm], writes=[b])
            self.C[nm] = b
        L = self.nlayers
        self.gain = {}
        for nm in ("norm_mix", "norm_cross", "norm_mem", "norm_ffn"):
            g = self.sb("g_" + nm, [128, L, KC], F32)
            for l in range(L):
                S.dma(g[:, l, :], self.din[nm][l].rearrange("(k p) -> p k", p=128), writes=[g],
                      allow_slow_non_contiguous=True)
            self.gain[nm] = g
        g8 = self.sb("g8_mix", [128, L, KC], F32)
        self.ts("dve", g8, g8[:], self.gain["norm_mix"], self.gain["norm_mix"][:], 0.125, None, ALU.mult)
        self.gain["norm_mix8"] = g8
        g8c = self.sb("g8_cross", [128, L, KC], F32)
        self.ts("dve", g8c, g8c[:], self.gain["norm_cross"], self.gain["norm_cross"][:], 0.125, None, ALU.mult)
        self.gain["norm_cross8"] = g8c
        self.nbf = self.sb("nbf", [4, L], F32)
        S.dma(self.nbf[:], self.din["b_forget"][0:L].rearrange("l h -> h l"), writes=[self.nbf],
              allow_slow_non_contiguous=True)
        self.ts("dve", self.nbf, self.nbf[:], self.nbf, self.nbf[:], -1.0, None, ALU.mult)
        self.cw = self.sb("cw", [128, L, 44, 4], F32)
        crow = Buf("crow", self.arena[0:4, 0:4 * DFF].bitcast(F32))
        idf = self.C["c_identf"]
        for l in range(L):
            S.dma(crow[0:3, :], self.din["conv_w"][l], writes=[crow])
            S.dma(crow[3:4, :], self.din["conv_b"][l:l + 1, :], writes=[crow])
            for c0 in range(0, 44, 11):
                ps = self.psum()
                for cc in range(c0, c0 + 11):
                    nc = self.nc
                    S.op("pe", lambda cc=cc, ps=ps: nc.tensor.transpose(
                        out=ps[:, (cc - c0) * 4:(cc - c0) * 4 + 4], in_=crow[0:4, cc * 128:(cc + 1) * 128],
                        identity=idf[0:4, 0:4]), reads=[crow, idf], writes=[ps])
                self.cp("dve", self.cw, self.cw[:, l, c0:c0 + 11, :].rearrange("p c k -> p (c k)"), ps, ps[:, 0:44])
        for g in range(2):
            self.memset("pool", self.selpad[g], self.selpad[g][:], 0.0)
        for g in range(2):
            self.cp("pool", self.vcc, self.vcc[:, g, 64:97], self.C["c_vext"], self.C["c_vext"][:, :])
        self.ones4b = Buf("ones4b", self.C["c_ones"][0:4, :])
        S.barrier()

    def prep_all(self):
        S, nc = self.S, self.nc
        ar = self.arena
        NS = 3
        pin = [Buf("pin%d" % i, ar[:, i * 4096:(i + 1) * 4096].bitcast(F32)) for i in range(NS)]
        pout = [Buf("pout%d" % i, ar[:, 12288 + i * 2048: 12288 + (i + 1) * 2048]) for i in range(NS)]
        rott = Buf("rott", ar[:, 18432:18432 + 224])
        self._pk = 0
        engs = ["act", "dve"]

        def piece(src, dst, rows, cols, scale, after=None):
            i = self._pk % NS
            eng = engs[self._pk % 2]
            self._pk += 1
            S.dma(pin[i][0:rows, 0:cols], src, writes=[pin[i]])
            o, a = pout[i][0:rows, 0:cols], pin[i][0:rows, 0:cols]
            if isinstance(scale, tuple):
                sbuf, sap = scale
                if eng == "act":
                    self.act(pout[i], o, pin[i], a, AF.Copy, reads=[sbuf], scale=sap)
                else:
                    self.ts(eng, pout[i], o, pin[i], a, sap, None, ALU.mult, reads=[sbuf])
            else:
                if eng == "act":
                    self.act(pout[i], o, pin[i], a, AF.Copy, scale=float(scale))
                else:
                    self.ts(eng, pout[i], o, pin[i], a, float(scale), None, ALU.mult)
            if after is not None:
                after(pout[i])
            S.dma(dst, pout[i][0:rows, 0:cols], reads=[pout[i]], q="pool")

        def mat(src, dst, R, Ccols, gain=None, l=0, scale=1.0):
            for r0 in range(0, R, 128):
                rows = min(128, R - r0)
                for c0 in range(0, Ccols, 2048):
                    cols = min(2048, Ccols - c0)
                    sc = (gain, gain[0:rows, l, r0 // 128:r0 // 128 + 1]) if gain is not None else scale
                    piece(src[r0:r0 + rows, c0:c0 + cols], dst[r0:r0 + rows, c0:c0 + cols], rows, cols, sc)

        for l in range(self.nlayers):
            W, D = self.W[l], self.din
            g, g8 = self.gain["norm_mix"], self.gain["norm_mix8"]
            for rc in range(KC):
                r0 = rc * 128
                gs, g8s = (g, g[:, l, rc:rc + 1]), (g8, g8[:, l, rc:rc + 1])

                def rot_ops(src_off, nh, roff):
                    def f(pb):
                        v = pb[:, src_off:src_off + 64 * nh].rearrange("p (h d) -> p h d", d=64)
                        rt = rott[:, :].rearrange("p (h d) -> p h d", d=16)
                        self.ts("dve", rott, rt[:, roff:roff + nh, 0:8], pb, v[:, :, 8:16], -1.0, None, ALU.mult)
                        self.cp("dve", rott, rt[:, roff:roff + nh, 8:16], pb, v[:, :, 0:8])
                    return f

                def rot2(pb):
                    rot_ops(0, 2, 8)(pb)
                    rot_ops(256, 2, 10)(pb)
                    rot_ops(512, 2, 12)(pb)

                segs = [(0, 512, g8s, rot_ops(0, 8, 0)), (512, 1304, gs, rot2), (1304, 1560, g8s, None),
                        (1560, 2072, gs, None), (2072, 2328, g8s, None), (2328, 2844, gs, None)]
                for (a, b, sc, aft) in segs:
                    piece(D["w_in"][l, r0:r0 + 128, a:b], W["win"][r0:r0 + 128, a:b], 128, b - a, sc, aft)
                S.dma(W["rot"][r0:r0 + 128, :], rott[:, :], reads=[rott])
            mat(D["w_out"][l], W["wout"], DM, DM)
            mat(D["w_mq"][l], W["mq"], DM, 256, gain=self.gain["norm_cross8"], l=l)
            mat(D["w_mk"][l], W["mk"], DM, 256, gain=self.gain["norm_mem"], l=l)
            mat(D["w_mv"][l], W["mv"], DM, 256, gain=self.gain["norm_mem"], l=l)
            mat(D["w_mo"][l], W["mo"], 256, DM)
            mat(D["w_up"][l], W["up"], DM, 2 * DFF, gain=self.gain["norm_ffn"], l=l)
            mat(D["w_down"][l], W["down"], DFF, DM)
            for (sn, dn) in (("cmp_wk1", "w1k"), ("cmp_wv1", "w1v")):
                for l0 in range(0, 32, 16):
                    i = self._pk % NS
                    self._pk += 1
                    S.dma(pin[i][0:64, 0:1024].rearrange("p (l c) -> p l c", c=64),
                          D[sn][l, l0:l0 + 16].rearrange("l d c -> d l c"), writes=[pin[i]])
                    self.cp("dve", pout[i], pout[i][0:64, 0:1024], pin[i], pin[i][0:64, 0:1024])
                    S.dma(W[dn][:, l0 * 64:(l0 + 16) * 64], pout[i][0:64, 0:1024], reads=[pout[i]])
            mat(D["cmp_wk2"][l], W["w2k"], 64, 64)
            mat(D["cmp_wv2"][l], W["w2v"], 64, 64)
            mat(D["cmp_pe_k"][l], W["pek"], 32, 64)
            mat(D["cmp_pe_v"][l], W["pev"], 32, 64)
        S.barrier()

    def load_w(self, src, shape_view, nbytes_cols):
        b = self.getw()
        v = b[:, 0:nbytes_cols]
        if shape_view is not None:
            v = v.rearrange(shape_view[0], **shape_view[1])
        self.S.dma(v, src, reads=[self.wbuf_d], writes=[b])
        return b, v

    def load_wcols(self, wd, c0, ncols, rows=DM):
        nk = rows // 128
        return self.load_w(wd[:, c0:c0 + ncols].rearrange("(k p) n -> p k n", p=128),
                           ("p (k n) -> p k n", dict(k=nk)), nk * ncols)

    def rmsnorm_T(self, tiles, hb, hview, col0):
        nc, S = self.nc, self.S
        for i, t in enumerate(tiles):
            xb = self.xt[t]
            xa = self.xres_t[:, t, :]
            sm = self.getsm()
            h = self.h16[self._h16rot]
            self._h16rot ^= 1
            self.act([h, sm], h[:], xb, xa, AF.Square, accum_out=sm[:, 0:1])
            self.act(sm, sm[:, 1:2], sm, sm[:, 0:1], AF.Ln, scale=1.0 / DM, bias=EPS)
            self.act(sm, sm[:, 2:3], sm, sm[:, 1:2], AF.Exp, scale=-0.5)
            self.ts("dve", h, h[:], xb, xa, sm[:, 2:3], None, ALU.mult, reads=[sm])
            for kc in range(KC):
                S.op("pe", lambda kc=kc, h=h: nc.tensor.transpose(
                    out=self.pst[:, kc * 128:(kc + 1) * 128], in_=h[:, kc * 128:(kc + 1) * 128],
                    identity=self.C["c_ident"][:]), reads=[h, self.C["c_ident"]], writes=[self.pst])
            self.cp("act" if i % 2 else "dve", hb, hview[:, :, col0 + i * 128: col0 + (i + 1) * 128],
                    self.pst, self.pst[:, :].rearrange("p (k t) -> p k t", k=KC))

    def projT(self, dstb, dst, wb, wv, c0, M, hb, hv, ncols, evac="act", scale=None):
        ps = self.psum()
        for kc in range(KC):
            self.mm(ps, ps[0:M, 0:ncols], wb, wv[:, kc, c0:c0 + M], hb, hv[:, kc, 0:ncols], kc == 0, kc == KC - 1)
        if dst is not None:
            if scale is not None:
                self.act(dstb, dst, ps, ps[0:M, 0:ncols], AF.Copy, scale=scale)
            else:
                self.cp(evac, dstb, dst, ps, ps[0:M, 0:ncols])
        return ps

    def rope_rows(self, dstb, dst16, psA, wrb, wrv, r0, hb, hv, t0, ncols):
        psB = self.ps[6]
        for kc in range(KC):
            self.mm(psB, psB[0:16, 0:ncols], wrb, wrv[:, kc, r0:r0 + 16], hb, hv[:, kc, 0:ncols], kc == 0, kc == KC - 1)
        cs, sn = self.C["c_cos"], self.C["c_sin"]
        f1, f2 = self.getf(), self.getf()
        self.tt("dve", f1, f1[0:16, 0:ncols], psA, psA[0:16, 0:ncols], cs, cs[:, t0:t0 + ncols], ALU.mult)
        self.tt("dve", f2, f2[0:16, 0:ncols], psB, psB[0:16, 0:ncols], sn, sn[:, t0:t0 + ncols], ALU.mult)
        self.tt("pool", dstb, dst16, f1, f1[0:16, 0:ncols], f2, f2[0:16, 0:ncols], ALU.add)

    def out_proj(self, o16b, ncol_o, wob, wov, tile):
        nc, S = self.nc, self.S
        nk = ncol_o // 128
        for k in range(nk):
            S.op("pe", lambda k=k: nc.tensor.transpose(
                out=self.pst[:, k * 128:(k + 1) * 128], in_=o16b[:, k * 128:(k + 1) * 128],
                identity=self.C["c_ident"][:]), reads=[o16b, self.C["c_ident"]], writes=[self.pst])
        oT = self.oT[self._otrot]
        self._otrot ^= 1
        self.cp("act", oT, oT[:, 0:nk, :], self.pst, self.pst[:, 0:nk * 128].rearrange("p (k t) -> p k t", k=nk))
        xb, xa = self.xt[tile], self.xres_t[:, tile, :]
        for n in range(2):
            ps = self.psum()
            for k in range(nk):
                self.mm(ps, ps[:, :], oT, oT[:, k, :], wob, wov[:, k, n * 512:(n + 1) * 512], k == 0, k == nk - 1)
            self.tt("dve", xb, xa[:, n * 512:(n + 1) * 512], xb, xa[:, n * 512:(n + 1) * 512], ps, ps[:, :], ALU.add)

    def load_hT(self, c):
        hb = self.gethT()
        self.S.dma(hb[:], self.hT_d[:, :, c * TCH:(c + 1) * TCH].rearrange("k p t -> p k t"),
                   reads=[self.hT_dbuf], writes=[hb])
        return hb

    def pvs(self, st, acc, out, ptb, lhsT, vb, rhs):
        self.mm(acc, out, ptb, lhsT, vb, rhs, st["first"], True)
        st["first"] = False

    def run_seq(self, s):
        S = self.S
        for t in range(NT):
            S.dma(self.xres_t[:, t, :], self.din["x"][s, t * 128:(t + 1) * 128, :], writes=[self.xt[t]])
        for l in range(self.nlayers):
            self.l = l
            self.s = s
            if any(p in self.phases for p in ("nsa", "sb", "fox")):
                self.phase_mix()
            if "mem" in self.phases:
                self.phase_mem()
            if "ffn" in self.phases:
                self.phase_ffn()
        if self.final:
            self.gfin = Buf("gfin", self.arena[:, 0:2048].bitcast(F32))
            S.dma(self.gfin[:, :], self.din["norm_final"].partition_broadcast(128), writes=[self.gfin])
        for t in range(NT):
            xb, xa = self.xt[t], self.xres_t[:, t, :]
            if self.final:
                sm = self.getsm()
                h = self.h16[self._h16rot]
                self._h16rot ^= 1
                self.act([h, sm], h[:], xb, xa, AF.Square, accum_out=sm[:, 0:1])
                self.act(sm, sm[:, 1:2], sm, sm[:, 0:1], AF.Ln, scale=1.0 / DM, bias=EPS)
                self.act(sm, sm[:, 2:3], sm, sm[:, 1:2], AF.Exp, scale=-0.5)
                self.stt(xb, xa, xb, xa, sm[:, 2:3], self.gfin, self.gfin[:, :], ALU.mult, ALU.mult, reads=[sm])
            self.S.dma(self.y[s, t * 128:(t + 1) * 128, :], xa, reads=[xb])

    ybuf = Buf("y", None)

    def finish(self):
        S = self.S
        deps = {k: v for k, v in S.cnt.items() if isinstance(k, tuple) and v > 0}
        S._wait("sp", deps)

    def phase_mix(self):
        S, nc, l = self.S, self.nc, self.l
        W = self.W[l]
        ar = self.arena
        A = lambda name, p, a, n: Buf(name, ar[0:p, a:a + n])
        kvc = A("kvc", 64, 0, 8192)
        ksT = A("ksT", 128, 8192, 4096)
        kwT = A("kwT", 128, 12288, 4096)
        vsw = A("vsw", 128, 16384, 4160)
        kcc = A("kcc", 64, 20544, 256)
        gat = Buf("gat", ar[:, 20800:20800 + 768].bitcast(F32))
        kvc_v = kvc[:, :].rearrange("p (j t) -> p j t", j=4)
        ksT_v = ksT[:, :].rearrange("p (g t) -> p g t", g=2)
        kwT_v = kwT[:, :].rearrange("p (g t) -> p g t", g=2)
        vsw_v = vsw[:, :].rearrange("p (t j d) -> p t j d", t=NT, j=4)
        kcc_v = kcc[:, :].rearrange("p (g n) -> p g n", g=2)
        do_nsa = "nsa" in self.phases
        if do_nsa:
            self.memset("pool", ksT, ksT[96:128, :], 0.0)
            self.memset("pool", kwT, kwT[64:128, :], 0.0)
            for g in range(2):
                S.dma(ksT_v[64:96, g, :], self.din["c_blk1h"], writes=[ksT])
            self.memset("pool", vsw, vsw_v[:, :, :, 64:65], 1.0)
        for c in range(NCH):
            hb = self.gethT()
            self.rmsnorm_T(range(4 * c, 4 * c + 4), hb, hb[:], 0)
            S.dma(self.hT_d[:, :, c * TCH:(c + 1) * TCH].rearrange("k p t -> p k t"), hb[:], reads=[hb],
                  writes=[self.hT_dbuf])
            if not do_nsa:
                continue
            wb, wv = self.load_wcols(W["win"], C_KC, 256)
            wrb, wrv = self.load_wcols(W["rot"], 128, 96)
            t0 = c * TCH
            for g in range(2):
                ps = self.projT(kvc, kvc_v[0:64, g, t0:t0 + TCH], wb, wv, g * 64, 64, hb, hb[:], TCH)
                self.rope_rows(kvc, kvc_v[0:16, g, t0:t0 + TCH], ps, wrb, wrv, g * 16, hb, hb[:], t0, TCH)
                self.projT(kvc, kvc_v[0:64, 2 + g, t0:t0 + TCH], wb, wv, 128 + g * 64, 64, hb, hb[:], TCH, evac="dve")
            wb, wv = self.load_wcols(W["win"], C_KS, 512)
            for g in range(2):
                ps = self.projT(ksT, ksT_v[0:64, g, t0:t0 + TCH], wb, wv, g * 64, 64, hb, hb[:], TCH)
                self.rope_rows(ksT, ksT_v[0:16, g, t0:t0 + TCH], ps, wrb, wrv, 32 + g * 16, hb, hb[:], t0, TCH)
                ps = self.projT(kwT, kwT_v[0:64, g, t0:t0 + TCH], wb, wv, 256 + g * 64, 64, hb, hb[:], TCH)
                self.rope_rows(kwT, kwT_v[0:16, g, t0:t0 + TCH], ps, wrb, wrv, 64 + g * 16, hb, hb[:], t0, TCH)
            for tl in range(4):
                t = 4 * c + tl
                ps = self.psum()
                for kc in range(KC):
                    self.mm(ps, ps[:, 0:128], hb, hb[:, kc, tl * 128:(tl + 1) * 128], wb, wv[:, kc, 128:256], kc == 0, kc == KC - 1)
                for kc in range(KC):
                    self.mm(ps, ps[:, 128:256], hb, hb[:, kc, tl * 128:(tl + 1) * 128], wb, wv[:, kc, 384:512], kc == 0, kc == KC - 1)
                self.cp("act", vsw, vsw_v[:, t, :, 0:64], ps, ps[:, 0:256].rearrange("p (j d) -> p j d", j=4))
        if do_nsa:
            self.nsa_compress(kvc, kvc_v, kcc, kcc_v)
            S.barrier()
            self.nsa_q(kvc, ksT, ksT_v, kwT, kwT_v, vsw, vsw_v, kcc, kcc_v, gat)
            S.barrier()
        if "sb" in self.phases:
            self.phase_sb()
            S.barrier()
        if "fox" in self.phases:
            self.phase_fox()
            S.barrier()

    def nsa_compress(self, kvc, kvc_v, kcc, kcc_v):
        S, nc, l = self.S, self.nc, self.l
        W = self.W[l]
        wb = self.getw()
        w1 = wb[0:64, 0:4096].rearrange("p (j l c) -> p j l c", j=2, l=32)
        S.dma(w1[:, 0, :, :], W["w1k"].rearrange("p (l c) -> p l c", c=64), reads=[self.wbuf_d], writes=[wb])
        S.dma(w1[:, 1, :, :], W["w1v"].rearrange("p (l c) -> p l c", c=64), reads=[self.wbuf_d], writes=[wb])
        wb2 = self.getw()
        w2 = wb2[0:64, 0:128].rearrange("p (j e) -> p j e", j=2)
        S.dma(w2[:, 0, :], W["w2k"], reads=[self.wbuf_d], writes=[wb2])
        S.dma(w2[:, 1, :], W["w2v"], reads=[self.wbuf_d], writes=[wb2])
        pen = wb2[0:32, 256:384].rearrange("p (j d) -> p j d", j=2)
        S.dma(pen[:, 0, :], W["pek"], reads=[self.wbuf_d], writes=[wb2])
        S.dma(pen[:, 1, :], W["pev"], reads=[self.wbuf_d], writes=[wb2])
        idn = self.C["c_ident"]
        for j in range(2):
            S.op("pe", lambda j=j: nc.tensor.transpose(out=self.pst[0:64, j * 32:(j + 1) * 32], in_=pen[:, j, :],
                                                       identity=idn[0:32, 0:32]), reads=[wb2, idn], writes=[self.pst])
        pet = self.getPT()
        peT = pet[0:64, 0:64].rearrange("p (j l) -> p j l", j=2)
        self.cp("dve", pet, pet[0:64, 0:64], self.pst, self.pst[0:64, 0:64])
        sm = self.getsm()
        for j in range(2):
            ps = self.psum()
            for li in range(32):
                self.mm(ps, ps[0:64, 0:1], wb, w1[:, j, li, :], pet, peT[:, j, li:li + 1], li == 0, li == 31)
            self.cp("dve", sm, sm[0:64, j:j + 1], ps, ps[0:64, 0:1])
        for j in range(2):
            for g in range(2):
                ps = self.psum()
                for li in range(32):
                    self.mm(ps, ps[0:64, 0:127], wb, w1[:, j, li, :], kvc, kvc_v[0:64, 2 * j + g, li:li + 16 * 126 + 1:16],
                            li == 0, li == 31)
                hid = self.getPT()
                self.act(hid, hid[0:64, 0:127], ps, ps[0:64, 0:127], AF.Silu, reads=[sm], bias=sm[0:64, j:j + 1], scale=1.0)
                ps2 = self.psum()
                if j == 0:
                    self.mm(ps2, ps2[0:64, 0:127], wb2, w2[:, 0, :], hid, hid[0:64, 0:127], True, True)
                    self.cp("dve", kcc, kcc_v[0:64, g, 0:127], ps2, ps2[0:64, 0:127])
                else:
                    self.mm(ps2, ps2[0:127, 0:64], hid, hid[0:64, 0:127], wb2, w2[:, 1, :], True, True)
                    self.cp("dve", self.vcc, self.vcc[0:127, g, 0:64], ps2, ps2[0:127, 0:64])

    def nsa_q(self, kvc, ksT, ksT_v, kwT, kwT_v, vsw, vsw_v, kcc, kcc_v, gat):
        S, nc, l = self.S, self.nc, self.l
        W = self.W[l]
        QnT = Buf("QnT", self.arena[0:128, 0:4096])
        Q = QnT[:, :].rearrange("p (g b h q) -> p g b h q", g=2, b=4, h=4)
        self.Qsel = Buf("Qsel", None)
        self.memset("pool", QnT, QnT[64:128, :], 0.0)
        gat_v = gat[:, :].rearrange("p (t c) -> p t c", c=24)
        idn = self.C["c_ident"]
        wob, wov = None, None
        for c in range(NCH):
            hb = self.load_hT(c)
            wb, wv = self.load_wcols(W["win"], C_QN, 512)
            wrb, wrv = self.load_wcols(W["rot"], 0, 128)
            wgb, wgv = self.load_wcols(W["win"], C_G, 24)
            if wob is None or True:
                wob, wov = self.load_w(W["wout"][0:512, :].rearrange("(k p) n -> p k n", p=128),
                                       ("p (k n) -> p k n", dict(k=4)), 4096)
            t0 = c * TCH
            for hh in range(8):
                g, h = hh // 4, hh % 4
                ps = self.psum()
                for kc in range(KC):
                    self.mm(ps, ps[0:64, :], wb, wv[:, kc, hh * 64:(hh + 1) * 64], hb, hb[:, kc, :], kc == 0, kc == KC - 1)
                self.cp("act", QnT, Q[0:64, g, :, h, :], ps, ps[0:64, :].rearrange("p (b q) -> p b q", b=4))
                psB = self.ps[6]
                for kc in range(KC):
                    self.mm(psB, psB[0:16, :], wrb, wrv[:, kc, hh * 16:(hh + 1) * 16], hb, hb[:, kc, :], kc == 0, kc == KC - 1)
                cs, sn = self.C["c_cos"], self.C["c_sin"]
                f1, f2 = self.getf(), self.getf()
                self.tt("dve", f1, f1[0:16, 0:TCH], ps, ps[0:16, :], cs, cs[:, t0:t0 + TCH], ALU.mult)
                self.tt("dve", f2, f2[0:16, 0:TCH], psB, psB[0:16, :], sn, sn[:, t0:t0 + TCH], ALU.mult)
                self.tt("pool", QnT, Q[0:16, g, :, h, :], f1, f1[0:16, 0:TCH].rearrange("p (b q) -> p b q", b=4),
                        f2, f2[0:16, 0:TCH].rearrange("p (b q) -> p b q", b=4), ALU.add)
            for tl in range(4):
                t = 4 * c + tl
                ps = self.psum()
                for kc in range(KC):
                    self.mm(ps, ps[:, 0:24], hb, hb[:, kc, tl * 128:(tl + 1) * 128], wgb, wgv[:, kc, 0:24], kc == 0, kc == KC - 1)
                self.act(gat, gat_v[:, t, :], ps, ps[:, 0:24], AF.Exp, scale=-1.0)
                self.ts("dve", gat, gat_v[:, t, :], gat, gat_v[:, t, :], 1.0, None, ALU.add)
                S.op("dve", lambda t=t: nc.vector.reciprocal(out=gat_v[:, t, :], in_=gat_v[:, t, :]), reads=[gat], writes=[gat])
            for bl in range(4):
                qb = 4 * c + bl
                self.nsa_qblock(qb, bl, QnT, Q, ksT, ksT_v, kwT, kwT_v, vsw, vsw_v, kcc, kcc_v, gat, gat_v)

                def fin(qb=qb, wob=wob, wov=wov):
                    o16 = self.o16[self._orot]
                    self._orot ^= 1
                    self.cp("pool", o16, o16[:, :], self.oacc, self.oacc[:, :, :].rearrange("p h d -> p (h d)"))
                    self.out_proj(o16, 512, wob, wov, qb)
                self.pipe_defer(fin)
            self.pipe_drain()

    def nsa_qblock(self, qb, bl, QnT, Q, ksT, ksT_v, kwT, kwT_v, vsw, vsw_v, kcc, kcc_v, gat, gat_v):
        S, nc = self.S, self.nc
        idn = self.C["c_ident"]
        gvs = [gat_v[:, qb, g * 12:(g + 1) * 12].rearrange("p (h k) -> p h k", k=3) for g in range(2)]
        q64s = [Q[0:64, g, bl, :, :].rearrange("p h q -> p (h q)") for g in range(2)]
        q128s = [Q[0:128, g, bl, :, :].rearrange("p h q -> p (h q)") for g in range(2)]
        for g in range(2):
            acc = self.ps[3] if g == 0 else self.ps[6]
            st = {"first": True}

            def qk(g=g):
                ps = self.psum()
                self.mm(ps, ps[0:127, :], kcc, kcc_v[0:64, g, 0:127], QnT, q64s[g], True, False)
                nmc = self.C["c_nm_cmp"]
                self.mm(ps, ps[0:127, :].rearrange("p (h q) -> p h q", h=4), idn, idn[0:127, 0:127], nmc,
                        nmc[0:127, qb * 128:(qb + 1) * 128].unsqueeze(1).broadcast_to([127, 4, 128]), False, True)
                pt = self.getPT()
                self.act(pt, pt[0:127, :], ps, ps[0:127, :], AF.Exp)
                return pt

            def pv(pt, g=g, acc=acc, st=st):
                for h in range(4):
                    self.pvs(st, acc, acc[:, h * 97:(h + 1) * 97], pt, pt[0:127, h * 128:(h + 1) * 128],
                             self.vcc, self.vcc[0:127, g, :])

            def epi(g=g, acc=acc):
                gv = gvs[g]
                accv = acc[:, 0:388].rearrange("p (h d) -> p h d", h=4)
                sm = self.getsm()
                self.ts("dve", sm, sm[:, 0:4], acc, accv[:, :, 64], 1e-30, None, ALU.max)
                S.op("dve", lambda sm=sm: nc.vector.reciprocal(out=sm[:, 4:8], in_=sm[:, 0:4]), reads=[sm], writes=[sm])
                self.tt("dve", sm, sm[:, 8:12], sm, sm[:, 4:8], gat, gv[:, :, 0], ALU.mult)
                f = self.getf()
                fv = f[:, 0:128].rearrange("p (h j) -> p h j", h=4)
                self.tt("dve", f, fv, acc, accv[:, :, 65:97], sm, sm[:, 4:8].unsqueeze(2).broadcast_to([128, 4, 32]), ALU.mult)
                ov = self.oacc[:, g * 4:(g + 1) * 4, :]
                self.tt("dve", self.oacc, ov, acc, accv[:, :, 0:64], sm, sm[:, 8:12].unsqueeze(2).broadcast_to([128, 4, 64]), ALU.mult)
                imp = f[:, 128:160]
                self.tt("dve", f, imp, f, fv[:, 0, :], f, fv[:, 1, :], ALU.add)
                self.tt("dve", f, f[:, 160:192], f, fv[:, 2, :], f, fv[:, 3, :], ALU.add)
                self.tt("dve", f, imp, f, imp, f, f[:, 160:192], ALU.add)
                am = self.C["c_addmask"]
                self.tt("dve", f, imp, f, imp, am, am[:, qb, :], ALU.add)
                S.op("dve", lambda f=f: nc.vector.max(out=f[:, 192:200], in_=f[:, 128:160]), reads=[f], writes=[f])
                S.op("dve", lambda f=f: nc.vector.match_replace(out=f[:, 200:232], in_to_replace=f[:, 192:200],
                                                               in_values=f[:, 128:160], imm_value=-3.0e38), reads=[f], writes=[f])
                S.op("dve", lambda f=f: nc.vector.max(out=f[:, 232:240], in_=f[:, 200:232]), reads=[f], writes=[f])
                sp_ = self.selpad[g]
                self.ts("dve", sp_, sp_[:, 64:96], f, imp, f[:, 239:240], NEG, ALU.is_lt, ALU.mult)

            self.pipe_unit(qk, pv)
            self.pipe_defer(epi)

        def epi_b(g):
            sp_ = self.selpad[g]
            ps = self.psum()
            self.mm(ps, ps[0:96, 0:128], sp_, sp_[:, :], idn, idn[:, :], True, True)
            self.cp("act", self.Qsel, Q[64:96, g, bl, :, :], ps, ps[64:96, 0:128].unsqueeze(1).broadcast_to([32, 4, 128]))
        for g in range(2):
            acc = self.ps[5]
            st = {"first": True}
            for kb in range(max(0, qb - 4), qb + 1):
                d = qb - kb

                def qk(g=g, kb=kb, d=d):
                    ps = self.psum()
                    msk = d == 0 or d == 4
                    self.mm(ps, ps[:, :], kwT, kwT_v[0:128, g, kb * 128:(kb + 1) * 128], QnT, q128s[g], True, not msk)
                    if msk:
                        nm = self.C["c_nm_incl"] if d == 0 else self.C["c_nm_win"]
                        self.mm(ps, ps[:, :], idn, idn[:, :], nm, nm[:, :], False, True)
                    pt = self.getPT()
                    self.act(pt, pt[:, :], ps, ps[:, :], AF.Exp)
                    return pt

                def pv(pt, g=g, kb=kb, acc=acc, st=st):
                    for h in range(4):
                        self.pvs(st, acc, acc[:, h * 65:(h + 1) * 65], pt, pt[:, h * 128:(h + 1) * 128], vsw, vsw_v[:, kb, 2 + g, :])

                self.pipe_unit(qk, pv)
            self.pipe_defer(lambda g=g, acc=acc: self.nsa_accum(acc, gat, gvs[g], 2, g))
        if self._deferred:
            self.pipe_drain()
        for g in range(2):
            epi_b(g)
        for g in range(2):
            acc = self.ps[4]
            st = {"first": True}
            for kb in range(qb + 1):
                def qk(g=g, kb=kb):
                    ps = self.psum()
                    diag = kb == qb
                    self.mm(ps, ps[:, :], ksT, ksT_v[0:128, g, kb * 128:(kb + 1) * 128], QnT, q128s[g], True, not diag,
                            extra=[self.Qsel])
                    if diag:
                        nm = self.C["c_nm_incl"]
                        self.mm(ps, ps[:, :], idn, idn[:, :], nm, nm[:, :], False, True)
                    pt = self.getPT()
                    self.act(pt, pt[:, :], ps, ps[:, :], AF.Exp)
                    return pt

                def pv(pt, g=g, kb=kb, acc=acc, st=st):
                    for h in range(4):
                        self.pvs(st, acc, acc[:, h * 65:(h + 1) * 65], pt, pt[:, h * 128:(h + 1) * 128], vsw, vsw_v[:, kb, g, :])

                self.pipe_unit(qk, pv)
            self.pipe_defer(lambda g=g, acc=acc: self.nsa_accum(acc, gat, gvs[g], 1, g))

    def nsa_accum(self, acc, gat, gv, k, g):
        S, nc = self.S, self.nc
        accv = acc[:, 0:260].rearrange("p (h d) -> p h d", h=4)
        sm = self.getsm()
        S.op("dve", lambda: nc.vector.reciprocal(out=sm[:, 0:4], in_=accv[:, :, 64]), reads=[acc], writes=[sm])
        self.tt("dve", sm, sm[:, 4:8], sm, sm[:, 0:4], gat, gv[:, :, k], ALU.mult)
        f = self.getf()
        fv = f[:, 0:256].rearrange("p (h d) -> p h d", h=4)
        self.tt("dve", f, fv, acc, accv[:, :, 0:64], sm, sm[:, 4:8].unsqueeze(2).broadcast_to([128, 4, 64]), ALU.mult)
        ov = self.oacc[:, g * 4:(g + 1) * 4, :]
        self.tt("pool", self.oacc, ov, self.oacc, ov, f, fv, ALU.add)

    def phase_sb(self):
        S, nc, l = self.S, self.nc, self.l
        W = self.W[l]
        ar = self.arena
        kT = Buf("sbk", ar[0:128, 0:8192])
        kT_v = kT[:, :].rearrange("p (h t) -> p h t", h=4)
        vb = Buf("sbv", ar[:, 8192:8192 + 4096])
        v_v = vb[:, :].rearrange("p (t c) -> p t c", t=NT)
        qT = Buf("sbq", ar[0:128, 12288:12288 + 2048])
        q_v = qT[:, :].rearrange("p (h t) -> p h t", h=4)
        self.memset("pool", kT, kT[64:128, :], 0.0)
        self.memset("pool", qT, qT[64:128, :], 0.0)
        lacc = Buf("lacc", ar[:, 14336:14336 + 1024].bitcast(F32))
        lacc16 = [Buf("lacc16_%d" % i, ar[:, 15360 + i * 512:15360 + (i + 1) * 512]) for i in range(3)]
        l16 = [Buf("l16_%d" % i, ar[:, 16896 + i * 512:16896 + (i + 1) * 512]) for i in range(2)]
        idn, tri, ones = self.C["c_ident"], self.C["c_tri"], self.C["c_ones"]
        self._rotbanks = [0, 1, 2, 5, 6]
        for c in range(NCH):
            hb = self.load_hT(c)
            wb, wv = self.load_wcols(W["win"], C_KSB, 512)
            for h in range(4):
                self.projT(kT, kT_v[0:64, h, c * TCH:(c + 1) * TCH], wb, wv, h * 64, 64, hb, hb[:], TCH,
                           evac="act" if h % 2 else "dve")
            for tl in range(4):
                ps = self.psum()
                for kc in range(KC):
                    self.mm(ps, ps[:, 0:256], hb, hb[:, kc, tl * 128:(tl + 1) * 128], wb, wv[:, kc, 256:512], kc == 0, kc == KC - 1)
                self.cp("act", vb, v_v[:, 4 * c + tl, :], ps, ps[:, 0:256])
        for c in range(NCH):
            hb = self.load_hT(c)
            wb, wv = self.load_wcols(W["win"], C_QS, 256)
            wob, wov = self.load_w(W["wout"][512:768, :].rearrange("(k p) n -> p k n", p=128),
                                   ("p (k n) -> p k n", dict(k=2)), 2048)
            for h in range(4):
                self.projT(qT, q_v[0:64, h, :], wb, wv, h * 64, 64, hb, hb[:], TCH, evac="act" if h % 2 else "dve")
            o16s = [self.sb_dummy(i) for i in range(4)]
            units = []
            for h in range(4):
                kbs = list(range(4 * c + 3, -1, -1))
                for j, kb in enumerate(kbs):
                    units.append(dict(h=h, kb=kb, first=j == 0, last=j == len(kbs) - 1,
                                      off=max(0, (kb - 4 * c) * 128), diag=kb >= 4 * c,
                                      acc=self.ps[3 + (h % 2)], st=None))
            sts = {}
            n = len(units)

            def stageA(i):
                u = units[i]
                h, kb, off = u["h"], u["kb"], u["off"]
                if u["first"]:
                    self.memset("pool", lacc, lacc[:, :], 0.0)
                    sts[h] = {"first": True}
                ks = kT_v[0:128, h, kb * 128:(kb + 1) * 128]
                ps1 = self.psum()
                self.mm(ps1, ps1[:, off:512], kT, ks, qT, q_v[0:128, h, off:512], True, True)
                sp = self.getf()
                self.act(sp, sp[:, off:512], ps1, ps1[:, off:512], AF.Exp, scale=-1.0)
                self.act(sp, sp[:, off:512], sp, sp[:, off:512], AF.Ln, bias=1.0, scale=1.0)
                lb = l16[i % 2]
                self.stt(lb, lb[:, off:512], ps1, ps1[:, off:512], -1.0, sp, sp[:, off:512], ALU.mult, ALU.subtract)
                if u["diag"]:
                    m01 = self.C["c_m01_strict"]
                    self.tt("pool", lb, lb[:, off:off + 128], lb, lb[:, off:off + 128], m01, m01[:, :], ALU.mult)
                if not u["last"]:
                    self.tt("dve", lacc, lacc[:, off:512], lacc, lacc[:, off:512], lb, lb[:, off:512], ALU.add)
                    la = lacc16[i % 3]
                    self.cp("dve", la, la[:, :], lacc, lacc[:, :])

            def stageB(i):
                u = units[i]
                h, kb, off = u["h"], u["kb"], u["off"]
                ks = kT_v[0:128, h, kb * 128:(kb + 1) * 128]
                lb = l16[i % 2]
                ps2 = self.psum()
                grp = [(ps2[:, off:512], kT, ks, qT, q_v[0:128, h, off:512]),
                       (ps2[:, off:512], tri, tri[:, :], lb, lb[:, off:512])]
                if not u["first"]:
                    la = lacc16[(i - 1) % 3]
                    grp.append((ps2[:, off:512], ones, ones[:, 0:128], la, la[:, off:512]))
                if u["diag"]:
                    nm = self.C["c_nm_strict"]
                    grp.append((ps2[:, off:off + 128], idn, idn[:, :], nm, nm[:, 0:128]))
                for gi, (o_, lb_, l_, rb_, r_) in enumerate(grp):
                    self.mm(ps2, o_, lb_, l_, rb_, r_, gi == 0, gi == len(grp) - 1)
                pt = self.getPT()
                self.act(pt, pt[:, off:512], ps2, ps2[:, off:512], AF.Exp)
                u["pt"] = pt

            def stageC(i):
                u = units[i]
                h, kb, off, acc, pt = u["h"], u["kb"], u["off"], u["acc"], u["pt"]
                for qbl in range(off // 128, 4):
                    self.pvs(sts[h], acc, acc[:, qbl * 64:(qbl + 1) * 64], pt, pt[:, qbl * 128:(qbl + 1) * 128],
                             vb, v_v[:, kb, h * 64:(h + 1) * 64])
                if u["last"]:
                    for qbl in range(4):
                        self.cp("dve", o16s[qbl], o16s[qbl][:, h * 64:(h + 1) * 64], acc, acc[:, qbl * 64:(qbl + 1) * 64])

            for i in range(n + 2):
                if i < n:
                    stageA(i)
                if 0 <= i - 1 < n:
                    stageB(i - 1)
                if 0 <= i - 2 < n:
                    stageC(i - 2)
            for qbl in range(4):
                self.out_proj(o16s[qbl], 256, wob, wov, 4 * c + qbl)
        self._rotbanks = [0, 1, 2]

    def sb_dummy(self, i):
        if not hasattr(self, "_o4"):
            self._o4 = [Buf("o4_%d" % j, self.o16[j // 2][:, (j % 2) * 256:(j % 2 + 1) * 256]) for j in range(4)]
        return self._o4[i]

    def phase_fox(self):
        S, nc, l = self.S, self.nc, self.l
        W = self.W[l]
        ar = self.arena
        kT = Buf("fxk", ar[0:128, 0:8192])
        kT_v = kT[:, :].rearrange("p (h t) -> p h t", h=4)
        self.memset("pool", kT, kT[64:128, :], 0.0)
        vb = Buf("fxv", ar[:, 8192:8192 + 4160])
        v_v = vb[:, :].rearrange("p (t h d) -> p t h d", t=NT, h=4)
        qT = Buf("fxq", ar[0:128, 12352:12352 + 2048])
        q_v = qT[:, :].rearrange("p (h t) -> p h t", h=4)
        self.memset("pool", qT, qT[64:128, :], 0.0)
        csp = Buf("csp", ar[0:4, 14400:14400 + 4096].bitcast(F32))
        hi = Buf("hi", ar[0:4, 18496:18496 + 2048])
        nhi = Buf("nhi", ar[0:4, 22016:22016 + 512])
        idn = self.C["c_ident"]
        place = self.C["c_place"][:, :].rearrange("p (k h m) -> p k h m", k=6, h=4)
        plb = self.C["c_place"]
        self.memset("pool", vb, v_v[:, :, :, 64:65], 1.0)
        self._rotbanks = [0, 1, 2, 5, 6]
        for c in range(NCH):
            hb = self.load_hT(c)
            wb, wv = self.load_wcols(W["win"], C_KF, 512)
            wfb, wfv = self.load_wcols(W["win"], C_FL, 4)
            t0 = c * TCH
            for h in range(4):
                self.projT(kT, kT_v[0:64, h, t0:t0 + TCH], wb, wv, h * 64, 64, hb, hb[:], TCH,
                           evac="act" if h % 2 else "dve")
            for tl in range(4):
                ps = self.psum()
                for kc in range(KC):
                    self.mm(ps, ps[:, 0:256], hb, hb[:, kc, tl * 128:(tl + 1) * 128], wb, wv[:, kc, 256:512], kc == 0, kc == KC - 1)
                self.cp("act", vb, v_v[:, 4 * c + tl, :, 0:64], ps, ps[:, 0:256].rearrange("p (h d) -> p h d", h=4))
            ps = self.psum()
            for kc in range(KC):
                self.mm(ps, ps[0:4, :], wfb, wfv[:, kc, 0:4], hb, hb[:, kc, :], kc == 0, kc == KC - 1)
            e, sp = self.getf(), self.getf()
            self.act(e, e[0:4, 0:TCH], ps, ps[0:4, :], AF.Exp, reads=[self.nbf], scale=-1.0, bias=self.nbf[:, self.l:self.l + 1])
            self.act(sp, sp[0:4, 0:TCH], e, e[0:4, 0:TCH], AF.Ln, bias=1.0, scale=1.0)
            init = 0.0 if c == 0 else csp[:, t0 - 1:t0]
            S.op("dve", lambda init=init, sp=sp, t0=t0: nc.vector.tensor_tensor_scan(
                out=csp[:, t0:t0 + TCH], data0=self.ones4b[:, :], data1=sp[0:4, 0:TCH], initial=init,
                op0=ALU.mult, op1=ALU.add), reads=[self.ones4b, sp, csp], writes=[csp])
            self.cp("dve", hi, hi[:, t0:t0 + TCH], csp, csp[:, t0:t0 + TCH])
            f = self.getf()
            self.tt("dve", f, f[0:4, 0:TCH], csp, csp[:, t0:t0 + TCH], hi, hi[:, t0:t0 + TCH], ALU.subtract)
            lo16 = self.getPT()
            self.cp("dve", lo16, lo16[0:4, 0:TCH], f, f[0:4, 0:TCH])
            for h in range(4):
                ps = self.psum()
                self.mm(ps, ps[0:68, :], plb, place[:, 3, h, :], hi, hi[:, t0:t0 + TCH], True, False)
                self.mm(ps, ps[0:68, :], plb, place[:, 4, h, :], lo16, lo16[0:4, 0:TCH], False, False)
                self.mm(ps, ps[0:68, :], plb, place[:, 5, h, :], self.ones4b, self.ones4b[:, :], False, True)
                self.cp("act", kT, kT_v[64:68, h, t0:t0 + TCH], ps, ps[64:68, :])
        for c in range(NCH):
            hb = self.load_hT(c)
            t0 = c * TCH
            wb, wv = self.load_wcols(W["win"], C_QF, 256)
            wob, wov = self.load_w(W["wout"][768:1024, :].rearrange("(k p) n -> p k n", p=128),
                                   ("p (k n) -> p k n", dict(k=2)), 2048)
            for h in range(4):
                self.projT(qT, q_v[0:64, h, :], wb, wv, h * 64, 64, hb, hb[:], TCH, evac="act" if h % 2 else "dve")
            self.ts("dve", nhi, nhi[:, 0:TCH], hi, hi[:, t0:t0 + TCH], -1.0, None, ALU.mult)
            f = self.getf()
            self.tt("dve", f, f[0:4, 0:TCH], hi, hi[:, t0:t0 + TCH], csp, csp[:, t0:t0 + TCH], ALU.subtract)
            nlo = self.getPT()
            self.cp("dve", nlo, nlo[0:4, 0:TCH], f, f[0:4, 0:TCH])
            for h in range(4):
                ps = self.psum()
                self.mm(ps, ps[0:68, :], plb, place[:, 0, h, :], nhi, nhi[:, 0:TCH], True, False)
                self.mm(ps, ps[0:68, :], plb, place[:, 1, h, :], nlo, nlo[0:4, 0:TCH], False, False)
                self.mm(ps, ps[0:68, :], plb, place[:, 2, h, :], self.ones4b, self.ones4b[:, :], False, True)
                self.cp("act", qT, q_v[64:68, h, :], ps, ps[64:68, :])
            o16s = [self.sb_dummy(i) for i in range(4)]
            for h in range(4):
                acc = self.ps[3 + (h % 2)]
                st = {"first": True}
                for kb in range(4 * c + 3, -1, -1):
                    off = max(0, (kb - 4 * c) * 128)
                    diag = kb >= 4 * c

                    def qk(h=h, kb=kb, off=off, diag=diag):
                        ps = self.psum()
                        self.mm(ps, ps[:, off:512], kT, kT_v[0:128, h, kb * 128:(kb + 1) * 128], qT, q_v[0:128, h, off:512], True, not diag)
                        if diag:
                            nm = self.C["c_nm_incl"]
                            self.mm(ps, ps[:, off:off + 128], idn, idn[:, :], nm, nm[:, 0:128], False, True)
                        pt = self.getPT()
                        self.act(pt, pt[:, off:512], ps, ps[:, off:512], AF.Exp)
                        return pt

                    def pv(pt, h=h, kb=kb, off=off, acc=acc, st=st):
                        for qbl in range(off // 128, 4):
                            self.pvs(st, acc, acc[:, qbl * 65:(qbl + 1) * 65], pt, pt[:, qbl * 128:(qbl + 1) * 128],
                                     vb, v_v[:, kb, h, :])

                    self.pipe_unit(qk, pv)

                def epi(h=h, acc=acc):
                    accv = acc[:, 0:260].rearrange("p (b d) -> p b d", b=4)
                    sm = self.getsm()
                    S.op("dve", lambda: nc.vector.reciprocal(out=sm[:, 0:4], in_=accv[:, :, 64]), reads=[acc], writes=[sm])
                    for qbl in range(4):
                        self.ts("dve", o16s[qbl], o16s[qbl][:, h * 64:(h + 1) * 64], acc, accv[:, qbl, 0:64],
                                sm[:, qbl:qbl + 1], None, ALU.mult, reads=[sm])
                self.pipe_defer(epi)
            self.pipe_drain()
            for qbl in range(4):
                self.out_proj(o16s[qbl], 256, wob, wov, 4 * c + qbl)
        self._rotbanks = [0, 1, 2]

    def phase_mem(self):
        S, nc, l = self.S, self.nc, self.l
        W = self.W[l]
        ar = self.arena
        mx = Buf("mx", ar[:, 0:4096].bitcast(F32))
        mx_v = mx[:, :].rearrange("p (t d) -> p t d", t=2)
        mT = Buf("mT", ar[:, 4096:4096 + 2048])
        mT_v = mT[:, :].rearrange("p (k t) -> p k t", k=KC)
        kT = Buf("mk", ar[0:128, 6144:6144 + 1024])
        kT_v = kT[:, :].rearrange("p (h t) -> p h t", h=4)
        self.memset("pool", kT, kT[64:128, :], 0.0)
        vb = Buf("mv", ar[:, 7168:7168 + 520])
        v_v = vb[:, :].rearrange("p (t h d) -> p t h d", t=2, h=4)
        qT = Buf("mq", ar[0:128, 7688:7688 + 2048])
        q_v = qT[:, :].rearrange("p (h t) -> p h t", h=4)
        self.memset("pool", qT, qT[64:128, :], 0.0)
        idn = self.C["c_ident"]
        for t in range(2):
            S.dma(mx_v[:, t, :], self.din["mem"][self.s, t * 128:(t + 1) * 128, :], writes=[mx])
        self.memset("pool", vb, v_v[:, :, :, 64:65], 1.0)
        for t in range(2):
            sm = self.getsm()
            h = self.h16[self._h16rot]
            self._h16rot ^= 1
            self.act([h, sm], h[:], mx, mx_v[:, t, :], AF.Square, accum_out=sm[:, 0:1])
            self.act(sm, sm[:, 1:2], sm, sm[:, 0:1], AF.Ln, scale=1.0 / DM, bias=EPS)
            self.act(sm, sm[:, 2:3], sm, sm[:, 1:2], AF.Exp, scale=-0.5)
            self.ts("dve", h, h[:], mx, mx_v[:, t, :], sm[:, 2:3], None, ALU.mult, reads=[sm])
            for kc in range(KC):
                S.op("pe", lambda kc=kc, h=h: nc.tensor.transpose(
                    out=self.pst[:, kc * 128:(kc + 1) * 128], in_=h[:, kc * 128:(kc + 1) * 128],
                    identity=idn[:]), reads=[h, idn], writes=[self.pst])
            self.cp("dve", mT, mT_v[:, :, t * 128:(t + 1) * 128], self.pst, self.pst[:, :].rearrange("p (k t) -> p k t", k=KC))
        wb, wv = self.load_wcols(W["mk"], 0, 256)
        for h in range(4):
            self.projT(kT, kT_v[0:64, h, :], wb, wv, h * 64, 64, mT, mT_v, 256)
        wb, wv = self.load_wcols(W["mv"], 0, 256)
        for t in range(2):
            ps = self.psum()
            for kc in range(KC):
                self.mm(ps, ps[:, 0:256], mT, mT_v[:, kc, t * 128:(t + 1) * 128], wb, wv[:, kc, :], kc == 0, kc == KC - 1)
            self.cp("act", vb, v_v[:, t, :, 0:64], ps, ps[:, 0:256].rearrange("p (h d) -> p h d", h=4))
        wqb, wqv = self.load_wcols(W["mq"], 0, 256)
        wob, wov = self.load_w(W["mo"].rearrange("(k p) n -> p k n", p=128), ("p (k n) -> p k n", dict(k=2)), 2048)
        for c in range(NCH):
            hb = self.gethT()
            self.rmsnorm_T(range(4 * c, 4 * c + 4), hb, hb[:], 0)
            for h in range(4):
                self.projT(qT, q_v[0:64, h, :], wqb, wqv, h * 64, 64, hb, hb[:], TCH, evac="act" if h % 2 else "dve")
            o16s = [self.sb_dummy(i) for i in range(4)]
            for h in range(4):
                acc = self.ps[3 + (h % 2)]
                st = {"first": True}
                for kb in range(2):
                    def qk(h=h, kb=kb):
                        ps = self.psum()
                        self.mm(ps, ps[:, :], kT, kT_v[0:128, h, kb * 128:(kb + 1) * 128], qT, q_v[0:128, h, :], True, True)
                        pt = self.getPT()
                        self.act(pt, pt[:, :], ps, ps[:, :], AF.Exp)
                        return pt

                    def pv(pt, h=h, kb=kb, acc=acc, st=st):
                        for qbl in range(4):
                            self.pvs(st, acc, acc[:, qbl * 65:(qbl + 1) * 65], pt, pt[:, qbl * 128:(qbl + 1) * 128], vb, v_v[:, kb, h, :])

                    self.pipe_unit(qk, pv)

                def epi(h=h, acc=acc):
                    accv = acc[:, 0:260].rearrange("p (b d) -> p b d", b=4)
                    sm = self.getsm()
                    S.op("dve", lambda: nc.vector.reciprocal(out=sm[:, 0:4], in_=accv[:, :, 64]), reads=[acc], writes=[sm])
                    for qbl in range(4):
                        self.ts("dve", o16s[qbl], o16s[qbl][:, h * 64:(h + 1) * 64], acc, accv[:, qbl, 0:64],
                                sm[:, qbl:qbl + 1], None, ALU.mult, reads=[sm])
                self.pipe_defer(epi)
            self.pipe_drain()
            for qbl in range(4):
                self.out_proj(o16s[qbl], 256, wob, wov, 4 * c + qbl)
        S.barrier()

    def phase_ffn(self):
        S, nc, l = self.S, self.nc, self.l
        W = self.W[l]
        ar = self.arena
        gT = Buf("gT", ar[:, 0:11264])
        g_v = gT[:, :].rearrange("p (k t) -> p k t", k=22)
        halo = Buf("halo", ar[:, 11264:11264 + 176].bitcast(F32))
        halo_v = halo[:, :].rearrange("p (c k) -> p c k", k=2)
        self.memset("pool", halo, halo[:, :], 0.0)
        cw = self.cw
        for c in range(NCH):
            hb = self.gethT()
            self.rmsnorm_T(range(4 * c, 4 * c + 4), hb, hb[:], 0)
            wcur = {}
            uy = {}

            def stage1(cc):
                cg, ci = cc // 4, cc % 4
                if ci == 0:
                    wcur["w"] = self.load_wcols(W["up"], cg * 512, 512)
                wb, wv = wcur["w"]
                ps = self.psum()
                for kc in range(KC):
                    self.mm(ps, ps[:, :], wb, wv[:, kc, ci * 128:(ci + 1) * 128], hb, hb[:, kc, :], kc == 0, kc == KC - 1)
                u = self.getf()
                y = self.getf()
                self.cp("pool", u, u[:, 0:2], halo, halo_v[:, cc, :])
                self.cp("act", u, u[:, 2:514], ps, ps[:, :])
                self.act(y, y[:, 0:512], ps, ps[:, :], AF.Copy, reads=[cw], scale=cw[:, l, cc, 2:3])
                self.cp("pool", halo, halo_v[:, cc, :], u, u[:, 512:514])
                uy[cc] = [u, y]

            def stage2(cc):
                u, y = uy[cc]
                self.stt(y, y[:, 0:512], u, u[:, 1:513], cw[:, l, cc, 1:2], y, y[:, 0:512], ALU.mult, ALU.add, reads=[cw])
                self.stt(y, y[:, 0:512], u, u[:, 0:512], cw[:, l, cc, 0:1], y, y[:, 0:512], ALU.mult, ALU.add, reads=[cw])

            def stage3(cc):
                y = uy.pop(cc)[1]
                if cc < 22:
                    self.act(gT, g_v[:, cc, :], y, y[:, 0:512], AF.Silu, reads=[cw], bias=cw[:, l, cc, 3:4], scale=1.0)
                else:
                    self.stt(gT, g_v[:, cc - 22, :], y, y[:, 0:512], cw[:, l, cc, 3:4], gT, g_v[:, cc - 22, :],
                             ALU.add, ALU.mult, reads=[cw])

            for i in range(44 + 2):
                if i < 44:
                    stage1(i)
                if 0 <= i - 1 < 44:
                    stage2(i - 1)
                if 0 <= i - 2 < 44:
                    stage3(i - 2)
            for n in range(2):
                accs = [self.ps[3 + tl] for tl in range(4)]
                for pc in range(6):
                    k0 = pc * 4
                    nk = min(4, 22 - k0)
                    wdb, wdv = self.load_w(W["down"][k0 * 128:(k0 + nk) * 128, n * 512:(n + 1) * 512].rearrange("(k p) n -> p k n", p=128),
                                           ("p (k n) -> p k n", dict(k=nk)), nk * 512)
                    for tl in range(4):
                        for k in range(nk):
                            self.mm(accs[tl], accs[tl][:, :], gT, g_v[:, k0 + k, tl * 128:(tl + 1) * 128], wdb, wdv[:, k, :],
                                    k0 + k == 0, k0 + k == 21)
                for tl in range(4):
                    t = 4 * c + tl
                    xb, xa = self.xt[t], self.xres_t[:, t, :]
                    self.tt("dve", xb, xa[:, n * 512:(n + 1) * 512], xb, xa[:, n * 512:(n + 1) * 512], accs[tl], accs[tl][:, :], ALU.add)
        S.barrier()


_PROG = {}


def _get_prog(nseq):
    if nseq not in _PROG:
        _PROG[nseq] = K(nseq)
    return _PROG[nseq]


def kernel(**inputs):
    x = np.ascontiguousarray(np.asarray(inputs["x"], dtype=np.float32))
    mem = np.ascontiguousarray(np.asarray(inputs["mem"], dtype=np.float32))
    B = x.shape[0]
    ncores = 8
    per = B // ncores
    prog = _get_prog(per)
    consts = _consts()
    in_maps = []
    for i in range(ncores):
        m = {"x": x[i * per:(i + 1) * per], "mem": mem[i * per:(i + 1) * per]}
        for k, v in inputs.items():
            if k not in ("x", "mem"):
                m[k] = np.ascontiguousarray(np.asarray(v, dtype=np.float32))
        m.update(consts)
        in_maps.append(m)
    res = run_bass_kernel_spmd(prog.nc, in_maps, core_ids=list(range(ncores)))
    return np.concatenate([np.asarray(r["y"], dtype=np.float32) for r in res.results], axis=0)
```

```python
import numpy as np
import ml_dtypes
import concourse.bass as bass
import concourse.mybir as mybir
from concourse.bass_utils import run_bass_kernel_spmd

F32 = mybir.dt.float32
BF16 = mybir.dt.bfloat16
AF = mybir.ActivationFunctionType
ALU = mybir.AluOpType

SEQ, DM, KC, NT, TCH, NCH = 2048, 1024, 8, 16, 512, 4
DEPTH = 4
DFF = 2816
NEG = -30000.0
BIG = 1.0e30
EPS = 1e-6
C_QN, C_KC, C_VC, C_KS, C_VS, C_KW, C_VW, C_G = 0, 512, 640, 768, 896, 1024, 1152, 1280
C_QS, C_KSB, C_VSB, C_QF, C_KF, C_VF, C_FL = 1304, 1560, 1816, 2072, 2328, 2584, 2840
INC = 2844


class Buf:
    __slots__ = ("name", "t", "lw", "rd", "excl")

    def __init__(self, name, t, excl=False):
        self.name, self.t, self.lw, self.rd, self.excl = name, t, None, {}, excl

    def __getitem__(self, idx):
        return self.t[idx]


class Sched:
    NDSEM = 24

    def __init__(self, nc):
        self.nc = nc
        self.E = {"pe": nc.tensor, "act": nc.scalar, "dve": nc.vector, "pool": nc.gpsimd, "sp": nc.sync}
        self.sems, self.cnt = {}, {}
        for k in self.E:
            self.sems[k] = nc.alloc_semaphore("s_" + k)
            self.cnt[k] = 0
        for i in range(self.NDSEM):
            self.sems[("d", i)] = nc.alloc_semaphore("d%d" % i)
            self.cnt[("d", i)] = 0
        self.seen = {k: {} for k in self.E}
        self.dnext = 0
        self.ninstr = 0
        self.nwait = 0

    def _deps(self, reads, writes):
        deps = {}

        def add(kv):
            if kv is not None and deps.get(kv[0], 0) < kv[1]:
                deps[kv[0]] = kv[1]

        for b in reads:
            add(b.lw)
            if b.excl:
                for kv in b.rd.items():
                    add(kv)
        for b in writes:
            add(b.lw)
            for kv in b.rd.items():
                add(kv)
        return deps

    def _wait(self, eng, deps):
        seen, e = self.seen[eng], self.E[eng]
        for k, v in deps.items():
            if k == "pe" and eng == "pe":
                continue
            if seen.get(k, 0) >= v:
                continue
            e.wait_ge(self.sems[k], v)
            seen[k] = v
            self.nwait += 1

    def _mark(self, key, val, reads, writes):
        for b in reads:
            if b.excl:
                b.lw, b.rd = (key, val), {}
            else:
                b.rd[key] = val
        for b in writes:
            b.lw, b.rd = (key, val), {}

    def op(self, eng, fn, reads=(), writes=()):
        self._wait(eng, self._deps(reads, writes))
        ins = fn()
        self.cnt[eng] += 1
        ins.then_inc(self.sems[eng], 1)
        self._mark(eng, self.cnt[eng], reads, writes)
        self.ninstr += 1
        return ins

    def dma(self, out_ap, in_ap, reads=(), writes=(), q="sp", **kw):
        deps = self._deps(reads, writes)
        dk = ("d", self.dnext)
        self.dnext = (self.dnext + 1) % self.NDSEM
        if self.cnt[dk] > deps.get(dk, 0):
            deps[dk] = self.cnt[dk]
        self._wait(q, deps)
        ins = self.E[q].dma_start(out=out_ap, in_=in_ap, **kw)
        self.cnt[dk] += 16
        ins.then_inc(self.sems[dk], 16)
        self._mark(dk, self.cnt[dk], reads, writes)
        self.ninstr += 1
        return ins

    def barrier(self):
        deps = {k: v for k, v in self.cnt.items() if v > 0 and k != "sp"}
        for eng in self.E:
            self._wait(eng, dict(deps))


def _consts():
    bf = ml_dtypes.bfloat16
    c = {}
    half = 8
    inv = 500000.0 ** (-np.arange(half, dtype=np.float32) / half)
    ang = np.arange(SEQ, dtype=np.float32)[None, :] * inv[:, None]
    cs = np.concatenate([np.cos(ang), np.cos(ang)], 0).astype(np.float32)
    sn = np.concatenate([np.sin(ang), np.sin(ang)], 0).astype(np.float32)
    c["c_cos"], c["c_sin"] = cs.astype(bf), sn.astype(bf)
    j = np.arange(128)[:, None]
    t = np.arange(128)[None, :]
    c["c_ident"] = np.eye(128, dtype=np.float32).astype(bf)
    c["c_identf"] = np.eye(4, dtype=np.float32)
    c["c_ones"] = np.ones((128, 512), np.float32).astype(bf)
    c["c_tri"] = (j >= t).astype(np.float32).astype(bf)
    c["c_nm_incl"] = np.tile(np.where(j > t, NEG, 0.0), (1, 4)).astype(np.float32).astype(bf)
    c["c_nm_strict"] = np.tile(np.where(j >= t, NEG, 0.0), (1, 4)).astype(np.float32).astype(bf)
    c["c_nm_win"] = np.tile(np.where(j <= t, NEG, 0.0), (1, 4)).astype(np.float32).astype(bf)
    c["c_m01_strict"] = (j < t).astype(np.float32).astype(bf)
    n = np.arange(128)[:, None]
    tt = np.arange(SEQ)[None, :]
    c["c_nm_cmp"] = np.where(16 * n + 31 > tt, NEG, 0.0).astype(np.float32).astype(bf)
    starts = np.arange(127) * 16
    sel_starts = np.arange(32) * 64
    ovl = ((starts[:, None] < sel_starts[None, :] + 64) & (starts[:, None] + 32 > sel_starts[None, :]))
    vext = np.zeros((128, 33), np.float32)
    vext[:, 0] = 1.0
    vext[:127, 1:] = ovl
    c["c_vext"] = vext.astype(bf)
    onehot = (np.arange(SEQ)[None, :] // 64 == np.arange(32)[:, None]).astype(np.float32)
    c["c_blk1h"] = onehot.astype(bf)
    tq = np.arange(SEQ)
    cur = (tq // 64)[:, None]
    bid = np.arange(32)[None, :]
    forced = (bid == 0) | (bid == cur) | (bid == cur - 1)
    am = np.where(bid <= cur, np.where(forced, BIG, 0.0), -BIG).astype(np.float32)
    c["c_addmask"] = np.ascontiguousarray(am.reshape(16, 128, 32).transpose(1, 0, 2)).astype(bf)
    pl = np.zeros((4, 6, 4, 68), np.float32)
    for h in range(4):
        pl[h, 0, h, 64] = 1.0
        pl[h, 1, h, 65] = 1.0
        pl[0, 2, h, 66] = 1.0
        pl[0, 2, h, 67] = 1.0
        pl[h, 3, h, 66] = 1.0
        pl[h, 4, h, 67] = 1.0
        pl[0, 5, h, 64] = 1.0
        pl[0, 5, h, 65] = 1.0
    c["c_place"] = pl.reshape(4, 6 * 4 * 68).astype(bf)
    return c


_CONST_SPECS = None


class K:
    def __init__(self, nseq, nlayers=DEPTH, phases=("nsa", "sb", "fox", "mem", "ffn"), final=True):
        self.nseq, self.nlayers, self.phases, self.final = nseq, nlayers, phases, final
        nc = self.nc = bass.Bass("TRN2", target_bir_lowering=False)
        self.S = Sched(nc)
        self._n = 0
        self.din = {}
        self.declare_io()
        self.alloc()
        self.load_consts()
        self.prep_all()
        print("sbuf bytes remaining", nc.sbuf_bytes_remaining)
        for s in range(nseq):
            self.run_seq(s)
            self.S.barrier()
        self.finish()

    def dram_in(self, name, shape, dt=F32):
        self.din[name] = self.nc.dram_tensor(name, list(shape), dt, kind="ExternalInput").ap()
        return self.din[name]

    def declare_io(self):
        nc, L = self.nc, DEPTH
        self.dram_in("x", [self.nseq, SEQ, DM])
        self.dram_in("mem", [self.nseq, 256, DM])
        for nm, shp in [("norm_mix", [L, DM]), ("w_in", [L, DM, INC]), ("b_forget", [L, 4]),
                        ("cmp_pe_k", [L, 32, 64]), ("cmp_pe_v", [L, 32, 64]),
                        ("cmp_wk1", [L, 32, 64, 64]), ("cmp_wk2", [L, 64, 64]),
                        ("cmp_wv1", [L, 32, 64, 64]), ("cmp_wv2", [L, 64, 64]),
                        ("w_out", [L, DM, DM]), ("norm_cross", [L, DM]), ("norm_mem", [L, DM]),
                        ("w_mq", [L, DM, 256]), ("w_mk", [L, DM, 256]), ("w_mv", [L, DM, 256]),
                        ("w_mo", [L, 256, DM]), ("norm_ffn", [L, DM]), ("w_up", [L, DM, 2 * DFF]),
                        ("conv_w", [L, 3, 2 * DFF]), ("conv_b", [L, 2 * DFF]), ("w_down", [L, DFF, DM]),
                        ("norm_final", [DM])]:
            self.dram_in(nm, shp)
        for nm, arr in _consts().items():
            self.dram_in(nm, arr.shape, BF16 if arr.dtype == ml_dtypes.bfloat16 else F32)
        self.y = nc.dram_tensor("y", [self.nseq, SEQ, DM], F32, kind="ExternalOutput").ap()
        d = lambda nm, shp: nc.dram_tensor(nm, shp, BF16).ap()
        self.W = []
        for l in range(self.nlayers):
            self.W.append(dict(
                win=d("b_win%d" % l, [DM, INC]), rot=d("b_rot%d" % l, [DM, 224]),
                wout=d("b_wout%d" % l, [DM, DM]), mq=d("b_mq%d" % l, [DM, 256]),
                mk=d("b_mk%d" % l, [DM, 256]), mv=d("b_mv%d" % l, [DM, 256]),
                mo=d("b_mo%d" % l, [256, DM]), up=d("b_up%d" % l, [DM, 2 * DFF]),
                down=d("b_down%d" % l, [DFF, DM]),
                w1k=d("b_w1k%d" % l, [64, 32 * 64]), w1v=d("b_w1v%d" % l, [64, 32 * 64]),
                w2k=d("b_w2k%d" % l, [64, 64]), w2v=d("b_w2v%d" % l, [64, 64]),
                pek=d("b_pek%d" % l, [32, 64]), pev=d("b_pev%d" % l, [32, 64])))
        self.hT_d = nc.dram_tensor("b_hT", [KC, 128, SEQ], BF16).ap()
        self.hT_dbuf = Buf("hT_d", None)
        self.wbuf_d = Buf("wdram", None)

    def sb(self, name, shape, dt=F32):
        return Buf(name, self.nc.alloc_sbuf_tensor(name, list(shape), dt))

    def alloc(self):
        nc = self.nc
        self.xres_t = nc.alloc_sbuf_tensor("xres", [128, NT, DM], F32)
        self.xt = [Buf("x%d" % i, self.xres_t) for i in range(NT)]
        self.ps = [Buf("ps%d" % i, nc.alloc_psum_tensor("ps%d" % i, [128, 512], F32), excl=True) for i in range(7)]
        self.pst = Buf("pst", nc.alloc_psum_tensor("pst", [128, 1024], BF16), excl=True)
        self._rot = 0
        self._rotbanks = [0, 1, 2]
        self._pending = None
        self._deferred = []
        self.wbuf = [self.sb("wbuf%d" % i, [128, 4096], BF16) for i in range(4)]
        self._wrot = 0
        self.hTb = [self.sb("hTb%d" % i, [128, KC, TCH], BF16) for i in range(2)]
        self._hrot = 0
        self.ARENA = 22528
        self.arena = nc.alloc_sbuf_tensor("arena", [128, self.ARENA], BF16)
        self.PT = [self.sb("PT%d" % i, [128, 512], BF16) for i in range(3)]
        self._prot = 0
        self.wf = [self.sb("wf%d" % i, [128, 514], F32) for i in range(5)]
        self._frot = 0
        self.h16 = [self.sb("h16_%d" % i, [128, DM], BF16) for i in range(2)]
        self._h16rot = 0
        self.small = [self.sb("sm%d" % i, [128, 64], F32) for i in range(4)]
        self._srot = 0
        self.o16 = [self.sb("o16_%d" % i, [128, 512], BF16) for i in range(2)]
        self._orot = 0
        self.oT = [self.sb("oT%d" % i, [128, 4, 128], BF16) for i in range(2)]
        self._otrot = 0
        self.oaccs = [self.sb("oacc%d" % i, [128, 8, 64], F32) for i in range(2)]
        self.oacc = self.oaccs[0]
        self.vcc = self.sb("vcc", [128, 2, 97], BF16)
        self.selpad = [self.sb("selpad%d" % g, [128, 96], BF16) for g in range(2)]

    def psum(self):
        rb = self._rotbanks
        self._rot = (self._rot + 1) % len(rb)
        return self.ps[rb[self._rot]]

    def pipe_unit(self, qk, pv):
        pt = qk()
        self.pipe_flush_pv()
        self._run_deferred(False)
        self._pending = (pv, pt)

    def _run_deferred(self, force):
        d, self._deferred = self._deferred, []
        keep = []
        for (f, n) in d:
            if n <= 0 or force:
                f()
            else:
                keep.append((f, n - 1))
        self._deferred = keep + self._deferred

    def pipe_flush_pv(self):
        if self._pending is not None:
            pv, pt = self._pending
            self._pending = None
            pv(pt)

    def pipe_defer(self, fn, delay=0):
        self._deferred.append((fn, delay))

    def pipe_drain(self):
        self.pipe_flush_pv()
        while self._deferred:
            self._run_deferred(True)

    def getw(self):
        b = self.wbuf[self._wrot]
        self._wrot = (self._wrot + 1) % 4
        return b

    def gethT(self):
        b = self.hTb[self._hrot]
        self._hrot = (self._hrot + 1) % 2
        return b

    def getPT(self):
        b = self.PT[self._prot]
        self._prot = (self._prot + 1) % 3
        return b

    def getf(self):
        b = self.wf[self._frot]
        self._frot = (self._frot + 1) % 5
        return b

    def getsm(self):
        b = self.small[self._srot]
        self._srot = (self._srot + 1) % 4
        return b

    def mm(self, ob, out, lb, lhsT, rb, rhs, start, stop, extra=()):
        nc = self.nc
        self.S.op("pe", lambda: nc.tensor.matmul(out, lhsT=lhsT, rhs=rhs, start=start, stop=stop,
                                                 skip_group_check=True),
                  reads=[lb, rb] + list(extra), writes=[ob])

    def act(self, ob, out, ib, in_, func, reads=(), **kw):
        nc = self.nc
        self.S.op("act", lambda: nc.scalar.activation(out=out, in_=in_, func=func, **kw),
                  reads=[ib] + list(reads), writes=[ob] if not isinstance(ob, (list, tuple)) else list(ob))

    def ts(self, eng, ob, out, ib, in0, s1, s2, op0, op1=None, reads=()):
        e = self.S.E[eng]
        if op1 is None:
            fn = lambda: e.tensor_scalar(out=out, in0=in0, scalar1=s1, scalar2=None, op0=op0)
        else:
            fn = lambda: e.tensor_scalar(out=out, in0=in0, scalar1=s1, scalar2=s2, op0=op0, op1=op1)
        self.S.op(eng, fn, reads=[ib] + list(reads), writes=[ob])

    def tt(self, eng, ob, out, ab, a, bb, b, op):
        e = self.S.E[eng]
        self.S.op(eng, lambda: e.tensor_tensor(out=out, in0=a, in1=b, op=op), reads=[ab, bb], writes=[ob])

    def stt(self, ob, out, ab, in0, scalar, bb, in1, op0, op1, reads=()):
        nc = self.nc
        self.S.op("dve", lambda: nc.vector.scalar_tensor_tensor(out=out, in0=in0, scalar=scalar, in1=in1,
                                                                op0=op0, op1=op1),
                  reads=[ab, bb] + list(reads), writes=[ob])

    def cp(self, eng, ob, out, ib, in_):
        if eng == "act":
            nc = self.nc
            self.S.op("act", lambda: nc.scalar.copy(out=out, in_=in_), reads=[ib], writes=[ob])
        else:
            e = self.S.E[eng]
            self.S.op(eng, lambda: e.tensor_copy(out=out, in_=in_), reads=[ib], writes=[ob])

    def memset(self, eng, ob, ap, val):
        e = self.S.E[eng]
        self.S.op(eng, lambda: e.memset(ap, val), writes=[ob])

    def load_consts(self):
        S = self.S
        self.C = {}
        for nm, arr in _consts().items():
            shp = list(arr.shape)
            if nm == "c_blk1h":
                continue
            b = self.sb("k_" + nm, shp, BF16 if arr.dtype == ml_dtypes.bfloat16 else F32)
            S.dma(b[:], self.din[nm], writes=[b])
            self.C[nm] = b
        L = self.nlayers
        self.gain = {}
        for nm in ("norm_mix", "norm_cross", "norm_mem", "norm_ffn"):
            g = self.sb("g_" + nm, [128, L, KC], F32)
            for l in range(L):
                S.dma(g[:, l, :], self.din[nm][l].rearrange("(k p) -> p k", p=128), writes=[g],
                      allow_slow_non_contiguous=True)
            self.gain[nm] = g
        g8 = self.sb("g8_mix", [128, L, KC], F32)
        self.ts("dve", g8, g8[:], self.gain["norm_mix"], self.gain["norm_mix"][:], 0.125, None, ALU.mult)
        self.gain["norm_mix8"] = g8
        g8c = self.sb("g8_cross", [128, L, KC], F32)
        self.ts("dve", g8c, g8c[:], self.gain["norm_cross"], self.gain["norm_cross"][:], 0.125, None, ALU.mult)
        self.gain["norm_cross8"] = g8c
        self.nbf = self.sb("nbf", [4, L], F32)
        S.dma(self.nbf[:], self.din["b_forget"][0:L].rearrange("l h -> h l"), writes=[self.nbf],
              allow_slow_non_contiguous=True)
        self.ts("dve", self.nbf, self.nbf[:], self.nbf, self.nbf[:], -1.0, None, ALU.mult)
        self.cw = self.sb("cw", [128, L, 44, 4], F32)
        crow = Buf("crow", self.arena[0:4, 0:4 * DFF].bitcast(F32))
        idf = self.C["c_identf"]
        for l in range(L):
            S.dma(crow[0:3, :], self.din["conv_w"][l], writes=[crow])
            S.dma(crow[3:4, :], self.din["conv_b"][l:l + 1, :], writes=[crow])
            for c0 in range(0, 44, 11):
                ps = self.psum()
                for cc in range(c0, c0 + 11):
                    nc = self.nc
                    S.op("pe", lambda cc=cc, ps=ps: nc.tensor.transpose(
                        out=ps[:, (cc - c0) * 4:(cc - c0) * 4 + 4], in_=crow[0:4, cc * 128:(cc + 1) * 128],
                        identity=idf[0:4, 0:4]), reads=[crow, idf], writes=[ps])
                self.cp("dve", self.cw, self.cw[:, l, c0:c0 + 11, :].rearrange("p c k -> p (c k)"), ps, ps[:, 0:44])
        for g in range(2):
            self.memset("pool", self.selpad[g], self.selpad[g][:], 0.0)
        for g in range(2):
            self.cp("pool", self.vcc, self.vcc[:, g, 64:97], self.C["c_vext"], self.C["c_vext"][:, :])
        self.ones4b = Buf("ones4b", self.C["c_ones"][0:4, :])
        S.barrier()

    def prep_all(self):
        S, nc = self.S, self.nc
        ar = self.arena
        NS = 3
        pin = [Buf("pin%d" % i, ar[:, i * 4096:(i + 1) * 4096].bitcast(F32)) for i in range(NS)]
        pout = [Buf("pout%d" % i, ar[:, 12288 + i * 2048: 12288 + (i + 1) * 2048]) for i in range(NS)]
        rott = Buf("rott", ar[:, 18432:18432 + 224])
        self._pk = 0
        engs = ["act", "dve"]

        def piece(src, dst, rows, cols, scale, after=None):
            i = self._pk % NS
            eng = engs[self._pk % 2]
            self._pk += 1
            S.dma(pin[i][0:rows, 0:cols], src, writes=[pin[i]])
            o, a = pout[i][0:rows, 0:cols], pin[i][0:rows, 0:cols]
            if isinstance(scale, tuple):
                sbuf, sap = scale
                if eng == "act":
                    self.act(pout[i], o, pin[i], a, AF.Copy, reads=[sbuf], scale=sap)
                else:
                    self.ts(eng, pout[i], o, pin[i], a, sap, None, ALU.mult, reads=[sbuf])
            else:
                if eng == "act":
                    self.act(pout[i], o, pin[i], a, AF.Copy, scale=float(scale))
                else:
                    self.ts(eng, pout[i], o, pin[i], a, float(scale), None, ALU.mult)
            if after is not None:
                after(pout[i])
            S.dma(dst, pout[i][0:rows, 0:cols], reads=[pout[i]], q="pool")

        def mat(src, dst, R, Ccols, gain=None, l=0, scale=1.0):
            for r0 in range(0, R, 128):
                rows = min(128, R - r0)
                for c0 in range(0, Ccols, 2048):
                    cols = min(2048, Ccols - c0)
                    sc = (gain, gain[0:rows, l, r0 // 128:r0 // 128 + 1]) if gain is not None else scale
                    piece(src[r0:r0 + rows, c0:c0 + cols], dst[r0:r0 + rows, c0:c0 + cols], rows, cols, sc)

        for l in range(self.nlayers):
            W, D = self.W[l], self.din
            g, g8 = self.gain["norm_mix"], self.gain["norm_mix8"]
            for rc in range(KC):
                r0 = rc * 128
                gs, g8s = (g, g[:, l, rc:rc + 1]), (g8, g8[:, l, rc:rc + 1])

                def rot_ops(src_off, nh, roff):
                    def f(pb):
                        v = pb[:, src_off:src_off + 64 * nh].rearrange("p (h d) -> p h d", d=64)
                        rt = rott[:, :].rearrange("p (h d) -> p h d", d=16)
                        self.ts("dve", rott, rt[:, roff:roff + nh, 0:8], pb, v[:, :, 8:16], -1.0, None, ALU.mult)
                        self.cp("dve", rott, rt[:, roff:roff + nh, 8:16], pb, v[:, :, 0:8])
                    return f

                def rot2(pb):
                    rot_ops(0, 2, 8)(pb)
                    rot_ops(256, 2, 10)(pb)
                    rot_ops(512, 2, 12)(pb)

                segs = [(0, 512, g8s, rot_ops(0, 8, 0)), (512, 1304, gs, rot2), (1304, 1560, g8s, None),
                        (1560, 2072, gs, None), (2072, 2328, g8s, None), (2328, 2844, gs, None)]
                for (a, b, sc, aft) in segs:
                    piece(D["w_in"][l, r0:r0 + 128, a:b], W["win"][r0:r0 + 128, a:b], 128, b - a, sc, aft)
                S.dma(W["rot"][r0:r0 + 128, :], rott[:, :], reads=[rott])
            mat(D["w_out"][l], W["wout"], DM, DM)
            mat(D["w_mq"][l], W["mq"], DM, 256, gain=self.gain["norm_cross8"], l=l)
            mat(D["w_mk"][l], W["mk"], DM, 256, gain=self.gain["norm_mem"], l=l)
            mat(D["w_mv"][l], W["mv"], DM, 256, gain=self.gain["norm_mem"], l=l)
            mat(D["w_mo"][l], W["mo"], 256, DM)
            mat(D["w_up"][l], W["up"], DM, 2 * DFF, gain=self.gain["norm_ffn"], l=l)
            mat(D["w_down"][l], W["down"], DFF, DM)
            for (sn, dn) in (("cmp_wk1", "w1k"), ("cmp_wv1", "w1v")):
                for l0 in range(0, 32, 16):
                    i = self._pk % NS
                    self._pk += 1
                    S.dma(pin[i][0:64, 0:1024].rearrange("p (l c) -> p l c", c=64),
                          D[sn][l, l0:l0 + 16].rearrange("l d c -> d l c"), writes=[pin[i]])
                    self.cp("dve", pout[i], pout[i][0:64, 0:1024], pin[i], pin[i][0:64, 0:1024])
                    S.dma(W[dn][:, l0 * 64:(l0 + 16) * 64], pout[i][0:64, 0:1024], reads=[pout[i]])
            mat(D["cmp_wk2"][l], W["w2k"], 64, 64)
            mat(D["cmp_wv2"][l], W["w2v"], 64, 64)
            mat(D["cmp_pe_k"][l], W["pek"], 32, 64)
            mat(D["cmp_pe_v"][l], W["pev"], 32, 64)
        S.barrier()

    def load_w(self, src, shape_view, nbytes_cols):
        b = self.getw()
        v = b[:, 0:nbytes_cols]
        if shape_view is not None:
            v = v.rearrange(shape_view[0], **shape_view[1])
        self.S.dma(v, src, reads=[self.wbuf_d], writes=[b])
        return b, v

    def load_wcols(self, wd, c0, ncols, rows=DM):
        nk = rows // 128
        return self.load_w(wd[:, c0:c0 + ncols].rearrange("(k p) n -> p k n", p=128),
                           ("p (k n) -> p k n", dict(k=nk)), nk * ncols)

    def rmsnorm_T(self, tiles, hb, hview, col0):
        nc, S = self.nc, self.S
        for i, t in enumerate(tiles):
            xb = self.xt[t]
            xa = self.xres_t[:, t, :]
            sm = self.getsm()
            h = self.h16[self._h16rot]
            self._h16rot ^= 1
            self.act([h, sm], h[:], xb, xa, AF.Square, accum_out=sm[:, 0:1])
            self.act(sm, sm[:, 1:2], sm, sm[:, 0:1], AF.Ln, scale=1.0 / DM, bias=EPS)
            self.act(sm, sm[:, 2:3], sm, sm[:, 1:2], AF.Exp, scale=-0.5)
            self.ts("dve", h, h[:], xb, xa, sm[:, 2:3], None, ALU.mult, reads=[sm])
            for kc in range(KC):
                S.op("pe", lambda kc=kc, h=h: nc.tensor.transpose(
                    out=self.pst[:, kc * 128:(kc + 1) * 128], in_=h[:, kc * 128:(kc + 1) * 128],
                    identity=self.C["c_ident"][:]), reads=[h, self.C["c_ident"]], writes=[self.pst])
            self.cp("act" if i % 2 else "dve", hb, hview[:, :, col0 + i * 128: col0 + (i + 1) * 128],
                    self.pst, self.pst[:, :].rearrange("p (k t) -> p k t", k=KC))

    def projT(self, dstb, dst, wb, wv, c0, M, hb, hv, ncols, evac="act", scale=None):
        ps = self.psum()
        for kc in range(KC):
            self.mm(ps, ps[0:M, 0:ncols], wb, wv[:, kc, c0:c0 + M], hb, hv[:, kc, 0:ncols], kc == 0, kc == KC - 1)
        if dst is not None:
            if scale is not None:
                self.act(dstb, dst, ps, ps[0:M, 0:ncols], AF.Copy, scale=scale)
            else:
                self.cp(evac, dstb, dst, ps, ps[0:M, 0:ncols])
        return ps

    def rope_rows(self, dstb, dst16, psA, wrb, wrv, r0, hb, hv, t0, ncols):
        psB = self.ps[6]
        for kc in range(KC):
            self.mm(psB, psB[0:16, 0:ncols], wrb, wrv[:, kc, r0:r0 + 16], hb, hv[:, kc, 0:ncols], kc == 0, kc == KC - 1)
        cs, sn = self.C["c_cos"], self.C["c_sin"]
        f1, f2 = self.getf(), self.getf()
        self.tt("dve", f1, f1[0:16, 0:ncols], psA, psA[0:16, 0:ncols], cs, cs[:, t0:t0 + ncols], ALU.mult)
        self.tt("dve", f2, f2[0:16, 0:ncols], psB, psB[0:16, 0:ncols], sn, sn[:, t0:t0 + ncols], ALU.mult)
        self.tt("pool", dstb, dst16, f1, f1[0:16, 0:ncols], f2, f2[0:16, 0:ncols], ALU.add)

    def out_proj(self, o16b, ncol_o, wob, wov, tile):
        nc, S = self.nc, self.S
        nk = ncol_o // 128
        for k in range(nk):
            S.op("pe", lambda k=k: nc.tensor.transpose(
                out=self.pst[:, k * 128:(k + 1) * 128], in_=o16b[:, k * 128:(k + 1) * 128],
                identity=self.C["c_ident"][:]), reads=[o16b, self.C["c_ident"]], writes=[self.pst])
        oT = self.oT[self._otrot]
        self._otrot ^= 1
        self.cp("act", oT, oT[:, 0:nk, :], self.pst, self.pst[:, 0:nk * 128].rearrange("p (k t) -> p k t", k=nk))
        xb, xa = self.xt[tile], self.xres_t[:, tile, :]
        for n in range(2):
            ps = self.psum()
            for k in range(nk):
                self.mm(ps, ps[:, :], oT, oT[:, k, :], wob, wov[:, k, n * 512:(n + 1) * 512], k == 0, k == nk - 1)
            self.tt("dve", xb, xa[:, n * 512:(n + 1) * 512], xb, xa[:, n * 512:(n + 1) * 512], ps, ps[:, :], ALU.add)

    def load_hT(self, c):
        hb = self.gethT()
        self.S.dma(hb[:], self.hT_d[:, :, c * TCH:(c + 1) * TCH].rearrange("k p t -> p k t"),
                   reads=[self.hT_dbuf], writes=[hb])
        return hb

    def pvs(self, st, acc, out, ptb, lhsT, vb, rhs):
        self.mm(acc, out, ptb, lhsT, vb, rhs, st["first"], True)
        st["first"] = False

    def run_seq(self, s):
        S = self.S
        for t in range(NT):
            S.dma(self.xres_t[:, t, :], self.din["x"][s, t * 128:(t + 1) * 128, :], writes=[self.xt[t]])
        for l in range(self.nlayers):
            self.l = l
            self.s = s
            if any(p in self.phases for p in ("nsa", "sb", "fox")):
                self.phase_mix()
            if "mem" in self.phases:
                self.phase_mem()
            if "ffn" in self.phases:
                self.phase_ffn()
        if self.final:
            self.gfin = Buf("gfin", self.arena[:, 0:2048].bitcast(F32))
            S.dma(self.gfin[:, :], self.din["norm_final"].partition_broadcast(128), writes=[self.gfin])
        for t in range(NT):
            xb, xa = self.xt[t], self.xres_t[:, t, :]
            if self.final:
                sm = self.getsm()
                h = self.h16[self._h16rot]
                self._h16rot ^= 1
                self.act([h, sm], h[:], xb, xa, AF.Square, accum_out=sm[:, 0:1])
                self.act(sm, sm[:, 1:2], sm, sm[:, 0:1], AF.Ln, scale=1.0 / DM, bias=EPS)
                self.act(sm, sm[:, 2:3], sm, sm[:, 1:2], AF.Exp, scale=-0.5)
                self.stt(xb, xa, xb, xa, sm[:, 2:3], self.gfin, self.gfin[:, :], ALU.mult, ALU.mult, reads=[sm])
            self.S.dma(self.y[s, t * 128:(t + 1) * 128, :], xa, reads=[xb])

    ybuf = Buf("y", None)

    def finish(self):
        S = self.S
        deps = {k: v for k, v in S.cnt.items() if isinstance(k, tuple) and v > 0}
        S._wait("sp", deps)

    def phase_mix(self):
        S, nc, l = self.S, self.nc, self.l
        W = self.W[l]
        ar = self.arena
        A = lambda name, p, a, n: Buf(name, ar[0:p, a:a + n])
        kvc = A("kvc", 64, 0, 8192)
        ksT = A("ksT", 128, 8192, 4096)
        kwT = A("kwT", 128, 12288, 4096)
        vsw = A("vsw", 128, 16384, 4160)
        kcc = A("kcc", 64, 20544, 256)
        gat = Buf("gat", ar[:, 20800:20800 + 768].bitcast(F32))
        kvc_v = kvc[:, :].rearrange("p (j t) -> p j t", j=4)
        ksT_v = ksT[:, :].rearrange("p (g t) -> p g t", g=2)
        kwT_v = kwT[:, :].rearrange("p (g t) -> p g t", g=2)
        vsw_v = vsw[:, :].rearrange("p (t j d) -> p t j d", t=NT, j=4)
        kcc_v = kcc[:, :].rearrange("p (g n) -> p g n", g=2)
        do_nsa = "nsa" in self.phases
        if do_nsa:
            self.memset("pool", ksT, ksT[96:128, :], 0.0)
            self.memset("pool", kwT, kwT[64:128, :], 0.0)
            for g in range(2):
                S.dma(ksT_v[64:96, g, :], self.din["c_blk1h"], writes=[ksT])
            self.memset("pool", vsw, vsw_v[:, :, :, 64:65], 1.0)
        for c in range(NCH):
            hb = self.gethT()
            self.rmsnorm_T(range(4 * c, 4 * c + 4), hb, hb[:], 0)
            S.dma(self.hT_d[:, :, c * TCH:(c + 1) * TCH].rearrange("k p t -> p k t"), hb[:], reads=[hb],
                  writes=[self.hT_dbuf])
            if not do_nsa:
                continue
            wb, wv = self.load_wcols(W["win"], C_KC, 256)
            wrb, wrv = self.load_wcols(W["rot"], 128, 96)
            t0 = c * TCH
            for g in range(2):
                ps = self.projT(kvc, kvc_v[0:64, g, t0:t0 + TCH], wb, wv, g * 64, 64, hb, hb[:], TCH)
                self.rope_rows(kvc, kvc_v[0:16, g, t0:t0 + TCH], ps, wrb, wrv, g * 16, hb, hb[:], t0, TCH)
                self.projT(kvc, kvc_v[0:64, 2 + g, t0:t0 + TCH], wb, wv, 128 + g * 64, 64, hb, hb[:], TCH, evac="dve")
            wb, wv = self.load_wcols(W["win"], C_KS, 512)
            for g in range(2):
                ps = self.projT(ksT, ksT_v[0:64, g, t0:t0 + TCH], wb, wv, g * 64, 64, hb, hb[:], TCH)
                self.rope_rows(ksT, ksT_v[0:16, g, t0:t0 + TCH], ps, wrb, wrv, 32 + g * 16, hb, hb[:], t0, TCH)
                ps = self.projT(kwT, kwT_v[0:64, g, t0:t0 + TCH], wb, wv, 256 + g * 64, 64, hb, hb[:], TCH)
                self.rope_rows(kwT, kwT_v[0:16, g, t0:t0 + TCH], ps, wrb, wrv, 64 + g * 16, hb, hb[:], t0, TCH)
            for tl in range(4):
                t = 4 * c + tl
                ps = self.psum()
                for kc in range(KC):
                    self.mm(ps, ps[:, 0:128], hb, hb[:, kc, tl * 128:(tl + 1) * 128], wb, wv[:, kc, 128:256], kc == 0, kc == KC - 1)
                for kc in range(KC):
                    self.mm(ps, ps[:, 128:256], hb, hb[:, kc, tl * 128:(tl + 1) * 128], wb, wv[:, kc, 384:512], kc == 0, kc == KC - 1)
                self.cp("act", vsw, vsw_v[:, t, :, 0:64], ps, ps[:, 0:256].rearrange("p (j d) -> p j d", j=4))
        if do_nsa:
            self.nsa_compress(kvc, kvc_v, kcc, kcc_v)
            S.barrier()
            self.nsa_q(kvc, ksT, ksT_v, kwT, kwT_v, vsw, vsw_v, kcc, kcc_v, gat)
            S.barrier()
        if "sb" in self.phases:
            self.phase_sb()
            S.barrier()
        if "fox" in self.phases:
            self.phase_fox()
            S.barrier()

    def nsa_compress(self, kvc, kvc_v, kcc, kcc_v):
        S, nc, l = self.S, self.nc, self.l
        W = self.W[l]
        wb = self.getw()
        w1 = wb[0:64, 0:4096].rearrange("p (j l c) -> p j l c", j=2, l=32)
        S.dma(w1[:, 0, :, :], W["w1k"].rearrange("p (l c) -> p l c", c=64), reads=[self.wbuf_d], writes=[wb])
        S.dma(w1[:, 1, :, :], W["w1v"].rearrange("p (l c) -> p l c", c=64), reads=[self.wbuf_d], writes=[wb])
        wb2 = self.getw()
        w2 = wb2[0:64, 0:128].rearrange("p (j e) -> p j e", j=2)
        S.dma(w2[:, 0, :], W["w2k"], reads=[self.wbuf_d], writes=[wb2])
        S.dma(w2[:, 1, :], W["w2v"], reads=[self.wbuf_d], writes=[wb2])
        pen = wb2[0:32, 256:384].rearrange("p (j d) -> p j d", j=2)
        S.dma(pen[:, 0, :], W["pek"], reads=[self.wbuf_d], writes=[wb2])
        S.dma(pen[:, 1, :], W["pev"], reads=[self.wbuf_d], writes=[wb2])
        idn = self.C["c_ident"]
        for j in range(2):
            S.op("pe", lambda j=j: nc.tensor.transpose(out=self.pst[0:64, j * 32:(j + 1) * 32], in_=pen[:, j, :],
                                                       identity=idn[0:32, 0:32]), reads=[wb2, idn], writes=[self.pst])
        pet = self.getPT()
        peT = pet[0:64, 0:64].rearrange("p (j l) -> p j l", j=2)
        self.cp("dve", pet, pet[0:64, 0:64], self.pst, self.pst[0:64, 0:64])
        sm = self.getsm()
        for j in range(2):
            ps = self.psum()
            for li in range(32):
                self.mm(ps, ps[0:64, 0:1], wb, w1[:, j, li, :], pet, peT[:, j, li:li + 1], li == 0, li == 31)
            self.cp("dve", sm, sm[0:64, j:j + 1], ps, ps[0:64, 0:1])
        for j in range(2):
            for g in range(2):
                ps = self.psum()
                for li in range(32):
                    self.mm(ps, ps[0:64, 0:127], wb, w1[:, j, li, :], kvc, kvc_v[0:64, 2 * j + g, li:li + 16 * 126 + 1:16],
                            li == 0, li == 31)
                hid = self.getPT()
                self.act(hid, hid[0:64, 0:127], ps, ps[0:64, 0:127], AF.Silu, reads=[sm], bias=sm[0:64, j:j + 1], scale=1.0)
                ps2 = self.psum()
                if j == 0:
                    self.mm(ps2, ps2[0:64, 0:127], wb2, w2[:, 0, :], hid, hid[0:64, 0:127], True, True)
                    self.cp("dve", kcc, kcc_v[0:64, g, 0:127], ps2, ps2[0:64, 0:127])
                else:
                    self.mm(ps2, ps2[0:127, 0:64], hid, hid[0:64, 0:127], wb2, w2[:, 1, :], True, True)
                    self.cp("dve", self.vcc, self.vcc[0:127, g, 0:64], ps2, ps2[0:127, 0:64])

    def nsa_q(self, kvc, ksT, ksT_v, kwT, kwT_v, vsw, vsw_v, kcc, kcc_v, gat):
        S, nc, l = self.S, self.nc, self.l
        W = self.W[l]
        QnT = Buf("QnT", self.arena[0:128, 0:4096])
        Q = QnT[:, :].rearrange("p (g b h q) -> p g b h q", g=2, b=4, h=4)
        self.Qsel = Buf("Qsel", None)
        self.memset("pool", QnT, QnT[64:128, :], 0.0)
        gat_v = gat[:, :].rearrange("p (t c) -> p t c", c=24)
        idn = self.C["c_ident"]
        wob, wov = None, None
        for c in range(NCH):
            hb = self.load_hT(c)
            wb, wv = self.load_wcols(W["win"], C_QN, 512)
            wrb, wrv = self.load_wcols(W["rot"], 0, 128)
            wgb, wgv = self.load_wcols(W["win"], C_G, 24)
            if wob is None or True:
                wob, wov = self.load_w(W["wout"][0:512, :].rearrange("(k p) n -> p k n", p=128),
                                       ("p (k n) -> p k n", dict(k=4)), 4096)
            t0 = c * TCH
            for hh in range(8):
                g, h = hh // 4, hh % 4
                ps = self.psum()
                for kc in range(KC):
                    self.mm(ps, ps[0:64, :], wb, wv[:, kc, hh * 64:(hh + 1) * 64], hb, hb[:, kc, :], kc == 0, kc == KC - 1)
                self.cp("act", QnT, Q[0:64, g, :, h, :], ps, ps[0:64, :].rearrange("p (b q) -> p b q", b=4))
                psB = self.ps[6]
                for kc in range(KC):
                    self.mm(psB, psB[0:16, :], wrb, wrv[:, kc, hh * 16:(hh + 1) * 16], hb, hb[:, kc, :], kc == 0, kc == KC - 1)
                cs, sn = self.C["c_cos"], self.C["c_sin"]
                f1, f2 = self.getf(), self.getf()
                self.tt("dve", f1, f1[0:16, 0:TCH], ps, ps[0:16, :], cs, cs[:, t0:t0 + TCH], ALU.mult)
                self.tt("dve", f2, f2[0:16, 0:TCH], psB, psB[0:16, :], sn, sn[:, t0:t0 + TCH], ALU.mult)
                self.tt("pool", QnT, Q[0:16, g, :, h, :], f1, f1[0:16, 0:TCH].rearrange("p (b q) -> p b q", b=4),
                        f2, f2[0:16, 0:TCH].rearrange("p (b q) -> p b q", b=4), ALU.add)
            for tl in range(4):
                t = 4 * c + tl
                ps = self.psum()
                for kc in range(KC):
                    self.mm(ps, ps[:, 0:24], hb, hb[:, kc, tl * 128:(tl + 1) * 128], wgb, wgv[:, kc, 0:24], kc == 0, kc == KC - 1)
                self.act(gat, gat_v[:, t, :], ps, ps[:, 0:24], AF.Exp, scale=-1.0)
                self.ts("dve", gat, gat_v[:, t, :], gat, gat_v[:, t, :], 1.0, None, ALU.add)
                S.op("dve", lambda t=t: nc.vector.reciprocal(out=gat_v[:, t, :], in_=gat_v[:, t, :]), reads=[gat], writes=[gat])
            for bl in range(4):
                qb = 4 * c + bl
                self.nsa_qblock(qb, bl, QnT, Q, ksT, ksT_v, kwT, kwT_v, vsw, vsw_v, kcc, kcc_v, gat, gat_v)

                def fin(qb=qb, wob=wob, wov=wov, oacc=self.oaccs[qb % 2]):
                    o16 = self.o16[self._orot]
                    self._orot ^= 1
                    self.cp("dve", o16, o16[:, :], oacc, oacc[:, :, :].rearrange("p h d -> p (h d)"))
                    self.out_proj(o16, 512, wob, wov, qb)
                self.pipe_defer(fin, delay=8)
            self.pipe_drain()

    def nsa_qblock(self, qb, bl, QnT, Q, ksT, ksT_v, kwT, kwT_v, vsw, vsw_v, kcc, kcc_v, gat, gat_v):
        S, nc = self.S, self.nc
        idn = self.C["c_ident"]
        oacc = self.oaccs[qb % 2]
        gvs = [gat_v[:, qb, g * 12:(g + 1) * 12].rearrange("p (h k) -> p h k", k=3) for g in range(2)]
        q64s = [Q[0:64, g, bl, :, :].rearrange("p h q -> p (h q)") for g in range(2)]
        q128s = [Q[0:128, g, bl, :, :].rearrange("p h q -> p (h q)") for g in range(2)]
        for g in range(2):
            acc = self.ps[3] if g == 0 else self.ps[6]
            st = {"first": True}

            def qk(g=g):
                ps = self.psum()
                self.mm(ps, ps[0:127, :], kcc, kcc_v[0:64, g, 0:127], QnT, q64s[g], True, False)
                nmc = self.C["c_nm_cmp"]
                self.mm(ps, ps[0:127, :].rearrange("p (h q) -> p h q", h=4), idn, idn[0:127, 0:127], nmc,
                        nmc[0:127, qb * 128:(qb + 1) * 128].unsqueeze(1).broadcast_to([127, 4, 128]), False, True)
                pt = self.getPT()
                self.act(pt, pt[0:127, :], ps, ps[0:127, :], AF.Exp)
                return pt

            def pv(pt, g=g, acc=acc, st=st):
                for h in range(4):
                    self.pvs(st, acc, acc[:, h * 97:(h + 1) * 97], pt, pt[0:127, h * 128:(h + 1) * 128],
                             self.vcc, self.vcc[0:127, g, :])

            def epi(g=g, acc=acc):
                gv = gvs[g]
                accv = acc[:, 0:388].rearrange("p (h d) -> p h d", h=4)
                sm = self.getsm()
                self.ts("dve", sm, sm[:, 0:4], acc, accv[:, :, 64], 1e-30, None, ALU.max)
                S.op("dve", lambda sm=sm: nc.vector.reciprocal(out=sm[:, 4:8], in_=sm[:, 0:4]), reads=[sm], writes=[sm])
                self.tt("dve", sm, sm[:, 8:12], sm, sm[:, 4:8], gat, gv[:, :, 0], ALU.mult)
                f = self.getf()
                fv = f[:, 0:128].rearrange("p (h j) -> p h j", h=4)
                self.tt("dve", f, fv, acc, accv[:, :, 65:97], sm, sm[:, 4:8].unsqueeze(2).broadcast_to([128, 4, 32]), ALU.mult)
                ov = oacc[:, g * 4:(g + 1) * 4, :]
                self.tt("dve", oacc, ov, acc, accv[:, :, 0:64], sm, sm[:, 8:12].unsqueeze(2).broadcast_to([128, 4, 64]), ALU.mult)
                imp = f[:, 128:160]
                self.tt("dve", f, imp, f, fv[:, 0, :], f, fv[:, 1, :], ALU.add)
                self.tt("dve", f, f[:, 160:192], f, fv[:, 2, :], f, fv[:, 3, :], ALU.add)
                self.tt("dve", f, imp, f, imp, f, f[:, 160:192], ALU.add)
                am = self.C["c_addmask"]
                self.tt("dve", f, imp, f, imp, am, am[:, qb, :], ALU.add)
                S.op("dve", lambda f=f: nc.vector.max(out=f[:, 192:200], in_=f[:, 128:160]), reads=[f], writes=[f])
                S.op("dve", lambda f=f: nc.vector.match_replace(out=f[:, 200:232], in_to_replace=f[:, 192:200],
                                                               in_values=f[:, 128:160], imm_value=-3.0e38), reads=[f], writes=[f])
                S.op("dve", lambda f=f: nc.vector.max(out=f[:, 232:240], in_=f[:, 200:232]), reads=[f], writes=[f])
                sp_ = self.selpad[g]
                self.ts("dve", sp_, sp_[:, 64:96], f, imp, f[:, 239:240], NEG, ALU.is_lt, ALU.mult)

            self.pipe_unit(qk, pv)
            self.pipe_defer(epi)

        def epi_b(g):
            sp_ = self.selpad[g]
            ps = self.psum()
            self.mm(ps, ps[0:96, 0:128], sp_, sp_[:, :], idn, idn[:, :], True, True)
            self.cp("act", self.Qsel, Q[64:96, g, bl, :, :], ps, ps[64:96, 0:128].unsqueeze(1).broadcast_to([32, 4, 128]))
        for g in range(2):
            acc = self.ps[5]
            st = {"first": True}
            for kb in range(max(0, qb - 4), qb + 1):
                d = qb - kb

                def qk(g=g, kb=kb, d=d):
                    ps = self.psum()
                    msk = d == 0 or d == 4
                    self.mm(ps, ps[:, :], kwT, kwT_v[0:128, g, kb * 128:(kb + 1) * 128], QnT, q128s[g], True, not msk)
                    if msk:
                        nm = self.C["c_nm_incl"] if d == 0 else self.C["c_nm_win"]
                        self.mm(ps, ps[:, :], idn, idn[:, :], nm, nm[:, :], False, True)
                    pt = self.getPT()
                    self.act(pt, pt[:, :], ps, ps[:, :], AF.Exp)
                    return pt

                def pv(pt, g=g, kb=kb, acc=acc, st=st):
                    for h in range(4):
                        self.pvs(st, acc, acc[:, h * 65:(h + 1) * 65], pt, pt[:, h * 128:(h + 1) * 128], vsw, vsw_v[:, kb, 2 + g, :])

                self.pipe_unit(qk, pv)
            self.pipe_defer(lambda g=g, acc=acc: self.nsa_accum(acc, gat, gvs[g], 2, g, oacc))
        for g in range(2):
            epi_b(g)
        for g in range(2):
            acc = self.ps[4]
            st = {"first": True}
            for kb in range(qb + 1):
                def qk(g=g, kb=kb):
                    ps = self.psum()
                    diag = kb == qb
                    self.mm(ps, ps[:, :], ksT, ksT_v[0:128, g, kb * 128:(kb + 1) * 128], QnT, q128s[g], True, not diag,
                            extra=[self.Qsel])
                    if diag:
                        nm = self.C["c_nm_incl"]
                        self.mm(ps, ps[:, :], idn, idn[:, :], nm, nm[:, :], False, True)
                    pt = self.getPT()
                    self.act(pt, pt[:, :], ps, ps[:, :], AF.Exp)
                    return pt

                def pv(pt, g=g, kb=kb, acc=acc, st=st):
                    for h in range(4):
                        self.pvs(st, acc, acc[:, h * 65:(h + 1) * 65], pt, pt[:, h * 128:(h + 1) * 128], vsw, vsw_v[:, kb, g, :])

                self.pipe_unit(qk, pv)
            self.pipe_defer(lambda g=g, acc=acc: self.nsa_accum(acc, gat, gvs[g], 1, g, oacc))

    def nsa_accum(self, acc, gat, gv, k, g, oacc):
        S, nc = self.S, self.nc
        accv = acc[:, 0:260].rearrange("p (h d) -> p h d", h=4)
        sm = self.getsm()
        S.op("dve", lambda: nc.vector.reciprocal(out=sm[:, 0:4], in_=accv[:, :, 64]), reads=[acc], writes=[sm])
        self.tt("dve", sm, sm[:, 4:8], sm, sm[:, 0:4], gat, gv[:, :, k], ALU.mult)
        f = self.getf()
        fv = f[:, 0:256].rearrange("p (h d) -> p h d", h=4)
        self.tt("dve", f, fv, acc, accv[:, :, 0:64], sm, sm[:, 4:8].unsqueeze(2).broadcast_to([128, 4, 64]), ALU.mult)
        ov = oacc[:, g * 4:(g + 1) * 4, :]
        self.tt("dve", oacc, ov, oacc, ov, f, fv, ALU.add)

    def phase_sb(self):
        S, nc, l = self.S, self.nc, self.l
        W = self.W[l]
        ar = self.arena
        kT = Buf("sbk", ar[0:128, 0:8192])
        kT_v = kT[:, :].rearrange("p (h t) -> p h t", h=4)
        vb = Buf("sbv", ar[:, 8192:8192 + 4096])
        v_v = vb[:, :].rearrange("p (t c) -> p t c", t=NT)
        qT = Buf("sbq", ar[0:128, 12288:12288 + 2048])
        q_v = qT[:, :].rearrange("p (h t) -> p h t", h=4)
        self.memset("pool", kT, kT[64:128, :], 0.0)
        self.memset("pool", qT, qT[64:128, :], 0.0)
        lacc = Buf("lacc", ar[:, 14336:14336 + 1024].bitcast(F32))
        lacc16 = [Buf("lacc16_%d" % i, ar[:, 15360 + i * 512:15360 + (i + 1) * 512]) for i in range(3)]
        l16 = [Buf("l16_%d" % i, ar[:, 16896 + i * 512:16896 + (i + 1) * 512]) for i in range(2)]
        idn, tri, ones = self.C["c_ident"], self.C["c_tri"], self.C["c_ones"]
        self._rotbanks = [0, 1, 2, 5, 6]
        for c in range(NCH):
            hb = self.load_hT(c)
            wb, wv = self.load_wcols(W["win"], C_KSB, 512)
            for h in range(4):
                self.projT(kT, kT_v[0:64, h, c * TCH:(c + 1) * TCH], wb, wv, h * 64, 64, hb, hb[:], TCH,
                           evac="act" if h % 2 else "dve")
            for tl in range(4):
                ps = self.psum()
                for kc in range(KC):
                    self.mm(ps, ps[:, 0:256], hb, hb[:, kc, tl * 128:(tl + 1) * 128], wb, wv[:, kc, 256:512], kc == 0, kc == KC - 1)
                self.cp("act", vb, v_v[:, 4 * c + tl, :], ps, ps[:, 0:256])
        for c in range(NCH):
            hb = self.load_hT(c)
            wb, wv = self.load_wcols(W["win"], C_QS, 256)
            wob, wov = self.load_w(W["wout"][512:768, :].rearrange("(k p) n -> p k n", p=128),
                                   ("p (k n) -> p k n", dict(k=2)), 2048)
            for h in range(4):
                self.projT(qT, q_v[0:64, h, :], wb, wv, h * 64, 64, hb, hb[:], TCH, evac="act" if h % 2 else "dve")
            o16s = [self.sb_dummy(i) for i in range(4)]
            units = []
            for h in range(4):
                kbs = list(range(4 * c + 3, -1, -1))
                for j, kb in enumerate(kbs):
                    units.append(dict(h=h, kb=kb, first=j == 0, last=j == len(kbs) - 1,
                                      off=max(0, (kb - 4 * c) * 128), diag=kb >= 4 * c,
                                      acc=self.ps[3 + (h % 2)], st=None))
            sts = {}
            n = len(units)

            def stageA(i):
                u = units[i]
                h, kb, off = u["h"], u["kb"], u["off"]
                if u["first"]:
                    self.memset("pool", lacc, lacc[:, :], 0.0)
                    sts[h] = {"first": True}
                ks = kT_v[0:128, h, kb * 128:(kb + 1) * 128]
                ps1 = self.psum()
                self.mm(ps1, ps1[:, off:512], kT, ks, qT, q_v[0:128, h, off:512], True, True)
                sp = self.getf()
                self.act(sp, sp[:, off:512], ps1, ps1[:, off:512], AF.Exp, scale=-1.0)
                self.act(sp, sp[:, off:512], sp, sp[:, off:512], AF.Ln, bias=1.0, scale=1.0)
                lb = l16[i % 2]
                self.stt(lb, lb[:, off:512], ps1, ps1[:, off:512], -1.0, sp, sp[:, off:512], ALU.mult, ALU.subtract)
                if u["diag"]:
                    m01 = self.C["c_m01_strict"]
                    self.tt("pool", lb, lb[:, off:off + 128], lb, lb[:, off:off + 128], m01, m01[:, :], ALU.mult)
                if not u["last"]:
                    self.tt("dve", lacc, lacc[:, off:512], lacc, lacc[:, off:512], lb, lb[:, off:512], ALU.add)
                    la = lacc16[i % 3]
                    self.cp("dve", la, la[:, :], lacc, lacc[:, :])

            def stageB(i):
                u = units[i]
                h, kb, off = u["h"], u["kb"], u["off"]
                ks = kT_v[0:128, h, kb * 128:(kb + 1) * 128]
                lb = l16[i % 2]
                ps2 = self.psum()
                grp = [(ps2[:, off:512], kT, ks, qT, q_v[0:128, h, off:512]),
                       (ps2[:, off:512], tri, tri[:, :], lb, lb[:, off:512])]
                if not u["first"]:
                    la = lacc16[(i - 1) % 3]
                    grp.append((ps2[:, off:512], ones, ones[:, 0:128], la, la[:, off:512]))
                if u["diag"]:
                    nm = self.C["c_nm_strict"]
                    grp.append((ps2[:, off:off + 128], idn, idn[:, :], nm, nm[:, 0:128]))
                for gi, (o_, lb_, l_, rb_, r_) in enumerate(grp):
                    self.mm(ps2, o_, lb_, l_, rb_, r_, gi == 0, gi == len(grp) - 1)
                pt = self.getPT()
                self.act(pt, pt[:, off:512], ps2, ps2[:, off:512], AF.Exp)
                u["pt"] = pt

            def stageC(i):
                u = units[i]
                h, kb, off, acc, pt = u["h"], u["kb"], u["off"], u["acc"], u["pt"]
                for qbl in range(off // 128, 4):
                    self.pvs(sts[h], acc, acc[:, qbl * 64:(qbl + 1) * 64], pt, pt[:, qbl * 128:(qbl + 1) * 128],
                             vb, v_v[:, kb, h * 64:(h + 1) * 64])
                if u["last"]:
                    for qbl in range(4):
                        self.cp("dve", o16s[qbl], o16s[qbl][:, h * 64:(h + 1) * 64], acc, acc[:, qbl * 64:(qbl + 1) * 64])

            for i in range(n + 2):
                if i < n:
                    stageA(i)
                if 0 <= i - 1 < n:
                    stageB(i - 1)
                if 0 <= i - 2 < n:
                    stageC(i - 2)
            for qbl in range(4):
                self.out_proj(o16s[qbl], 256, wob, wov, 4 * c + qbl)
        self._rotbanks = [0, 1, 2]

    def sb_dummy(self, i):
        if not hasattr(self, "_o4"):
            self._o4 = [Buf("o4_%d" % j, self.o16[j // 2][:, (j % 2) * 256:(j % 2 + 1) * 256]) for j in range(4)]
        return self._o4[i]

    def phase_fox(self):
        S, nc, l = self.S, self.nc, self.l
        W = self.W[l]
        ar = self.arena
        kT = Buf("fxk", ar[0:128, 0:8192])
        kT_v = kT[:, :].rearrange("p (h t) -> p h t", h=4)
        self.memset("pool", kT, kT[64:128, :], 0.0)
        vb = Buf("fxv", ar[:, 8192:8192 + 4160])
        v_v = vb[:, :].rearrange("p (t h d) -> p t h d", t=NT, h=4)
        qT = Buf("fxq", ar[0:128, 12352:12352 + 2048])
        q_v = qT[:, :].rearrange("p (h t) -> p h t", h=4)
        self.memset("pool", qT, qT[64:128, :], 0.0)
        csp = Buf("csp", ar[0:4, 14400:14400 + 4096].bitcast(F32))
        hi = Buf("hi", ar[0:4, 18496:18496 + 2048])
        nhi = Buf("nhi", ar[0:4, 22016:22016 + 512])
        idn = self.C["c_ident"]
        place = self.C["c_place"][:, :].rearrange("p (k h m) -> p k h m", k=6, h=4)
        plb = self.C["c_place"]
        self.memset("pool", vb, v_v[:, :, :, 64:65], 1.0)
        self._rotbanks = [0, 1, 2, 5, 6]
        for c in range(NCH):
            hb = self.load_hT(c)
            wb, wv = self.load_wcols(W["win"], C_KF, 512)
            wfb, wfv = self.load_wcols(W["win"], C_FL, 4)
            t0 = c * TCH
            for h in range(4):
                self.projT(kT, kT_v[0:64, h, t0:t0 + TCH], wb, wv, h * 64, 64, hb, hb[:], TCH,
                           evac="act" if h % 2 else "dve")
            for tl in range(4):
                ps = self.psum()
                for kc in range(KC):
                    self.mm(ps, ps[:, 0:256], hb, hb[:, kc, tl * 128:(tl + 1) * 128], wb, wv[:, kc, 256:512], kc == 0, kc == KC - 1)
                self.cp("act", vb, v_v[:, 4 * c + tl, :, 0:64], ps, ps[:, 0:256].rearrange("p (h d) -> p h d", h=4))
            ps = self.psum()
            for kc in range(KC):
                self.mm(ps, ps[0:4, :], wfb, wfv[:, kc, 0:4], hb, hb[:, kc, :], kc == 0, kc == KC - 1)
            e, sp = self.getf(), self.getf()
            self.act(e, e[0:4, 0:TCH], ps, ps[0:4, :], AF.Exp, reads=[self.nbf], scale=-1.0, bias=self.nbf[:, self.l:self.l + 1])
            self.act(sp, sp[0:4, 0:TCH], e, e[0:4, 0:TCH], AF.Ln, bias=1.0, scale=1.0)
            init = 0.0 if c == 0 else csp[:, t0 - 1:t0]
            S.op("dve", lambda init=init, sp=sp, t0=t0: nc.vector.tensor_tensor_scan(
                out=csp[:, t0:t0 + TCH], data0=self.ones4b[:, :], data1=sp[0:4, 0:TCH], initial=init,
                op0=ALU.mult, op1=ALU.add), reads=[self.ones4b, sp, csp], writes=[csp])
            self.cp("dve", hi, hi[:, t0:t0 + TCH], csp, csp[:, t0:t0 + TCH])
            f = self.getf()
            self.tt("dve", f, f[0:4, 0:TCH], csp, csp[:, t0:t0 + TCH], hi, hi[:, t0:t0 + TCH], ALU.subtract)
            lo16 = self.getPT()
            self.cp("dve", lo16, lo16[0:4, 0:TCH], f, f[0:4, 0:TCH])
            for h in range(4):
                ps = self.psum()
                self.mm(ps, ps[0:68, :], plb, place[:, 3, h, :], hi, hi[:, t0:t0 + TCH], True, False)
                self.mm(ps, ps[0:68, :], plb, place[:, 4, h, :], lo16, lo16[0:4, 0:TCH], False, False)
                self.mm(ps, ps[0:68, :], plb, place[:, 5, h, :], self.ones4b, self.ones4b[:, :], False, True)
                self.cp("act", kT, kT_v[64:68, h, t0:t0 + TCH], ps, ps[64:68, :])
        for c in range(NCH):
            hb = self.load_hT(c)
            t0 = c * TCH
            wb, wv = self.load_wcols(W["win"], C_QF, 256)
            wob, wov = self.load_w(W["wout"][768:1024, :].rearrange("(k p) n -> p k n", p=128),
                                   ("p (k n) -> p k n", dict(k=2)), 2048)
            for h in range(4):
                self.projT(qT, q_v[0:64, h, :], wb, wv, h * 64, 64, hb, hb[:], TCH, evac="act" if h % 2 else "dve")
            self.ts("dve", nhi, nhi[:, 0:TCH], hi, hi[:, t0:t0 + TCH], -1.0, None, ALU.mult)
            f = self.getf()
            self.tt("dve", f, f[0:4, 0:TCH], hi, hi[:, t0:t0 + TCH], csp, csp[:, t0:t0 + TCH], ALU.subtract)
            nlo = self.getPT()
            self.cp("dve", nlo, nlo[0:4, 0:TCH], f, f[0:4, 0:TCH])
            for h in range(4):
                ps = self.psum()
                self.mm(ps, ps[0:68, :], plb, place[:, 0, h, :], nhi, nhi[:, 0:TCH], True, False)
                self.mm(ps, ps[0:68, :], plb, place[:, 1, h, :], nlo, nlo[0:4, 0:TCH], False, False)
                self.mm(ps, ps[0:68, :], plb, place[:, 2, h, :], self.ones4b, self.ones4b[:, :], False, True)
                self.cp("act", qT, q_v[64:68, h, :], ps, ps[64:68, :])
            o16s = [self.sb_dummy(i) for i in range(4)]
            for h in range(4):
                acc = self.ps[3 + (h % 2)]
                st = {"first": True}
                for kb in range(4 * c + 3, -1, -1):
                    off = max(0, (kb - 4 * c) * 128)
                    diag = kb >= 4 * c

                    def qk(h=h, kb=kb, off=off, diag=diag):
                        ps = self.psum()
                        self.mm(ps, ps[:, off:512], kT, kT_v[0:128, h, kb * 128:(kb + 1) * 128], qT, q_v[0:128, h, off:512], True, not diag)
                        if diag:
                            nm = self.C["c_nm_incl"]
                            self.mm(ps, ps[:, off:off + 128], idn, idn[:, :], nm, nm[:, 0:128], False, True)
                        pt = self.getPT()
                        self.act(pt, pt[:, off:512], ps, ps[:, off:512], AF.Exp)
                        return pt

                    def pv(pt, h=h, kb=kb, off=off, acc=acc, st=st):
                        for qbl in range(off // 128, 4):
                            self.pvs(st, acc, acc[:, qbl * 65:(qbl + 1) * 65], pt, pt[:, qbl * 128:(qbl + 1) * 128],
                                     vb, v_v[:, kb, h, :])

                    self.pipe_unit(qk, pv)

                def epi(h=h, acc=acc):
                    accv = acc[:, 0:260].rearrange("p (b d) -> p b d", b=4)
                    sm = self.getsm()
                    S.op("dve", lambda: nc.vector.reciprocal(out=sm[:, 0:4], in_=accv[:, :, 64]), reads=[acc], writes=[sm])
                    for qbl in range(4):
                        self.ts("dve", o16s[qbl], o16s[qbl][:, h * 64:(h + 1) * 64], acc, accv[:, qbl, 0:64],
                                sm[:, qbl:qbl + 1], None, ALU.mult, reads=[sm])
                self.pipe_defer(epi)
            self.pipe_drain()
            for qbl in range(4):
                self.out_proj(o16s[qbl], 256, wob, wov, 4 * c + qbl)
        self._rotbanks = [0, 1, 2]

    def phase_mem(self):
        S, nc, l = self.S, self.nc, self.l
        W = self.W[l]
        ar = self.arena
        mx = Buf("mx", ar[:, 0:4096].bitcast(F32))
        mx_v = mx[:, :].rearrange("p (t d) -> p t d", t=2)
        mT = Buf("mT", ar[:, 4096:4096 + 2048])
        mT_v = mT[:, :].rearrange("p (k t) -> p k t", k=KC)
        kT = Buf("mk", ar[0:128, 6144:6144 + 1024])
        kT_v = kT[:, :].rearrange("p (h t) -> p h t", h=4)
        self.memset("pool", kT, kT[64:128, :], 0.0)
        vb = Buf("mv", ar[:, 7168:7168 + 520])
        v_v = vb[:, :].rearrange("p (t h d) -> p t h d", t=2, h=4)
        qT = Buf("mq", ar[0:128, 7688:7688 + 2048])
        q_v = qT[:, :].rearrange("p (h t) -> p h t", h=4)
        self.memset("pool", qT, qT[64:128, :], 0.0)
        idn = self.C["c_ident"]
        for t in range(2):
            S.dma(mx_v[:, t, :], self.din["mem"][self.s, t * 128:(t + 1) * 128, :], writes=[mx])
        self.memset("pool", vb, v_v[:, :, :, 64:65], 1.0)
        for t in range(2):
            sm = self.getsm()
            h = self.h16[self._h16rot]
            self._h16rot ^= 1
            self.act([h, sm], h[:], mx, mx_v[:, t, :], AF.Square, accum_out=sm[:, 0:1])
            self.act(sm, sm[:, 1:2], sm, sm[:, 0:1], AF.Ln, scale=1.0 / DM, bias=EPS)
            self.act(sm, sm[:, 2:3], sm, sm[:, 1:2], AF.Exp, scale=-0.5)
            self.ts("dve", h, h[:], mx, mx_v[:, t, :], sm[:, 2:3], None, ALU.mult, reads=[sm])
            for kc in range(KC):
                S.op("pe", lambda kc=kc, h=h: nc.tensor.transpose(
                    out=self.pst[:, kc * 128:(kc + 1) * 128], in_=h[:, kc * 128:(kc + 1) * 128],
                    identity=idn[:]), reads=[h, idn], writes=[self.pst])
            self.cp("dve", mT, mT_v[:, :, t * 128:(t + 1) * 128], self.pst, self.pst[:, :].rearrange("p (k t) -> p k t", k=KC))
        wb, wv = self.load_wcols(W["mk"], 0, 256)
        for h in range(4):
            self.projT(kT, kT_v[0:64, h, :], wb, wv, h * 64, 64, mT, mT_v, 256)
        wb, wv = self.load_wcols(W["mv"], 0, 256)
        for t in range(2):
            ps = self.psum()
            for kc in range(KC):
                self.mm(ps, ps[:, 0:256], mT, mT_v[:, kc, t * 128:(t + 1) * 128], wb, wv[:, kc, :], kc == 0, kc == KC - 1)
            self.cp("act", vb, v_v[:, t, :, 0:64], ps, ps[:, 0:256].rearrange("p (h d) -> p h d", h=4))
        wqb, wqv = self.load_wcols(W["mq"], 0, 256)
        wob, wov = self.load_w(W["mo"].rearrange("(k p) n -> p k n", p=128), ("p (k n) -> p k n", dict(k=2)), 2048)
        for c in range(NCH):
            hb = self.gethT()
            self.rmsnorm_T(range(4 * c, 4 * c + 4), hb, hb[:], 0)
            for h in range(4):
                self.projT(qT, q_v[0:64, h, :], wqb, wqv, h * 64, 64, hb, hb[:], TCH, evac="act" if h % 2 else "dve")
            o16s = [self.sb_dummy(i) for i in range(4)]
            for h in range(4):
                acc = self.ps[3 + (h % 2)]
                st = {"first": True}
                for kb in range(2):
                    def qk(h=h, kb=kb):
                        ps = self.psum()
                        self.mm(ps, ps[:, :], kT, kT_v[0:128, h, kb * 128:(kb + 1) * 128], qT, q_v[0:128, h, :], True, True)
                        pt = self.getPT()
                        self.act(pt, pt[:, :], ps, ps[:, :], AF.Exp)
                        return pt

                    def pv(pt, h=h, kb=kb, acc=acc, st=st):
                        for qbl in range(4):
                            self.pvs(st, acc, acc[:, qbl * 65:(qbl + 1) * 65], pt, pt[:, qbl * 128:(qbl + 1) * 128], vb, v_v[:, kb, h, :])

                    self.pipe_unit(qk, pv)

                def epi(h=h, acc=acc):
                    accv = acc[:, 0:260].rearrange("p (b d) -> p b d", b=4)
                    sm = self.getsm()
                    S.op("dve", lambda: nc.vector.reciprocal(out=sm[:, 0:4], in_=accv[:, :, 64]), reads=[acc], writes=[sm])
                    for qbl in range(4):
                        self.ts("dve", o16s[qbl], o16s[qbl][:, h * 64:(h + 1) * 64], acc, accv[:, qbl, 0:64],
                                sm[:, qbl:qbl + 1], None, ALU.mult, reads=[sm])
                self.pipe_defer(epi)
            self.pipe_drain()
            for qbl in range(4):
                self.out_proj(o16s[qbl], 256, wob, wov, 4 * c + qbl)
        S.barrier()

    def phase_ffn(self):
        S, nc, l = self.S, self.nc, self.l
        W = self.W[l]
        ar = self.arena
        gT = Buf("gT", ar[:, 0:11264])
        g_v = gT[:, :].rearrange("p (k t) -> p k t", k=22)
        halo = Buf("halo", ar[:, 11264:11264 + 176].bitcast(F32))
        halo_v = halo[:, :].rearrange("p (c k) -> p c k", k=2)
        self.memset("pool", halo, halo[:, :], 0.0)
        cw = self.cw
        for c in range(NCH):
            hb = self.gethT()
            self.rmsnorm_T(range(4 * c, 4 * c + 4), hb, hb[:], 0)
            wcur = {}
            uy = {}

            def stage1(cc):
                cg, ci = cc // 4, cc % 4
                if ci == 0:
                    wcur["w"] = self.load_wcols(W["up"], cg * 512, 512)
                wb, wv = wcur["w"]
                ps = self.psum()
                for kc in range(KC):
                    self.mm(ps, ps[:, :], wb, wv[:, kc, ci * 128:(ci + 1) * 128], hb, hb[:, kc, :], kc == 0, kc == KC - 1)
                u = self.getf()
                y = self.getf()
                self.cp("pool", u, u[:, 0:2], halo, halo_v[:, cc, :])
                self.cp("act", u, u[:, 2:514], ps, ps[:, :])
                self.act(y, y[:, 0:512], ps, ps[:, :], AF.Copy, reads=[cw], scale=cw[:, l, cc, 2:3])
                self.cp("pool", halo, halo_v[:, cc, :], u, u[:, 512:514])
                uy[cc] = [u, y]

            def stage2(cc):
                u, y = uy[cc]
                self.stt(y, y[:, 0:512], u, u[:, 1:513], cw[:, l, cc, 1:2], y, y[:, 0:512], ALU.mult, ALU.add, reads=[cw])
                self.stt(y, y[:, 0:512], u, u[:, 0:512], cw[:, l, cc, 0:1], y, y[:, 0:512], ALU.mult, ALU.add, reads=[cw])

            def stage3(cc):
                y = uy.pop(cc)[1]
                if cc < 22:
                    self.act(gT, g_v[:, cc, :], y, y[:, 0:512], AF.Silu, reads=[cw], bias=cw[:, l, cc, 3:4], scale=1.0)
                else:
                    self.stt(gT, g_v[:, cc - 22, :], y, y[:, 0:512], cw[:, l, cc, 3:4], gT, g_v[:, cc - 22, :],
                             ALU.add, ALU.mult, reads=[cw])

            for i in range(44 + 2):
                if i < 44:
                    stage1(i)
                if 0 <= i - 1 < 44:
                    stage2(i - 1)
                if 0 <= i - 2 < 44:
                    stage3(i - 2)
            for n in range(2):
                accs = [self.ps[3 + tl] for tl in range(4)]
                for pc in range(6):
                    k0 = pc * 4
                    nk = min(4, 22 - k0)
                    wdb, wdv = self.load_w(W["down"][k0 * 128:(k0 + nk) * 128, n * 512:(n + 1) * 512].rearrange("(k p) n -> p k n", p=128),
                                           ("p (k n) -> p k n", dict(k=nk)), nk * 512)
                    for tl in range(4):
                        for k in range(nk):
                            self.mm(accs[tl], accs[tl][:, :], gT, g_v[:, k0 + k, tl * 128:(tl + 1) * 128], wdb, wdv[:, k, :],
                                    k0 + k == 0, k0 + k == 21)
                for tl in range(4):
                    t = 4 * c + tl
                    xb, xa = self.xt[t], self.xres_t[:, t, :]
                    self.tt("dve", xb, xa[:, n * 512:(n + 1) * 512], xb, xa[:, n * 512:(n + 1) * 512], accs[tl], accs[tl][:, :], ALU.add)
        S.barrier()


_PROG = {}


def _get_prog(nseq):
    if nseq not in _PROG:
        _PROG[nseq] = K(nseq)
    return _PROG[nseq]


def kernel(**inputs):
    x = np.ascontiguousarray(np.asarray(inputs["x"], dtype=np.float32))
    mem = np.ascontiguousarray(np.asarray(inputs["mem"], dtype=np.float32))
    B = x.shape[0]
    ncores = 8
    per = B // ncores
    prog = _get_prog(per)
    consts = _consts()
    in_maps = []
    for i in range(ncores):
        m = {"x": x[i * per:(i + 1) * per], "mem": mem[i * per:(i + 1) * per]}
        for k, v in inputs.items():
            if k not in ("x", "mem"):
                m[k] = np.ascontiguousarray(np.asarray(v, dtype=np.float32))
        m.update(consts)
        in_maps.append(m)
    res = run_bass_kernel_spmd(prog.nc, in_maps, core_ids=list(range(ncores)))
    return np.concatenate([np.asarray(r["y"], dtype=np.float32) for r in res.results], axis=0)
```

```python
import numpy as np
import ml_dtypes
import concourse.bass as bass
import concourse.mybir as mybir
from concourse.bass_utils import run_bass_kernel_spmd

F32 = mybir.dt.float32
BF16 = mybir.dt.bfloat16
AF = mybir.ActivationFunctionType
ALU = mybir.AluOpType

SEQ, DM, KC, NT, TCH, NCH = 2048, 1024, 8, 16, 512, 4
DEPTH = 4
DFF = 2816
NEG = -30000.0
BIG = 1.0e30
EPS = 1e-6
C_QN, C_KC, C_VC, C_KS, C_VS, C_KW, C_VW, C_G = 0, 512, 640, 768, 896, 1024, 1152, 1280
C_QS, C_KSB, C_VSB, C_QF, C_KF, C_VF, C_FL = 1304, 1560, 1816, 2072, 2328, 2584, 2840
INC = 2844


class Buf:
    __slots__ = ("name", "t", "lw", "rd", "excl")

    def __init__(self, name, t, excl=False):
        self.name, self.t, self.lw, self.rd, self.excl = name, t, None, {}, excl

    def __getitem__(self, idx):
        return self.t[idx]


class Sched:
    NDSEM = 24

    def __init__(self, nc):
        self.nc = nc
        self.E = {"pe": nc.tensor, "act": nc.scalar, "dve": nc.vector, "pool": nc.gpsimd, "sp": nc.sync}
        self.sems, self.cnt = {}, {}
        for k in self.E:
            self.sems[k] = nc.alloc_semaphore("s_" + k)
            self.cnt[k] = 0
        for i in range(self.NDSEM):
            self.sems[("d", i)] = nc.alloc_semaphore("d%d" % i)
            self.cnt[("d", i)] = 0
        self.seen = {k: {} for k in self.E}
        self.dnext = 0
        self.ninstr = 0
        self.nwait = 0

    def _deps(self, reads, writes):
        deps = {}

        def add(kv):
            if kv is not None and deps.get(kv[0], 0) < kv[1]:
                deps[kv[0]] = kv[1]

        for b in reads:
            add(b.lw)
            if b.excl:
                for kv in b.rd.items():
                    add(kv)
        for b in writes:
            add(b.lw)
            for kv in b.rd.items():
                add(kv)
        return deps

    def _wait(self, eng, deps):
        seen, e = self.seen[eng], self.E[eng]
        for k, v in deps.items():
            if k == "pe" and eng == "pe":
                continue
            if seen.get(k, 0) >= v:
                continue
            e.wait_ge(self.sems[k], v)
            seen[k] = v
            self.nwait += 1

    def _mark(self, key, val, reads, writes):
        for b in reads:
            if b.excl:
                b.lw, b.rd = (key, val), {}
            else:
                b.rd[key] = val
        for b in writes:
            b.lw, b.rd = (key, val), {}

    def op(self, eng, fn, reads=(), writes=()):
        self._wait(eng, self._deps(reads, writes))
        ins = fn()
        self.cnt[eng] += 1
        ins.then_inc(self.sems[eng], 1)
        self._mark(eng, self.cnt[eng], reads, writes)
        self.ninstr += 1
        return ins

    def dma(self, out_ap, in_ap, reads=(), writes=(), q="sp", **kw):
        deps = self._deps(reads, writes)
        dk = ("d", self.dnext)
        self.dnext = (self.dnext + 1) % self.NDSEM
        if self.cnt[dk] > deps.get(dk, 0):
            deps[dk] = self.cnt[dk]
        self._wait(q, deps)
        ins = self.E[q].dma_start(out=out_ap, in_=in_ap, **kw)
        self.cnt[dk] += 16
        ins.then_inc(self.sems[dk], 16)
        self._mark(dk, self.cnt[dk], reads, writes)
        self.ninstr += 1
        return ins

    def barrier(self):
        deps = {k: v for k, v in self.cnt.items() if v > 0 and k != "sp"}
        for eng in self.E:
            self._wait(eng, dict(deps))


def _consts():
    bf = ml_dtypes.bfloat16
    c = {}
    half = 8
    inv = 500000.0 ** (-np.arange(half, dtype=np.float32) / half)
    ang = np.arange(SEQ, dtype=np.float32)[None, :] * inv[:, None]
    cs = np.concatenate([np.cos(ang), np.cos(ang)], 0).astype(np.float32)
    sn = np.concatenate([np.sin(ang), np.sin(ang)], 0).astype(np.float32)
    c["c_cos"], c["c_sin"] = cs.astype(bf), sn.astype(bf)
    j = np.arange(128)[:, None]
    t = np.arange(128)[None, :]
    c["c_ident"] = np.eye(128, dtype=np.float32).astype(bf)
    c["c_identf"] = np.eye(8, dtype=np.float32)
    c["c_ones"] = np.ones((128, 512), np.float32).astype(bf)
    c["c_tri"] = (j >= t).astype(np.float32).astype(bf)
    c["c_nm_incl"] = np.tile(np.where(j > t, NEG, 0.0), (1, 4)).astype(np.float32).astype(bf)
    c["c_nm_strict"] = np.tile(np.where(j >= t, NEG, 0.0), (1, 4)).astype(np.float32).astype(bf)
    c["c_nm_win"] = np.tile(np.where(j <= t, NEG, 0.0), (1, 4)).astype(np.float32).astype(bf)
    c["c_m01_strict"] = (j < t).astype(np.float32).astype(bf)
    n = np.arange(128)[:, None]
    tt = np.arange(SEQ)[None, :]
    c["c_nm_cmp"] = np.where(16 * n + 31 > tt, NEG, 0.0).astype(np.float32).astype(bf)
    starts = np.arange(127) * 16
    sel_starts = np.arange(32) * 64
    ovl = ((starts[:, None] < sel_starts[None, :] + 64) & (starts[:, None] + 32 > sel_starts[None, :]))
    vext = np.zeros((128, 33), np.float32)
    vext[:, 0] = 1.0
    vext[:127, 1:] = ovl
    c["c_vext"] = vext.astype(bf)
    onehot = (np.arange(SEQ)[None, :] // 64 == np.arange(32)[:, None]).astype(np.float32)
    c["c_blk1h"] = onehot.astype(bf)
    tq = np.arange(SEQ)
    cur = (tq // 64)[:, None]
    bid = np.arange(32)[None, :]
    forced = (bid == 0) | (bid == cur) | (bid == cur - 1)
    am = np.where(bid <= cur, np.where(forced, BIG, 0.0), -BIG).astype(np.float32)
    c["c_addmask"] = np.ascontiguousarray(am.reshape(16, 128, 32).transpose(1, 0, 2)).astype(bf)
    pl = np.zeros((4, 6, 4, 68), np.float32)
    for h in range(4):
        pl[h, 0, h, 64] = 1.0
        pl[h, 1, h, 65] = 1.0
        pl[0, 2, h, 66] = 1.0
        pl[0, 2, h, 67] = 1.0
        pl[h, 3, h, 66] = 1.0
        pl[h, 4, h, 67] = 1.0
        pl[0, 5, h, 64] = 1.0
        pl[0, 5, h, 65] = 1.0
    c["c_place"] = pl.reshape(4, 6 * 4 * 68).astype(bf)
    rot = np.zeros((128, 16), np.float32)
    for m in range(8):
        rot[m + 8, m] = -1.0
        rot[m, m + 8] = 1.0
    c["c_rot"] = rot.astype(bf)
    return c


_CONST_SPECS = None


class K:
    def __init__(self, nseq, nlayers=DEPTH, phases=("nsa", "sb", "fox", "mem", "ffn"), final=True):
        self.nseq, self.nlayers, self.phases, self.final = nseq, nlayers, phases, final
        nc = self.nc = bass.Bass("TRN2", target_bir_lowering=False)
        self.S = Sched(nc)
        self._n = 0
        self.din = {}
        self.declare_io()
        self.alloc()
        self.load_consts()
        self.prep_all()
        print("sbuf bytes remaining", nc.sbuf_bytes_remaining)
        for s in range(nseq):
            self.run_seq(s)
            self.S.barrier()
        self.finish()

    def dram_in(self, name, shape, dt=F32):
        self.din[name] = self.nc.dram_tensor(name, list(shape), dt, kind="ExternalInput").ap()
        return self.din[name]

    def declare_io(self):
        nc, L = self.nc, DEPTH
        self.dram_in("x", [self.nseq, SEQ, DM])
        self.dram_in("mem", [self.nseq, 256, DM])
        for nm, shp in [("norm_mix", [L, DM]), ("w_in", [L, DM, INC]), ("b_forget", [L, 4]),
                        ("cmp_pe_k", [L, 32, 64]), ("cmp_pe_v", [L, 32, 64]),
                        ("cmp_wk1", [L, 32, 64, 64]), ("cmp_wk2", [L, 64, 64]),
                        ("cmp_wv1", [L, 32, 64, 64]), ("cmp_wv2", [L, 64, 64]),
                        ("w_out", [L, DM, DM]), ("norm_cross", [L, DM]), ("norm_mem", [L, DM]),
                        ("w_mq", [L, DM, 256]), ("w_mk", [L, DM, 256]), ("w_mv", [L, DM, 256]),
                        ("w_mo", [L, 256, DM]), ("norm_ffn", [L, DM]), ("w_up", [L, DM, 2 * DFF]),
                        ("conv_w", [L, 3, 2 * DFF]), ("conv_b", [L, 2 * DFF]), ("w_down", [L, DFF, DM]),
                        ("norm_final", [DM])]:
            self.dram_in(nm, shp)
        for nm, arr in _consts().items():
            self.dram_in(nm, arr.shape, BF16 if arr.dtype == ml_dtypes.bfloat16 else F32)
        self.y = nc.dram_tensor("y", [self.nseq, SEQ, DM], F32, kind="ExternalOutput").ap()
        d = lambda nm, shp: nc.dram_tensor(nm, shp, BF16).ap()
        self.W = []
        for l in range(self.nlayers):
            self.W.append(dict(
                win=d("b_win%d" % l, [DM, INC]), rot=d("b_rot%d" % l, [DM, 224]),
                wout=d("b_wout%d" % l, [DM, DM]), mq=d("b_mq%d" % l, [DM, 256]),
                mk=d("b_mk%d" % l, [DM, 256]), mv=d("b_mv%d" % l, [DM, 256]),
                mo=d("b_mo%d" % l, [256, DM]), up=d("b_up%d" % l, [DM, 2 * DFF]),
                down=d("b_down%d" % l, [DFF, DM]),
                w1k=d("b_w1k%d" % l, [64, 32 * 64]), w1v=d("b_w1v%d" % l, [64, 32 * 64]),
                w2k=d("b_w2k%d" % l, [64, 64]), w2v=d("b_w2v%d" % l, [64, 64]),
                pek=d("b_pek%d" % l, [32, 64]), pev=d("b_pev%d" % l, [32, 64])))
        self.hT_d = nc.dram_tensor("b_hT", [KC, 128, SEQ], BF16).ap()
        self.hT_dbuf = Buf("hT_d", None)
        self.wbuf_d = Buf("wdram", None)

    def sb(self, name, shape, dt=F32):
        return Buf(name, self.nc.alloc_sbuf_tensor(name, list(shape), dt))

    def alloc(self):
        nc = self.nc
        self.xres_t = nc.alloc_sbuf_tensor("xres", [128, NT, DM], F32)
        self.xt = [Buf("x%d" % i, self.xres_t) for i in range(NT)]
        self.ps = [Buf("ps%d" % i, nc.alloc_psum_tensor("ps%d" % i, [128, 512], F32), excl=True) for i in range(7)]
        self.pst = Buf("pst", nc.alloc_psum_tensor("pst", [128, 1024], BF16), excl=True)
        self._rot = 0
        self._rotbanks = [0, 1, 2]
        self._pending = None
        self._deferred = []
        self.wbuf = [self.sb("wbuf%d" % i, [128, 4096], BF16) for i in range(4)]
        self._wrot = 0
        self.hTb = [self.sb("hTb%d" % i, [128, KC, TCH], BF16) for i in range(2)]
        self._hrot = 0
        self.ARENA = 22528
        self.arena = nc.alloc_sbuf_tensor("arena", [128, self.ARENA], BF16)
        self.PT = [self.sb("PT%d" % i, [128, 512], BF16) for i in range(3)]
        self._prot = 0
        self.wf = [self.sb("wf%d" % i, [128, 514], F32) for i in range(5)]
        self._frot = 0
        self.h16 = [self.sb("h16_%d" % i, [128, DM], BF16) for i in range(2)]
        self._h16rot = 0
        self.small = [self.sb("sm%d" % i, [128, 64], F32) for i in range(4)]
        self._srot = 0
        self.o16 = [self.sb("o16_%d" % i, [128, 512], BF16) for i in range(2)]
        self._orot = 0
        self.oT = [self.sb("oT%d" % i, [128, 4, 128], BF16) for i in range(2)]
        self._otrot = 0
        self.oaccs = [self.sb("oacc%d" % i, [128, 8, 64], F32) for i in range(2)]
        self.oacc = self.oaccs[0]
        self.vcc = self.sb("vcc", [128, 2, 97], BF16)
        self.selpad = [self.sb("selpad%d" % g, [128, 96], BF16) for g in range(2)]

    def psum(self):
        rb = self._rotbanks
        self._rot = (self._rot + 1) % len(rb)
        return self.ps[rb[self._rot]]

    def pipe_unit(self, qk, pv):
        pt = qk()
        self.pipe_flush_pv()
        self._run_deferred(False)
        self._pending = (pv, pt)

    def _run_deferred(self, force):
        d, self._deferred = self._deferred, []
        keep = []
        for (f, n) in d:
            if n <= 0 or force:
                f()
            else:
                keep.append((f, n - 1))
        self._deferred = keep + self._deferred

    def pipe_flush_pv(self):
        if self._pending is not None:
            pv, pt = self._pending
            self._pending = None
            pv(pt)

    def pipe_defer(self, fn, delay=0):
        self._deferred.append((fn, delay))

    def pipe_drain(self):
        self.pipe_flush_pv()
        while self._deferred:
            self._run_deferred(True)

    def getw(self):
        b = self.wbuf[self._wrot]
        self._wrot = (self._wrot + 1) % 4
        return b

    def gethT(self):
        b = self.hTb[self._hrot]
        self._hrot = (self._hrot + 1) % 2
        return b

    def getPT(self):
        b = self.PT[self._prot]
        self._prot = (self._prot + 1) % 3
        return b

    def getf(self):
        b = self.wf[self._frot]
        self._frot = (self._frot + 1) % 5
        return b

    def getsm(self):
        b = self.small[self._srot]
        self._srot = (self._srot + 1) % 4
        return b

    def mm(self, ob, out, lb, lhsT, rb, rhs, start, stop, extra=()):
        nc = self.nc
        self.S.op("pe", lambda: nc.tensor.matmul(out, lhsT=lhsT, rhs=rhs, start=start, stop=stop,
                                                 skip_group_check=True),
                  reads=[lb, rb] + list(extra), writes=[ob])

    def act(self, ob, out, ib, in_, func, reads=(), **kw):
        nc = self.nc
        self.S.op("act", lambda: nc.scalar.activation(out=out, in_=in_, func=func, **kw),
                  reads=[ib] + list(reads), writes=[ob] if not isinstance(ob, (list, tuple)) else list(ob))

    def ts(self, eng, ob, out, ib, in0, s1, s2, op0, op1=None, reads=()):
        e = self.S.E[eng]
        if op1 is None:
            fn = lambda: e.tensor_scalar(out=out, in0=in0, scalar1=s1, scalar2=None, op0=op0)
        else:
            fn = lambda: e.tensor_scalar(out=out, in0=in0, scalar1=s1, scalar2=s2, op0=op0, op1=op1)
        self.S.op(eng, fn, reads=[ib] + list(reads), writes=[ob])

    def tt(self, eng, ob, out, ab, a, bb, b, op):
        e = self.S.E[eng]
        self.S.op(eng, lambda: e.tensor_tensor(out=out, in0=a, in1=b, op=op), reads=[ab, bb], writes=[ob])

    def stt(self, ob, out, ab, in0, scalar, bb, in1, op0, op1, reads=()):
        nc = self.nc
        self.S.op("dve", lambda: nc.vector.scalar_tensor_tensor(out=out, in0=in0, scalar=scalar, in1=in1,
                                                                op0=op0, op1=op1),
                  reads=[ab, bb] + list(reads), writes=[ob])

    def cp(self, eng, ob, out, ib, in_):
        if eng == "act":
            nc = self.nc
            self.S.op("act", lambda: nc.scalar.copy(out=out, in_=in_), reads=[ib], writes=[ob])
        else:
            e = self.S.E[eng]
            self.S.op(eng, lambda: e.tensor_copy(out=out, in_=in_), reads=[ib], writes=[ob])

    def memset(self, eng, ob, ap, val):
        e = self.S.E[eng]
        self.S.op(eng, lambda: e.memset(ap, val), writes=[ob])

    def load_consts(self):
        S = self.S
        self.C = {}
        for nm, arr in _consts().items():
            shp = list(arr.shape)
            if nm == "c_blk1h":
                continue
            b = self.sb("k_" + nm, shp, BF16 if arr.dtype == ml_dtypes.bfloat16 else F32)
            S.dma(b[:], self.din[nm], writes=[b])
            self.C[nm] = b
        L = self.nlayers
        self.gain = {}
        grow = Buf("grow", self.arena[0:8, 12288:12288 + 256].bitcast(F32))
        idf = self.C["c_identf"]
        nc = self.nc
        for nm in ("norm_mix", "norm_cross", "norm_mem", "norm_ffn"):
            g = self.sb("g_" + nm, [128, L, KC], F32)
            for l in range(L):
                S.dma(grow[:, :], self.din[nm][l].rearrange("(k p) -> k p", p=128), writes=[grow])
                ps = self.psum()
                S.op("pe", lambda ps=ps: nc.tensor.transpose(out=ps[:, 0:8], in_=grow[:, :], identity=idf[0:8, 0:8]),
                     reads=[grow, idf], writes=[ps])
                self.cp("dve", g, g[:, l, :], ps, ps[:, 0:8])
            self.gain[nm] = g
        g8 = self.sb("g8_mix", [128, L, KC], F32)
        self.ts("dve", g8, g8[:], self.gain["norm_mix"], self.gain["norm_mix"][:], 0.125, None, ALU.mult)
        self.gain["norm_mix8"] = g8
        g8c = self.sb("g8_cross", [128, L, KC], F32)
        self.ts("dve", g8c, g8c[:], self.gain["norm_cross"], self.gain["norm_cross"][:], 0.125, None, ALU.mult)
        self.gain["norm_cross8"] = g8c
        self.nbf = self.sb("nbf", [4, L], F32)
        S.dma(self.nbf[:], self.din["b_forget"][0:L].rearrange("l h -> h l"), writes=[self.nbf],
              allow_slow_non_contiguous=True)
        self.ts("dve", self.nbf, self.nbf[:], self.nbf, self.nbf[:], -1.0, None, ALU.mult)
        self.cw = self.sb("cw", [128, L, 44, 4], F32)
        crow = Buf("crow", self.arena[0:4, 0:4 * DFF].bitcast(F32))
        idf = self.C["c_identf"]
        for l in range(L):
            S.dma(crow[0:3, :], self.din["conv_w"][l], writes=[crow])
            S.dma(crow[3:4, :], self.din["conv_b"][l:l + 1, :], writes=[crow])
            for c0 in range(0, 44, 11):
                ps = self.psum()
                for cc in range(c0, c0 + 11):
                    nc = self.nc
                    S.op("pe", lambda cc=cc, ps=ps: nc.tensor.transpose(
                        out=ps[:, (cc - c0) * 4:(cc - c0) * 4 + 4], in_=crow[0:4, cc * 128:(cc + 1) * 128],
                        identity=idf[0:4, 0:4]), reads=[crow, idf], writes=[ps])
                self.cp("dve", self.cw, self.cw[:, l, c0:c0 + 11, :].rearrange("p c k -> p (c k)"), ps, ps[:, 0:44])
        for g in range(2):
            self.memset("pool", self.selpad[g], self.selpad[g][:], 0.0)
        for g in range(2):
            self.cp("pool", self.vcc, self.vcc[:, g, 64:97], self.C["c_vext"], self.C["c_vext"][:, :])
        self.ones4b = Buf("ones4b", self.C["c_ones"][0:4, :])
        S.barrier()

    def prep_all(self):
        S, nc = self.S, self.nc
        ar = self.arena
        NS = 3
        pin = [Buf("pin%d" % i, ar[:, i * 4096:(i + 1) * 4096].bitcast(F32)) for i in range(NS)]
        pout = [Buf("pout%d" % i, ar[:, 12288 + i * 2048: 12288 + (i + 1) * 2048]) for i in range(NS)]
        rott = Buf("rott", ar[:, 18432:18432 + 224])
        self._pk = 0
        engs = ["act", "dve"]

        def piece(src, dst, rows, cols, scale, after=None):
            i = self._pk % NS
            eng = engs[self._pk % 2]
            self._pk += 1
            S.dma(pin[i][0:rows, 0:cols], src, writes=[pin[i]])
            o, a = pout[i][0:rows, 0:cols], pin[i][0:rows, 0:cols]
            if isinstance(scale, tuple):
                sbuf, sap = scale
                if eng == "act":
                    self.act(pout[i], o, pin[i], a, AF.Copy, reads=[sbuf], scale=sap)
                else:
                    self.ts(eng, pout[i], o, pin[i], a, sap, None, ALU.mult, reads=[sbuf])
            else:
                if eng == "act":
                    self.act(pout[i], o, pin[i], a, AF.Copy, scale=float(scale))
                else:
                    self.ts(eng, pout[i], o, pin[i], a, float(scale), None, ALU.mult)
            if after is not None:
                after(pout[i])
            S.dma(dst, pout[i][0:rows, 0:cols], reads=[pout[i]], q="pool")

        def mat(src, dst, R, Ccols, gain=None, l=0, scale=1.0):
            for r0 in range(0, R, 128):
                rows = min(128, R - r0)
                for c0 in range(0, Ccols, 2048):
                    cols = min(2048, Ccols - c0)
                    sc = (gain, gain[0:rows, l, r0 // 128:r0 // 128 + 1]) if gain is not None else scale
                    piece(src[r0:r0 + rows, c0:c0 + cols], dst[r0:r0 + rows, c0:c0 + cols], rows, cols, sc)

        for l in range(self.nlayers):
            W, D = self.W[l], self.din
            g, g8 = self.gain["norm_mix"], self.gain["norm_mix8"]
            for rc in range(KC):
                r0 = rc * 128
                gs, g8s = (g, g[:, l, rc:rc + 1]), (g8, g8[:, l, rc:rc + 1])

                def rot_ops(src_off, nh, roff):
                    def f(pb):
                        v = pb[:, src_off:src_off + 64 * nh].rearrange("p (h d) -> p h d", d=64)
                        rt = rott[:, :].rearrange("p (h d) -> p h d", d=16)
                        self.ts("dve", rott, rt[:, roff:roff + nh, 0:8], pb, v[:, :, 8:16], -1.0, None, ALU.mult)
                        self.cp("dve", rott, rt[:, roff:roff + nh, 8:16], pb, v[:, :, 0:8])
                    return f

                def rot2(pb):
                    rot_ops(0, 2, 8)(pb)
                    rot_ops(256, 2, 10)(pb)
                    rot_ops(512, 2, 12)(pb)

                segs = [(0, 512, g8s, rot_ops(0, 8, 0)), (512, 1304, gs, rot2), (1304, 1560, g8s, None),
                        (1560, 2072, gs, None), (2072, 2328, g8s, None), (2328, 2844, gs, None)]
                for (a, b, sc, aft) in segs:
                    piece(D["w_in"][l, r0:r0 + 128, a:b], W["win"][r0:r0 + 128, a:b], 128, b - a, sc, aft)
                S.dma(W["rot"][r0:r0 + 128, :], rott[:, :], reads=[rott])
            mat(D["w_out"][l], W["wout"], DM, DM)
            mat(D["w_mq"][l], W["mq"], DM, 256, gain=self.gain["norm_cross8"], l=l)
            mat(D["w_mk"][l], W["mk"], DM, 256, gain=self.gain["norm_mem"], l=l)
            mat(D["w_mv"][l], W["mv"], DM, 256, gain=self.gain["norm_mem"], l=l)
            mat(D["w_mo"][l], W["mo"], 256, DM)
            mat(D["w_up"][l], W["up"], DM, 2 * DFF, gain=self.gain["norm_ffn"], l=l)
            mat(D["w_down"][l], W["down"], DFF, DM)
            for (sn, dn) in (("cmp_wk1", "w1k"), ("cmp_wv1", "w1v")):
                for l0 in range(0, 32, 16):
                    i = self._pk % NS
                    self._pk += 1
                    S.dma(pin[i][0:64, 0:1024].rearrange("p (l c) -> p l c", c=64),
                          D[sn][l, l0:l0 + 16].rearrange("l d c -> d l c"), writes=[pin[i]])
                    self.cp("dve", pout[i], pout[i][0:64, 0:1024], pin[i], pin[i][0:64, 0:1024])
                    S.dma(W[dn][:, l0 * 64:(l0 + 16) * 64], pout[i][0:64, 0:1024], reads=[pout[i]])
            mat(D["cmp_wk2"][l], W["w2k"], 64, 64)
            mat(D["cmp_wv2"][l], W["w2v"], 64, 64)
            mat(D["cmp_pe_k"][l], W["pek"], 32, 64)
            mat(D["cmp_pe_v"][l], W["pev"], 32, 64)
        S.barrier()

    def load_w(self, src, shape_view, nbytes_cols):
        b = self.getw()
        v = b[:, 0:nbytes_cols]
        if shape_view is not None:
            v = v.rearrange(shape_view[0], **shape_view[1])
        self.S.dma(v, src, reads=[self.wbuf_d], writes=[b])
        return b, v

    def load_wcols(self, wd, c0, ncols, rows=DM):
        nk = rows // 128
        return self.load_w(wd[:, c0:c0 + ncols].rearrange("(k p) n -> p k n", p=128),
                           ("p (k n) -> p k n", dict(k=nk)), nk * ncols)

    def rmsnorm_T(self, tiles, hb, hview, col0):
        nc, S = self.nc, self.S
        for i, t in enumerate(tiles):
            xb = self.xt[t]
            xa = self.xres_t[:, t, :]
            sm = self.getsm()
            h = self.h16[self._h16rot]
            self._h16rot ^= 1
            self.act([h, sm], h[:], xb, xa, AF.Square, accum_out=sm[:, 0:1])
            self.act(sm, sm[:, 1:2], sm, sm[:, 0:1], AF.Ln, scale=1.0 / DM, bias=EPS)
            self.act(sm, sm[:, 2:3], sm, sm[:, 1:2], AF.Exp, scale=-0.5)
            self.ts("dve", h, h[:], xb, xa, sm[:, 2:3], None, ALU.mult, reads=[sm])
            for kc in range(KC):
                S.op("pe", lambda kc=kc, h=h: nc.tensor.transpose(
                    out=self.pst[:, kc * 128:(kc + 1) * 128], in_=h[:, kc * 128:(kc + 1) * 128],
                    identity=self.C["c_ident"][:]), reads=[h, self.C["c_ident"]], writes=[self.pst])
            self.cp("act" if i % 2 else "dve", hb, hview[:, :, col0 + i * 128: col0 + (i + 1) * 128],
                    self.pst, self.pst[:, :].rearrange("p (k t) -> p k t", k=KC))

    def projT(self, dstb, dst, wb, wv, c0, M, hb, hv, ncols, evac="act", scale=None):
        ps = self.psum()
        for kc in range(KC):
            self.mm(ps, ps[0:M, 0:ncols], wb, wv[:, kc, c0:c0 + M], hb, hv[:, kc, 0:ncols], kc == 0, kc == KC - 1)
        self.rp_flush()
        if dst is not None:
            if scale is not None:
                self.act(dstb, dst, ps, ps[0:M, 0:ncols], AF.Copy, scale=scale)
            else:
                self.cp(evac, dstb, dst, ps, ps[0:M, 0:ncols])
        return ps

    _rp = None

    def rp_flush(self):
        if self._rp is not None:
            f, self._rp = self._rp, None
            f()

    def rope_rows(self, dstb, dst16, psA, src, K, t0, ncols):
        psB = self.ps[6]
        rot = self.C["c_rot"]
        self.mm(psB, psB[0:16, 0:ncols], rot, rot[0:K, :], dstb, src, True, True)
        cs, sn = self.C["c_cos"], self.C["c_sin"]
        f1, f2 = self.getf(), self.getf()
        self.tt("dve", f1, f1[0:16, 0:ncols], psA, psA[0:16, 0:ncols], cs, cs[:, t0:t0 + ncols], ALU.mult)
        self.tt("dve", f2, f2[0:16, 0:ncols], psB, psB[0:16, 0:ncols], sn, sn[:, t0:t0 + ncols], ALU.mult)
        self.tt("pool", dstb, dst16, f1, f1[0:16, 0:ncols], f2, f2[0:16, 0:ncols], ALU.add)

    def out_proj(self, o16b, ncol_o, wob, wov, tile):
        nc, S = self.nc, self.S
        nk = ncol_o // 128
        for k in range(nk):
            S.op("pe", lambda k=k: nc.tensor.transpose(
                out=self.pst[:, k * 128:(k + 1) * 128], in_=o16b[:, k * 128:(k + 1) * 128],
                identity=self.C["c_ident"][:]), reads=[o16b, self.C["c_ident"]], writes=[self.pst])
        oT = self.oT[self._otrot]
        self._otrot ^= 1
        self.cp("act", oT, oT[:, 0:nk, :], self.pst, self.pst[:, 0:nk * 128].rearrange("p (k t) -> p k t", k=nk))
        xb, xa = self.xt[tile], self.xres_t[:, tile, :]
        for n in range(2):
            ps = self.psum()
            for k in range(nk):
                self.mm(ps, ps[:, :], oT, oT[:, k, :], wob, wov[:, k, n * 512:(n + 1) * 512], k == 0, k == nk - 1)
            self.tt("dve", xb, xa[:, n * 512:(n + 1) * 512], xb, xa[:, n * 512:(n + 1) * 512], ps, ps[:, :], ALU.add)

    def load_hT(self, c):
        hb = self.gethT()
        self.S.dma(hb[:], self.hT_d[:, :, c * TCH:(c + 1) * TCH].rearrange("k p t -> p k t"),
                   reads=[self.hT_dbuf], writes=[hb])
        return hb

    def pvs(self, st, acc, out, ptb, lhsT, vb, rhs):
        self.mm(acc, out, ptb, lhsT, vb, rhs, st["first"], True)
        st["first"] = False

    def run_seq(self, s):
        S = self.S
        for t in range(NT):
            S.dma(self.xres_t[:, t, :], self.din["x"][s, t * 128:(t + 1) * 128, :], writes=[self.xt[t]])
        for l in range(self.nlayers):
            self.l = l
            self.s = s
            if any(p in self.phases for p in ("nsa", "sb", "fox")):
                self.phase_mix()
            if "mem" in self.phases:
                self.phase_mem()
            if "ffn" in self.phases:
                self.phase_ffn()
        if self.final:
            self.gfin = Buf("gfin", self.arena[:, 0:2048].bitcast(F32))
            S.dma(self.gfin[:, :], self.din["norm_final"].partition_broadcast(128), writes=[self.gfin])
        for t in range(NT):
            xb, xa = self.xt[t], self.xres_t[:, t, :]
            if self.final:
                sm = self.getsm()
                h = self.h16[self._h16rot]
                self._h16rot ^= 1
                self.act([h, sm], h[:], xb, xa, AF.Square, accum_out=sm[:, 0:1])
                self.act(sm, sm[:, 1:2], sm, sm[:, 0:1], AF.Ln, scale=1.0 / DM, bias=EPS)
                self.act(sm, sm[:, 2:3], sm, sm[:, 1:2], AF.Exp, scale=-0.5)
                self.stt(xb, xa, xb, xa, sm[:, 2:3], self.gfin, self.gfin[:, :], ALU.mult, ALU.mult, reads=[sm])
            self.S.dma(self.y[s, t * 128:(t + 1) * 128, :], xa, reads=[xb])

    ybuf = Buf("y", None)

    def finish(self):
        S = self.S
        deps = {k: v for k, v in S.cnt.items() if isinstance(k, tuple) and v > 0}
        S._wait("sp", deps)

    def phase_mix(self):
        S, nc, l = self.S, self.nc, self.l
        W = self.W[l]
        ar = self.arena
        A = lambda name, p, a, n: Buf(name, ar[0:p, a:a + n])
        kvc = A("kvc", 64, 0, 8192)
        ksT = A("ksT", 128, 8192, 4096)
        kwT = A("kwT", 128, 12288, 4096)
        vsw = A("vsw", 128, 16384, 4160)
        kcc = A("kcc", 64, 20544, 256)
        gat = Buf("gat", ar[:, 20800:20800 + 768].bitcast(F32))
        kvc_v = kvc[:, :].rearrange("p (j t) -> p j t", j=4)
        ksT_v = ksT[:, :].rearrange("p (g t) -> p g t", g=2)
        kwT_v = kwT[:, :].rearrange("p (g t) -> p g t", g=2)
        vsw_v = vsw[:, :].rearrange("p (t j d) -> p t j d", t=NT, j=4)
        kcc_v = kcc[:, :].rearrange("p (g n) -> p g n", g=2)
        do_nsa = "nsa" in self.phases
        if do_nsa:
            self.memset("pool", ksT, ksT[96:128, :], 0.0)
            self.memset("pool", kwT, kwT[64:128, :], 0.0)
            for g in range(2):
                S.dma(ksT_v[64:96, g, :], self.din["c_blk1h"], writes=[ksT])
            self.memset("pool", vsw, vsw_v[:, :, :, 64:65], 1.0)
        for c in range(NCH):
            hb = self.gethT()
            self.rmsnorm_T(range(4 * c, 4 * c + 4), hb, hb[:], 0)
            S.dma(self.hT_d[:, :, c * TCH:(c + 1) * TCH].rearrange("k p t -> p k t"), hb[:], reads=[hb],
                  writes=[self.hT_dbuf])
            if not do_nsa:
                continue
            wb, wv = self.load_wcols(W["win"], C_KC, 256)
            t0 = c * TCH

            def rp(dstb, dv, g, ps, K):
                self._rp = lambda: self.rope_rows(dstb, dv[0:16, g, t0:t0 + TCH], ps, dv[0:K, g, t0:t0 + TCH], K, t0, TCH)

            for g in range(2):
                ps = self.projT(kvc, kvc_v[0:64, g, t0:t0 + TCH], wb, wv, g * 64, 64, hb, hb[:], TCH)
                rp(kvc, kvc_v, g, ps, 64)
                self.projT(kvc, kvc_v[0:64, 2 + g, t0:t0 + TCH], wb, wv, 128 + g * 64, 64, hb, hb[:], TCH, evac="dve")
            wb, wv = self.load_wcols(W["win"], C_KS, 512)
            for g in range(2):
                ps = self.projT(ksT, ksT_v[0:64, g, t0:t0 + TCH], wb, wv, g * 64, 64, hb, hb[:], TCH)
                rp(ksT, ksT_v, g, ps, 128)
                ps = self.projT(kwT, kwT_v[0:64, g, t0:t0 + TCH], wb, wv, 256 + g * 64, 64, hb, hb[:], TCH)
                rp(kwT, kwT_v, g, ps, 128)
            for tl in range(4):
                t = 4 * c + tl
                ps = self.psum()
                if tl == 1:
                    self.rp_flush()
                for kc in range(KC):
                    self.mm(ps, ps[:, 0:128], hb, hb[:, kc, tl * 128:(tl + 1) * 128], wb, wv[:, kc, 128:256], kc == 0, kc == KC - 1)
                for kc in range(KC):
                    self.mm(ps, ps[:, 128:256], hb, hb[:, kc, tl * 128:(tl + 1) * 128], wb, wv[:, kc, 384:512], kc == 0, kc == KC - 1)
                self.cp("act", vsw, vsw_v[:, t, :, 0:64], ps, ps[:, 0:256].rearrange("p (j d) -> p j d", j=4))
        if do_nsa:
            self.nsa_compress(kvc, kvc_v, kcc, kcc_v)
            S.barrier()
            self.nsa_q(kvc, ksT, ksT_v, kwT, kwT_v, vsw, vsw_v, kcc, kcc_v, gat)
            S.barrier()
        if "sb" in self.phases:
            self.phase_sb()
            S.barrier()
        if "fox" in self.phases:
            self.phase_fox()
            S.barrier()

    def nsa_compress(self, kvc, kvc_v, kcc, kcc_v):
        S, nc, l = self.S, self.nc, self.l
        W = self.W[l]
        wb = self.getw()
        w1 = wb[0:64, 0:4096].rearrange("p (j l c) -> p j l c", j=2, l=32)
        S.dma(w1[:, 0, :, :], W["w1k"].rearrange("p (l c) -> p l c", c=64), reads=[self.wbuf_d], writes=[wb])
        S.dma(w1[:, 1, :, :], W["w1v"].rearrange("p (l c) -> p l c", c=64), reads=[self.wbuf_d], writes=[wb])
        wb2 = self.getw()
        w2 = wb2[0:64, 0:128].rearrange("p (j e) -> p j e", j=2)
        S.dma(w2[:, 0, :], W["w2k"], reads=[self.wbuf_d], writes=[wb2])
        S.dma(w2[:, 1, :], W["w2v"], reads=[self.wbuf_d], writes=[wb2])
        pen = wb2[0:32, 256:384].rearrange("p (j d) -> p j d", j=2)
        S.dma(pen[:, 0, :], W["pek"], reads=[self.wbuf_d], writes=[wb2])
        S.dma(pen[:, 1, :], W["pev"], reads=[self.wbuf_d], writes=[wb2])
        idn = self.C["c_ident"]
        for j in range(2):
            S.op("pe", lambda j=j: nc.tensor.transpose(out=self.pst[0:64, j * 32:(j + 1) * 32], in_=pen[:, j, :],
                                                       identity=idn[0:32, 0:32]), reads=[wb2, idn], writes=[self.pst])
        pet = self.getPT()
        peT = pet[0:64, 0:64].rearrange("p (j l) -> p j l", j=2)
        self.cp("dve", pet, pet[0:64, 0:64], self.pst, self.pst[0:64, 0:64])
        sm = self.getsm()
        for j in range(2):
            ps = self.psum()
            for li in range(32):
                self.mm(ps, ps[0:64, 0:1], wb, w1[:, j, li, :], pet, peT[:, j, li:li + 1], li == 0, li == 31)
            self.cp("dve", sm, sm[0:64, j:j + 1], ps, ps[0:64, 0:1])
        for j in range(2):
            for g in range(2):
                ps = self.psum()
                for li in range(32):
                    self.mm(ps, ps[0:64, 0:127], wb, w1[:, j, li, :], kvc, kvc_v[0:64, 2 * j + g, li:li + 16 * 126 + 1:16],
                            li == 0, li == 31)
                hid = self.getPT()
                self.act(hid, hid[0:64, 0:127], ps, ps[0:64, 0:127], AF.Silu, reads=[sm], bias=sm[0:64, j:j + 1], scale=1.0)
                ps2 = self.psum()
                if j == 0:
                    self.mm(ps2, ps2[0:64, 0:127], wb2, w2[:, 0, :], hid, hid[0:64, 0:127], True, True)
                    self.cp("dve", kcc, kcc_v[0:64, g, 0:127], ps2, ps2[0:64, 0:127])
                else:
                    self.mm(ps2, ps2[0:127, 0:64], hid, hid[0:64, 0:127], wb2, w2[:, 1, :], True, True)
                    self.cp("dve", self.vcc, self.vcc[0:127, g, 0:64], ps2, ps2[0:127, 0:64])

    def nsa_q(self, kvc, ksT, ksT_v, kwT, kwT_v, vsw, vsw_v, kcc, kcc_v, gat):
        S, nc, l = self.S, self.nc, self.l
        W = self.W[l]
        QnT = Buf("QnT", self.arena[0:128, 0:4096])
        Q = QnT[:, :].rearrange("p (g b h q) -> p g b h q", g=2, b=4, h=4)
        self.Qsel = Buf("Qsel", None)
        self.memset("pool", QnT, QnT[64:128, :], 0.0)
        gat_v = gat[:, :].rearrange("p (t c) -> p t c", c=24)
        idn = self.C["c_ident"]
        wob, wov = None, None
        for c in range(NCH):
            hb = self.load_hT(c)
            wb, wv = self.load_wcols(W["win"], C_QN, 512)
            wgb, wgv = self.load_wcols(W["win"], C_G, 24)
            if wob is None or True:
                wob, wov = self.load_w(W["wout"][0:512, :].rearrange("(k p) n -> p k n", p=128),
                                       ("p (k n) -> p k n", dict(k=4)), 4096)
            t0 = c * TCH
            for hh in range(8):
                g, h = hh // 4, hh % 4
                ps = self.psum()
                for kc in range(KC):
                    self.mm(ps, ps[0:64, :], wb, wv[:, kc, hh * 64:(hh + 1) * 64], hb, hb[:, kc, :], kc == 0, kc == KC - 1)
                self.rp_flush()
                self.cp("act", QnT, Q[0:64, g, :, h, :], ps, ps[0:64, :].rearrange("p (b q) -> p b q", b=4))

                def rq(g=g, h=h, ps=ps):
                    psB = self.ps[6]
                    rot = self.C["c_rot"]
                    self.mm(psB, psB[0:16, :].rearrange("p (b q) -> p b q", b=4), rot, rot[:, :], QnT, Q[0:128, g, :, h, :], True, True)
                    cs, sn = self.C["c_cos"], self.C["c_sin"]
                    f1, f2 = self.getf(), self.getf()
                    self.tt("dve", f1, f1[0:16, 0:TCH], ps, ps[0:16, :], cs, cs[:, t0:t0 + TCH], ALU.mult)
                    self.tt("dve", f2, f2[0:16, 0:TCH], psB, psB[0:16, :], sn, sn[:, t0:t0 + TCH], ALU.mult)
                    self.tt("pool", QnT, Q[0:16, g, :, h, :], f1, f1[0:16, 0:TCH].rearrange("p (b q) -> p b q", b=4),
                            f2, f2[0:16, 0:TCH].rearrange("p (b q) -> p b q", b=4), ALU.add)
                self._rp = rq
            for tl in range(4):
                t = 4 * c + tl
                ps = self.psum()
                if tl == 1:
                    self.rp_flush()
                for kc in range(KC):
                    self.mm(ps, ps[:, 0:24], hb, hb[:, kc, tl * 128:(tl + 1) * 128], wgb, wgv[:, kc, 0:24], kc == 0, kc == KC - 1)
                self.act(gat, gat_v[:, t, :], ps, ps[:, 0:24], AF.Exp, scale=-1.0)
                self.ts("dve", gat, gat_v[:, t, :], gat, gat_v[:, t, :], 1.0, None, ALU.add)
                S.op("dve", lambda t=t: nc.vector.reciprocal(out=gat_v[:, t, :], in_=gat_v[:, t, :]), reads=[gat], writes=[gat])
            for bl in range(4):
                qb = 4 * c + bl
                self.nsa_qblock(qb, bl, QnT, Q, ksT, ksT_v, kwT, kwT_v, vsw, vsw_v, kcc, kcc_v, gat, gat_v)

                def fin(qb=qb, wob=wob, wov=wov, oacc=self.oaccs[qb % 2]):
                    o16 = self.o16[self._orot]
                    self._orot ^= 1
                    self.cp("dve", o16, o16[:, :], oacc, oacc[:, :, :].rearrange("p h d -> p (h d)"))
                    self.out_proj(o16, 512, wob, wov, qb)
                self.pipe_defer(fin, delay=8)
            self.pipe_drain()

    def nsa_qblock(self, qb, bl, QnT, Q, ksT, ksT_v, kwT, kwT_v, vsw, vsw_v, kcc, kcc_v, gat, gat_v):
        S, nc = self.S, self.nc
        idn = self.C["c_ident"]
        oacc = self.oaccs[qb % 2]
        gvs = [gat_v[:, qb, g * 12:(g + 1) * 12].rearrange("p (h k) -> p h k", k=3) for g in range(2)]
        q64s = [Q[0:64, g, bl, :, :].rearrange("p h q -> p (h q)") for g in range(2)]
        q128s = [Q[0:128, g, bl, :, :].rearrange("p h q -> p (h q)") for g in range(2)]
        for g in range(2):
            acc = self.ps[3] if g == 0 else self.ps[6]
            st = {"first": True}

            def qk(g=g):
                ps = self.psum()
                self.mm(ps, ps[0:127, :], kcc, kcc_v[0:64, g, 0:127], QnT, q64s[g], True, False)
                nmc = self.C["c_nm_cmp"]
                self.mm(ps, ps[0:127, :].rearrange("p (h q) -> p h q", h=4), idn, idn[0:127, 0:127], nmc,
                        nmc[0:127, qb * 128:(qb + 1) * 128].unsqueeze(1).broadcast_to([127, 4, 128]), False, True)
                pt = self.getPT()
                self.act(pt, pt[0:127, :], ps, ps[0:127, :], AF.Exp)
                return pt

            def pv(pt, g=g, acc=acc, st=st):
                for h in range(4):
                    self.pvs(st, acc, acc[:, h * 97:(h + 1) * 97], pt, pt[0:127, h * 128:(h + 1) * 128],
                             self.vcc, self.vcc[0:127, g, :])

            def epi(g=g, acc=acc):
                gv = gvs[g]
                accv = acc[:, 0:388].rearrange("p (h d) -> p h d", h=4)
                sm = self.getsm()
                self.ts("dve", sm, sm[:, 0:4], acc, accv[:, :, 64], 1e-30, None, ALU.max)
                S.op("dve", lambda sm=sm: nc.vector.reciprocal(out=sm[:, 4:8], in_=sm[:, 0:4]), reads=[sm], writes=[sm])
                self.tt("dve", sm, sm[:, 8:12], sm, sm[:, 4:8], gat, gv[:, :, 0], ALU.mult)
                f = self.getf()
                fv = f[:, 0:128].rearrange("p (h j) -> p h j", h=4)
                self.tt("dve", f, fv, acc, accv[:, :, 65:97], sm, sm[:, 4:8].unsqueeze(2).broadcast_to([128, 4, 32]), ALU.mult)
                ov = oacc[:, g * 4:(g + 1) * 4, :]
                self.tt("dve", oacc, ov, acc, accv[:, :, 0:64], sm, sm[:, 8:12].unsqueeze(2).broadcast_to([128, 4, 64]), ALU.mult)
                imp = f[:, 128:160]
                self.tt("dve", f, imp, f, fv[:, 0, :], f, fv[:, 1, :], ALU.add)
                self.tt("dve", f, f[:, 160:192], f, fv[:, 2, :], f, fv[:, 3, :], ALU.add)
                self.tt("dve", f, imp, f, imp, f, f[:, 160:192], ALU.add)
                am = self.C["c_addmask"]
                self.tt("dve", f, imp, f, imp, am, am[:, qb, :], ALU.add)
                S.op("dve", lambda f=f: nc.vector.max(out=f[:, 192:200], in_=f[:, 128:160]), reads=[f], writes=[f])
                S.op("dve", lambda f=f: nc.vector.match_replace(out=f[:, 200:232], in_to_replace=f[:, 192:200],
                                                               in_values=f[:, 128:160], imm_value=-3.0e38), reads=[f], writes=[f])
                S.op("dve", lambda f=f: nc.vector.max(out=f[:, 232:240], in_=f[:, 200:232]), reads=[f], writes=[f])
                sp_ = self.selpad[g]
                self.ts("dve", sp_, sp_[:, 64:96], f, imp, f[:, 239:240], NEG, ALU.is_lt, ALU.mult)

            self.pipe_unit(qk, pv)
            self.pipe_defer(epi)

        def epi_b(g):
            sp_ = self.selpad[g]
            ps = self.psum()
            self.mm(ps, ps[0:96, 0:128], sp_, sp_[:, :], idn, idn[:, :], True, True)
            self.cp("act", self.Qsel, Q[64:96, g, bl, :, :], ps, ps[64:96, 0:128].unsqueeze(1).broadcast_to([32, 4, 128]))
        for g in range(2):
            acc = self.ps[5]
            st = {"first": True}
            for kb in range(max(0, qb - 4), qb + 1):
                d = qb - kb

                def qk(g=g, kb=kb, d=d):
                    ps = self.psum()
                    msk = d == 0 or d == 4
                    self.mm(ps, ps[:, :], kwT, kwT_v[0:128, g, kb * 128:(kb + 1) * 128], QnT, q128s[g], True, not msk)
                    if msk:
                        nm = self.C["c_nm_incl"] if d == 0 else self.C["c_nm_win"]
                        self.mm(ps, ps[:, :], idn, idn[:, :], nm, nm[:, :], False, True)
                    pt = self.getPT()
                    self.act(pt, pt[:, :], ps, ps[:, :], AF.Exp)
                    return pt

                def pv(pt, g=g, kb=kb, acc=acc, st=st):
                    for h in range(4):
                        self.pvs(st, acc, acc[:, h * 65:(h + 1) * 65], pt, pt[:, h * 128:(h + 1) * 128], vsw, vsw_v[:, kb, 2 + g, :])

                self.pipe_unit(qk, pv)
            self.pipe_defer(lambda g=g, acc=acc: self.nsa_accum(acc, gat, gvs[g], 2, g, oacc))
        for g in range(2):
            epi_b(g)
        for g in range(2):
            acc = self.ps[4]
            st = {"first": True}
            for kb in range(qb + 1):
                def qk(g=g, kb=kb):
                    ps = self.psum()
                    diag = kb == qb
                    self.mm(ps, ps[:, :], ksT, ksT_v[0:128, g, kb * 128:(kb + 1) * 128], QnT, q128s[g], True, not diag,
                            extra=[self.Qsel])
                    if diag:
                        nm = self.C["c_nm_incl"]
                        self.mm(ps, ps[:, :], idn, idn[:, :], nm, nm[:, :], False, True)
                    pt = self.getPT()
                    self.act(pt, pt[:, :], ps, ps[:, :], AF.Exp)
                    return pt

                def pv(pt, g=g, kb=kb, acc=acc, st=st):
                    for h in range(4):
                        self.pvs(st, acc, acc[:, h * 65:(h + 1) * 65], pt, pt[:, h * 128:(h + 1) * 128], vsw, vsw_v[:, kb, g, :])

                self.pipe_unit(qk, pv)
            self.pipe_defer(lambda g=g, acc=acc: self.nsa_accum(acc, gat, gvs[g], 1, g, oacc))

    def nsa_accum(self, acc, gat, gv, k, g, oacc):
        S, nc = self.S, self.nc
        accv = acc[:, 0:260].rearrange("p (h d) -> p h d", h=4)
        sm = self.getsm()
        S.op("dve", lambda: nc.vector.reciprocal(out=sm[:, 0:4], in_=accv[:, :, 64]), reads=[acc], writes=[sm])
        self.tt("dve", sm, sm[:, 4:8], sm, sm[:, 0:4], gat, gv[:, :, k], ALU.mult)
        f = self.getf()
        fv = f[:, 0:256].rearrange("p (h d) -> p h d", h=4)
        self.tt("dve", f, fv, acc, accv[:, :, 0:64], sm, sm[:, 4:8].unsqueeze(2).broadcast_to([128, 4, 64]), ALU.mult)
        ov = oacc[:, g * 4:(g + 1) * 4, :]
        self.tt("dve", oacc, ov, oacc, ov, f, fv, ALU.add)

    def phase_sb(self):
        S, nc, l = self.S, self.nc, self.l
        W = self.W[l]
        ar = self.arena
        kT = Buf("sbk", ar[0:128, 0:8192])
        kT_v = kT[:, :].rearrange("p (h t) -> p h t", h=4)
        vb = Buf("sbv", ar[:, 8192:8192 + 4096])
        v_v = vb[:, :].rearrange("p (t c) -> p t c", t=NT)
        qT = Buf("sbq", ar[0:128, 12288:12288 + 2048])
        q_v = qT[:, :].rearrange("p (h t) -> p h t", h=4)
        self.memset("pool", kT, kT[64:128, :], 0.0)
        self.memset("pool", qT, qT[64:128, :], 0.0)
        lacc = Buf("lacc", ar[:, 14336:14336 + 1024].bitcast(F32))
        lacc16 = [Buf("lacc16_%d" % i, ar[:, 15360 + i * 512:15360 + (i + 1) * 512]) for i in range(3)]
        l16 = [Buf("l16_%d" % i, ar[:, 16896 + i * 512:16896 + (i + 1) * 512]) for i in range(2)]
        idn, tri, ones = self.C["c_ident"], self.C["c_tri"], self.C["c_ones"]
        self._rotbanks = [0, 1, 2, 5, 6]
        for c in range(NCH):
            hb = self.load_hT(c)
            wb, wv = self.load_wcols(W["win"], C_KSB, 512)
            for h in range(4):
                self.projT(kT, kT_v[0:64, h, c * TCH:(c + 1) * TCH], wb, wv, h * 64, 64, hb, hb[:], TCH,
                           evac="act" if h % 2 else "dve")
            for tl in range(4):
                ps = self.psum()
                for kc in range(KC):
                    self.mm(ps, ps[:, 0:256], hb, hb[:, kc, tl * 128:(tl + 1) * 128], wb, wv[:, kc, 256:512], kc == 0, kc == KC - 1)
                self.cp("act", vb, v_v[:, 4 * c + tl, :], ps, ps[:, 0:256])
        for c in range(NCH):
            hb = self.load_hT(c)
            wb, wv = self.load_wcols(W["win"], C_QS, 256)
            wob, wov = self.load_w(W["wout"][512:768, :].rearrange("(k p) n -> p k n", p=128),
                                   ("p (k n) -> p k n", dict(k=2)), 2048)
            for h in range(4):
                self.projT(qT, q_v[0:64, h, :], wb, wv, h * 64, 64, hb, hb[:], TCH, evac="act" if h % 2 else "dve")
            o16s = [self.sb_dummy(i) for i in range(4)]
            units = []
            for h in range(4):
                kbs = list(range(4 * c + 3, -1, -1))
                for j, kb in enumerate(kbs):
                    units.append(dict(h=h, kb=kb, first=j == 0, last=j == len(kbs) - 1,
                                      off=max(0, (kb - 4 * c) * 128), diag=kb >= 4 * c,
                                      acc=self.ps[3 + (h % 2)], st=None))
            sts = {}
            n = len(units)

            def stageA(i):
                u = units[i]
                h, kb, off = u["h"], u["kb"], u["off"]
                if u["first"]:
                    self.memset("pool", lacc, lacc[:, :], 0.0)
                    sts[h] = {"first": True}
                ks = kT_v[0:128, h, kb * 128:(kb + 1) * 128]
                ps1 = self.psum()
                self.mm(ps1, ps1[:, off:512], kT, ks, qT, q_v[0:128, h, off:512], True, True)
                sp = self.getf()
                self.act(sp, sp[:, off:512], ps1, ps1[:, off:512], AF.Exp, scale=-1.0)
                self.act(sp, sp[:, off:512], sp, sp[:, off:512], AF.Ln, bias=1.0, scale=1.0)
                lb = l16[i % 2]
                self.stt(lb, lb[:, off:512], ps1, ps1[:, off:512], -1.0, sp, sp[:, off:512], ALU.mult, ALU.subtract)
                if u["diag"]:
                    m01 = self.C["c_m01_strict"]
                    self.tt("pool", lb, lb[:, off:off + 128], lb, lb[:, off:off + 128], m01, m01[:, :], ALU.mult)
                if not u["last"]:
                    self.tt("dve", lacc, lacc[:, off:512], lacc, lacc[:, off:512], lb, lb[:, off:512], ALU.add)
                    la = lacc16[i % 3]
                    self.cp("dve", la, la[:, :], lacc, lacc[:, :])

            def stageB(i):
                u = units[i]
                h, kb, off = u["h"], u["kb"], u["off"]
                ks = kT_v[0:128, h, kb * 128:(kb + 1) * 128]
                lb = l16[i % 2]
                ps2 = self.psum()
                grp = [(ps2[:, off:512], kT, ks, qT, q_v[0:128, h, off:512]),
                       (ps2[:, off:512], tri, tri[:, :], lb, lb[:, off:512])]
                if not u["first"]:
                    la = lacc16[(i - 1) % 3]
                    grp.append((ps2[:, off:512], ones, ones[:, 0:128], la, la[:, off:512]))
                if u["diag"]:
                    nm = self.C["c_nm_strict"]
                    grp.append((ps2[:, off:off + 128], idn, idn[:, :], nm, nm[:, 0:128]))
                for gi, (o_, lb_, l_, rb_, r_) in enumerate(grp):
                    self.mm(ps2, o_, lb_, l_, rb_, r_, gi == 0, gi == len(grp) - 1)
                pt = self.getPT()
                self.act(pt, pt[:, off:512], ps2, ps2[:, off:512], AF.Exp)
                u["pt"] = pt

            def stageC(i):
                u = units[i]
                h, kb, off, acc, pt = u["h"], u["kb"], u["off"], u["acc"], u["pt"]
                for qbl in range(off // 128, 4):
                    self.pvs(sts[h], acc, acc[:, qbl * 64:(qbl + 1) * 64], pt, pt[:, qbl * 128:(qbl + 1) * 128],
                             vb, v_v[:, kb, h * 64:(h + 1) * 64])
                if u["last"]:
                    for qbl in range(4):
                        self.cp("dve", o16s[qbl], o16s[qbl][:, h * 64:(h + 1) * 64], acc, acc[:, qbl * 64:(qbl + 1) * 64])

            for i in range(n + 2):
                if i < n:
                    stageA(i)
                if 0 <= i - 1 < n:
                    stageB(i - 1)
                if 0 <= i - 2 < n:
                    stageC(i - 2)
            for qbl in range(4):
                self.out_proj(o16s[qbl], 256, wob, wov, 4 * c + qbl)
        self._rotbanks = [0, 1, 2]

    def sb_dummy(self, i):
        if not hasattr(self, "_o4"):
            self._o4 = [Buf("o4_%d" % j, self.o16[j // 2][:, (j % 2) * 256:(j % 2 + 1) * 256]) for j in range(4)]
        return self._o4[i]

    def phase_fox(self):
        S, nc, l = self.S, self.nc, self.l
        W = self.W[l]
        ar = self.arena
        kT = Buf("fxk", ar[0:128, 0:8192])
        kT_v = kT[:, :].rearrange("p (h t) -> p h t", h=4)
        self.memset("pool", kT, kT[64:128, :], 0.0)
        vb = Buf("fxv", ar[:, 8192:8192 + 4160])
        v_v = vb[:, :].rearrange("p (t h d) -> p t h d", t=NT, h=4)
        qT = Buf("fxq", ar[0:128, 12352:12352 + 2048])
        q_v = qT[:, :].rearrange("p (h t) -> p h t", h=4)
        self.memset("pool", qT, qT[64:128, :], 0.0)
        csp = Buf("csp", ar[0:4, 14400:14400 + 4096].bitcast(F32))
        hi = Buf("hi", ar[0:4, 18496:18496 + 2048])
        nhi = Buf("nhi", ar[0:4, 22016:22016 + 512])
        idn = self.C["c_ident"]
        place = self.C["c_place"][:, :].rearrange("p (k h m) -> p k h m", k=6, h=4)
        plb = self.C["c_place"]
        self.memset("pool", vb, v_v[:, :, :, 64:65], 1.0)
        self._rotbanks = [0, 1, 2, 5, 6]
        for c in range(NCH):
            hb = self.load_hT(c)
            wb, wv = self.load_wcols(W["win"], C_KF, 512)
            wfb, wfv = self.load_wcols(W["win"], C_FL, 4)
            t0 = c * TCH
            for h in range(4):
                self.projT(kT, kT_v[0:64, h, t0:t0 + TCH], wb, wv, h * 64, 64, hb, hb[:], TCH,
                           evac="act" if h % 2 else "dve")
            for tl in range(4):
                ps = self.psum()
                for kc in range(KC):
                    self.mm(ps, ps[:, 0:256], hb, hb[:, kc, tl * 128:(tl + 1) * 128], wb, wv[:, kc, 256:512], kc == 0, kc == KC - 1)
                self.cp("act", vb, v_v[:, 4 * c + tl, :, 0:64], ps, ps[:, 0:256].rearrange("p (h d) -> p h d", h=4))
            ps = self.psum()
            for kc in range(KC):
                self.mm(ps, ps[0:4, :], wfb, wfv[:, kc, 0:4], hb, hb[:, kc, :], kc == 0, kc == KC - 1)
            e, sp = self.getf(), self.getf()
            self.act(e, e[0:4, 0:TCH], ps, ps[0:4, :], AF.Exp, reads=[self.nbf], scale=-1.0, bias=self.nbf[:, self.l:self.l + 1])
            self.act(sp, sp[0:4, 0:TCH], e, e[0:4, 0:TCH], AF.Ln, bias=1.0, scale=1.0)
            init = 0.0 if c == 0 else csp[:, t0 - 1:t0]
            S.op("dve", lambda init=init, sp=sp, t0=t0: nc.vector.tensor_tensor_scan(
                out=csp[:, t0:t0 + TCH], data0=self.ones4b[:, :], data1=sp[0:4, 0:TCH], initial=init,
                op0=ALU.mult, op1=ALU.add), reads=[self.ones4b, sp, csp], writes=[csp])
            self.cp("dve", hi, hi[:, t0:t0 + TCH], csp, csp[:, t0:t0 + TCH])
            f = self.getf()
            self.tt("dve", f, f[0:4, 0:TCH], csp, csp[:, t0:t0 + TCH], hi, hi[:, t0:t0 + TCH], ALU.subtract)
            lo16 = self.getPT()
            self.cp("dve", lo16, lo16[0:4, 0:TCH], f, f[0:4, 0:TCH])
            for h in range(4):
                ps = self.psum()
                self.mm(ps, ps[0:68, :], plb, place[:, 3, h, :], hi, hi[:, t0:t0 + TCH], True, False)
                self.mm(ps, ps[0:68, :], plb, place[:, 4, h, :], lo16, lo16[0:4, 0:TCH], False, False)
                self.mm(ps, ps[0:68, :], plb, place[:, 5, h, :], self.ones4b, self.ones4b[:, :], False, True)
                self.cp("act", kT, kT_v[64:68, h, t0:t0 + TCH], ps, ps[64:68, :])
        for c in range(NCH):
            hb = self.load_hT(c)
            t0 = c * TCH
            wb, wv = self.load_wcols(W["win"], C_QF, 256)
            wob, wov = self.load_w(W["wout"][768:1024, :].rearrange("(k p) n -> p k n", p=128),
                                   ("p (k n) -> p k n", dict(k=2)), 2048)
            for h in range(4):
                self.projT(qT, q_v[0:64, h, :], wb, wv, h * 64, 64, hb, hb[:], TCH, evac="act" if h % 2 else "dve")
            self.ts("dve", nhi, nhi[:, 0:TCH], hi, hi[:, t0:t0 + TCH], -1.0, None, ALU.mult)
            f = self.getf()
            self.tt("dve", f, f[0:4, 0:TCH], hi, hi[:, t0:t0 + TCH], csp, csp[:, t0:t0 + TCH], ALU.subtract)
            nlo = self.getPT()
            self.cp("dve", nlo, nlo[0:4, 0:TCH], f, f[0:4, 0:TCH])
            for h in range(4):
                ps = self.psum()
                self.mm(ps, ps[0:68, :], plb, place[:, 0, h, :], nhi, nhi[:, 0:TCH], True, False)
                self.mm(ps, ps[0:68, :], plb, place[:, 1, h, :], nlo, nlo[0:4, 0:TCH], False, False)
                self.mm(ps, ps[0:68, :], plb, place[:, 2, h, :], self.ones4b, self.ones4b[:, :], False, True)
                self.cp("act", qT, q_v[64:68, h, :], ps, ps[64:68, :])
            o16s = [self.sb_dummy(i) for i in range(4)]
            for h in range(4):
                acc = self.ps[3 + (h % 2)]
                st = {"first": True}
                for kb in range(4 * c + 3, -1, -1):
                    off = max(0, (kb - 4 * c) * 128)
                    diag = kb >= 4 * c

                    def qk(h=h, kb=kb, off=off, diag=diag):
                        ps = self.psum()
                        self.mm(ps, ps[:, off:512], kT, kT_v[0:128, h, kb * 128:(kb + 1) * 128], qT, q_v[0:128, h, off:512], True, not diag)
                        if diag:
                            nm = self.C["c_nm_incl"]
                            self.mm(ps, ps[:, off:off + 128], idn, idn[:, :], nm, nm[:, 0:128], False, True)
                        pt = self.getPT()
                        self.act(pt, pt[:, off:512], ps, ps[:, off:512], AF.Exp)
                        return pt

                    def pv(pt, h=h, kb=kb, off=off, acc=acc, st=st):
                        for qbl in range(off // 128, 4):
                            self.pvs(st, acc, acc[:, qbl * 65:(qbl + 1) * 65], pt, pt[:, qbl * 128:(qbl + 1) * 128],
                                     vb, v_v[:, kb, h, :])

                    self.pipe_unit(qk, pv)

                def epi(h=h, acc=acc):
                    accv = acc[:, 0:260].rearrange("p (b d) -> p b d", b=4)
                    sm = self.getsm()
                    S.op("dve", lambda: nc.vector.reciprocal(out=sm[:, 0:4], in_=accv[:, :, 64]), reads=[acc], writes=[sm])
                    for qbl in range(4):
                        self.ts("dve", o16s[qbl], o16s[qbl][:, h * 64:(h + 1) * 64], acc, accv[:, qbl, 0:64],
                                sm[:, qbl:qbl + 1], None, ALU.mult, reads=[sm])
                self.pipe_defer(epi)
            self.pipe_drain()
            for qbl in range(4):
                self.out_proj(o16s[qbl], 256, wob, wov, 4 * c + qbl)
        self._rotbanks = [0, 1, 2]

    def phase_mem(self):
        S, nc, l = self.S, self.nc, self.l
        W = self.W[l]
        ar = self.arena
        mx = Buf("mx", ar[:, 0:4096].bitcast(F32))
        mx_v = mx[:, :].rearrange("p (t d) -> p t d", t=2)
        mT = Buf("mT", ar[:, 4096:4096 + 2048])
        mT_v = mT[:, :].rearrange("p (k t) -> p k t", k=KC)
        kT = Buf("mk", ar[0:128, 6144:6144 + 1024])
        kT_v = kT[:, :].rearrange("p (h t) -> p h t", h=4)
        self.memset("pool", kT, kT[64:128, :], 0.0)
        vb = Buf("mv", ar[:, 7168:7168 + 520])
        v_v = vb[:, :].rearrange("p (t h d) -> p t h d", t=2, h=4)
        qT = Buf("mq", ar[0:128, 7688:7688 + 2048])
        q_v = qT[:, :].rearrange("p (h t) -> p h t", h=4)
        self.memset("pool", qT, qT[64:128, :], 0.0)
        idn = self.C["c_ident"]
        for t in range(2):
            S.dma(mx_v[:, t, :], self.din["mem"][self.s, t * 128:(t + 1) * 128, :], writes=[mx])
        self.memset("pool", vb, v_v[:, :, :, 64:65], 1.0)
        for t in range(2):
            sm = self.getsm()
            h = self.h16[self._h16rot]
            self._h16rot ^= 1
            self.act([h, sm], h[:], mx, mx_v[:, t, :], AF.Square, accum_out=sm[:, 0:1])
            self.act(sm, sm[:, 1:2], sm, sm[:, 0:1], AF.Ln, scale=1.0 / DM, bias=EPS)
            self.act(sm, sm[:, 2:3], sm, sm[:, 1:2], AF.Exp, scale=-0.5)
            self.ts("dve", h, h[:], mx, mx_v[:, t, :], sm[:, 2:3], None, ALU.mult, reads=[sm])
            for kc in range(KC):
                S.op("pe", lambda kc=kc, h=h: nc.tensor.transpose(
                    out=self.pst[:, kc * 128:(kc + 1) * 128], in_=h[:, kc * 128:(kc + 1) * 128],
                    identity=idn[:]), reads=[h, idn], writes=[self.pst])
            self.cp("dve", mT, mT_v[:, :, t * 128:(t + 1) * 128], self.pst, self.pst[:, :].rearrange("p (k t) -> p k t", k=KC))
        wb, wv = self.load_wcols(W["mk"], 0, 256)
        for h in range(4):
            self.projT(kT, kT_v[0:64, h, :], wb, wv, h * 64, 64, mT, mT_v, 256)
        wb, wv = self.load_wcols(W["mv"], 0, 256)
        for t in range(2):
            ps = self.psum()
            for kc in range(KC):
                self.mm(ps, ps[:, 0:256], mT, mT_v[:, kc, t * 128:(t + 1) * 128], wb, wv[:, kc, :], kc == 0, kc == KC - 1)
            self.cp("act", vb, v_v[:, t, :, 0:64], ps, ps[:, 0:256].rearrange("p (h d) -> p h d", h=4))
        wqb, wqv = self.load_wcols(W["mq"], 0, 256)
        wob, wov = self.load_w(W["mo"].rearrange("(k p) n -> p k n", p=128), ("p (k n) -> p k n", dict(k=2)), 2048)
        for c in range(NCH):
            hb = self.gethT()
            self.rmsnorm_T(range(4 * c, 4 * c + 4), hb, hb[:], 0)
            for h in range(4):
                self.projT(qT, q_v[0:64, h, :], wqb, wqv, h * 64, 64, hb, hb[:], TCH, evac="act" if h % 2 else "dve")
            o16s = [self.sb_dummy(i) for i in range(4)]
            for h in range(4):
                acc = self.ps[3 + (h % 2)]
                st = {"first": True}
                for kb in range(2):
                    def qk(h=h, kb=kb):
                        ps = self.psum()
                        self.mm(ps, ps[:, :], kT, kT_v[0:128, h, kb * 128:(kb + 1) * 128], qT, q_v[0:128, h, :], True, True)
                        pt = self.getPT()
                        self.act(pt, pt[:, :], ps, ps[:, :], AF.Exp)
                        return pt

                    def pv(pt, h=h, kb=kb, acc=acc, st=st):
                        for qbl in range(4):
                            self.pvs(st, acc, acc[:, qbl * 65:(qbl + 1) * 65], pt, pt[:, qbl * 128:(qbl + 1) * 128], vb, v_v[:, kb, h, :])

                    self.pipe_unit(qk, pv)

                def epi(h=h, acc=acc):
                    accv = acc[:, 0:260].rearrange("p (b d) -> p b d", b=4)
                    sm = self.getsm()
                    S.op("dve", lambda: nc.vector.reciprocal(out=sm[:, 0:4], in_=accv[:, :, 64]), reads=[acc], writes=[sm])
                    for qbl in range(4):
                        self.ts("dve", o16s[qbl], o16s[qbl][:, h * 64:(h + 1) * 64], acc, accv[:, qbl, 0:64],
                                sm[:, qbl:qbl + 1], None, ALU.mult, reads=[sm])
                self.pipe_defer(epi)
            self.pipe_drain()
            for qbl in range(4):
                self.out_proj(o16s[qbl], 256, wob, wov, 4 * c + qbl)
        S.barrier()

    def phase_ffn(self):
        S, nc, l = self.S, self.nc, self.l
        W = self.W[l]
        ar = self.arena
        gT = Buf("gT", ar[:, 0:11264])
        g_v = gT[:, :].rearrange("p (k t) -> p k t", k=22)
        halo = Buf("halo", ar[:, 11264:11264 + 176].bitcast(F32))
        halo_v = halo[:, :].rearrange("p (c k) -> p c k", k=2)
        self.memset("pool", halo, halo[:, :], 0.0)
        cw = self.cw
        for c in range(NCH):
            hb = self.gethT()
            self.rmsnorm_T(range(4 * c, 4 * c + 4), hb, hb[:], 0)
            wcur = {}
            uy = {}

            def stage1(cc):
                cg, ci = cc // 4, cc % 4
                if ci == 0:
                    wcur["w"] = self.load_wcols(W["up"], cg * 512, 512)
                wb, wv = wcur["w"]
                ps = self.psum()
                for kc in range(KC):
                    self.mm(ps, ps[:, :], wb, wv[:, kc, ci * 128:(ci + 1) * 128], hb, hb[:, kc, :], kc == 0, kc == KC - 1)
                u = self.getf()
                y = self.getf()
                self.cp("pool", u, u[:, 0:2], halo, halo_v[:, cc, :])
                self.cp("act", u, u[:, 2:514], ps, ps[:, :])
                self.act(y, y[:, 0:512], ps, ps[:, :], AF.Copy, reads=[cw], scale=cw[:, l, cc, 2:3])
                self.cp("pool", halo, halo_v[:, cc, :], u, u[:, 512:514])
                uy[cc] = [u, y]

            def stage2(cc):
                u, y = uy[cc]
                self.stt(y, y[:, 0:512], u, u[:, 1:513], cw[:, l, cc, 1:2], y, y[:, 0:512], ALU.mult, ALU.add, reads=[cw])
                self.stt(y, y[:, 0:512], u, u[:, 0:512], cw[:, l, cc, 0:1], y, y[:, 0:512], ALU.mult, ALU.add, reads=[cw])

            def stage3(cc):
                y = uy.pop(cc)[1]
                if cc < 22:
                    self.act(gT, g_v[:, cc, :], y, y[:, 0:512], AF.Silu, reads=[cw], bias=cw[:, l, cc, 3:4], scale=1.0)
                else:
                    self.stt(gT, g_v[:, cc - 22, :], y, y[:, 0:512], cw[:, l, cc, 3:4], gT, g_v[:, cc - 22, :],
                             ALU.add, ALU.mult, reads=[cw])

            for i in range(44 + 2):
                if i < 44:
                    stage1(i)
                if 0 <= i - 1 < 44:
                    stage2(i - 1)
                if 0 <= i - 2 < 44:
                    stage3(i - 2)
            for n in range(2):
                accs = [self.ps[3 + tl] for tl in range(4)]
                for pc in range(6):
                    k0 = pc * 4
                    nk = min(4, 22 - k0)
                    wdb, wdv = self.load_w(W["down"][k0 * 128:(k0 + nk) * 128, n * 512:(n + 1) * 512].rearrange("(k p) n -> p k n", p=128),
                                           ("p (k n) -> p k n", dict(k=nk)), nk * 512)
                    for tl in range(4):
                        for k in range(nk):
                            self.mm(accs[tl], accs[tl][:, :], gT, g_v[:, k0 + k, tl * 128:(tl + 1) * 128], wdb, wdv[:, k, :],
                                    k0 + k == 0, k0 + k == 21)
                for tl in range(4):
                    t = 4 * c + tl
                    xb, xa = self.xt[t], self.xres_t[:, t, :]
                    self.tt("dve", xb, xa[:, n * 512:(n + 1) * 512], xb, xa[:, n * 512:(n + 1) * 512], accs[tl], accs[tl][:, :], ALU.add)
        S.barrier()


_PROG = {}


def _get_prog(nseq):
    if nseq not in _PROG:
        _PROG[nseq] = K(nseq)
    return _PROG[nseq]


def kernel(**inputs):
    x = np.ascontiguousarray(np.asarray(inputs["x"], dtype=np.float32))
    mem = np.ascontiguousarray(np.asarray(inputs["mem"], dtype=np.float32))
    B = x.shape[0]
    ncores = 8
    per = B // ncores
    prog = _get_prog(per)
    consts = _consts()
    in_maps = []
    for i in range(ncores):
        m = {"x": x[i * per:(i + 1) * per], "mem": mem[i * per:(i + 1) * per]}
        for k, v in inputs.items():
            if k not in ("x", "mem"):
                m[k] = np.ascontiguousarray(np.asarray(v, dtype=np.float32))
        m.update(consts)
        in_maps.append(m)
    res = run_bass_kernel_spmd(prog.nc, in_maps, core_ids=list(range(ncores)))
    return np.concatenate([np.asarray(r["y"], dtype=np.float32) for r in res.results], axis=0)
```

```python
import numpy as np
import ml_dtypes
import concourse.bass as bass
import concourse.mybir as mybir
from concourse.bass_utils import run_bass_kernel_spmd

F32 = mybir.dt.float32
BF16 = mybir.dt.bfloat16
AF = mybir.ActivationFunctionType
ALU = mybir.AluOpType

SEQ, DM, KC, NT, TCH, NCH = 2048, 1024, 8, 16, 512, 4
DEPTH = 4
DFF = 2816
NEG = -30000.0
BIG = 1.0e30
EPS = 1e-6
C_QN, C_KC, C_VC, C_KS, C_VS, C_KW, C_VW, C_G = 0, 512, 640, 768, 896, 1024, 1152, 1280
C_QS, C_KSB, C_VSB, C_QF, C_KF, C_VF, C_FL = 1304, 1560, 1816, 2072, 2328, 2584, 2840
INC = 2844


class Buf:
    __slots__ = ("name", "t", "lw", "rd", "excl")

    def __init__(self, name, t, excl=False):
        self.name, self.t, self.lw, self.rd, self.excl = name, t, None, {}, excl

    def __getitem__(self, idx):
        return self.t[idx]


class Sched:
    NDSEM = 24

    def __init__(self, nc):
        self.nc = nc
        self.E = {"pe": nc.tensor, "act": nc.scalar, "dve": nc.vector, "pool": nc.gpsimd, "sp": nc.sync}
        self.sems, self.cnt = {}, {}
        for k in self.E:
            self.sems[k] = nc.alloc_semaphore("s_" + k)
            self.cnt[k] = 0
        for i in range(self.NDSEM):
            self.sems[("d", i)] = nc.alloc_semaphore("d%d" % i)
            self.cnt[("d", i)] = 0
        self.seen = {k: {} for k in self.E}
        self.dnext = 0
        self.ninstr = 0
        self.nwait = 0

    def _deps(self, reads, writes):
        deps = {}

        def add(kv):
            if kv is not None and deps.get(kv[0], 0) < kv[1]:
                deps[kv[0]] = kv[1]

        for b in reads:
            add(b.lw)
            if b.excl:
                for kv in b.rd.items():
                    add(kv)
        for b in writes:
            add(b.lw)
            for kv in b.rd.items():
                add(kv)
        return deps

    def _wait(self, eng, deps):
        seen, e = self.seen[eng], self.E[eng]
        for k, v in deps.items():
            if k == "pe" and eng == "pe":
                continue
            if seen.get(k, 0) >= v:
                continue
            e.wait_ge(self.sems[k], v)
            seen[k] = v
            self.nwait += 1

    def _mark(self, key, val, reads, writes):
        for b in reads:
            if b.excl:
                b.lw, b.rd = (key, val), {}
            else:
                b.rd[key] = val
        for b in writes:
            b.lw, b.rd = (key, val), {}

    def op(self, eng, fn, reads=(), writes=()):
        self._wait(eng, self._deps(reads, writes))
        ins = fn()
        self.cnt[eng] += 1
        ins.then_inc(self.sems[eng], 1)
        self._mark(eng, self.cnt[eng], reads, writes)
        self.ninstr += 1
        return ins

    def dma(self, out_ap, in_ap, reads=(), writes=(), q="sp", **kw):
        deps = self._deps(reads, writes)
        dk = ("d", self.dnext)
        self.dnext = (self.dnext + 1) % self.NDSEM
        if self.cnt[dk] > deps.get(dk, 0):
            deps[dk] = self.cnt[dk]
        self._wait(q, deps)
        ins = self.E[q].dma_start(out=out_ap, in_=in_ap, **kw)
        self.cnt[dk] += 16
        ins.then_inc(self.sems[dk], 16)
        self._mark(dk, self.cnt[dk], reads, writes)
        self.ninstr += 1
        return ins

    def barrier(self):
        deps = {k: v for k, v in self.cnt.items() if v > 0 and k != "sp"}
        for eng in self.E:
            self._wait(eng, dict(deps))


def _consts():
    bf = ml_dtypes.bfloat16
    c = {}
    half = 8
    inv = 500000.0 ** (-np.arange(half, dtype=np.float32) / half)
    ang = np.arange(SEQ, dtype=np.float32)[None, :] * inv[:, None]
    cs = np.concatenate([np.cos(ang), np.cos(ang)], 0).astype(np.float32)
    sn = np.concatenate([np.sin(ang), np.sin(ang)], 0).astype(np.float32)
    c["c_cos"], c["c_sin"] = cs.astype(bf), sn.astype(bf)
    j = np.arange(128)[:, None]
    t = np.arange(128)[None, :]
    c["c_ident"] = np.eye(128, dtype=np.float32).astype(bf)
    c["c_identf"] = np.eye(8, dtype=np.float32)
    c["c_ones"] = np.ones((128, 512), np.float32).astype(bf)
    c["c_tri"] = (j >= t).astype(np.float32).astype(bf)
    c["c_nm_incl"] = np.tile(np.where(j > t, NEG, 0.0), (1, 4)).astype(np.float32).astype(bf)
    c["c_nm_strict"] = np.tile(np.where(j >= t, NEG, 0.0), (1, 4)).astype(np.float32).astype(bf)
    c["c_nm_win"] = np.tile(np.where(j <= t, NEG, 0.0), (1, 4)).astype(np.float32).astype(bf)
    c["c_m01_strict"] = (j < t).astype(np.float32).astype(bf)
    n = np.arange(128)[:, None]
    tt = np.arange(SEQ)[None, :]
    c["c_nm_cmp"] = np.where(16 * n + 31 > tt, NEG, 0.0).astype(np.float32).astype(bf)
    starts = np.arange(127) * 16
    sel_starts = np.arange(32) * 64
    ovl = ((starts[:, None] < sel_starts[None, :] + 64) & (starts[:, None] + 32 > sel_starts[None, :]))
    vext = np.zeros((128, 33), np.float32)
    vext[:, 0] = 1.0
    vext[:127, 1:] = ovl
    c["c_vext"] = vext.astype(bf)
    onehot = (np.arange(SEQ)[None, :] // 64 == np.arange(32)[:, None]).astype(np.float32)
    c["c_blk1h"] = onehot.astype(bf)
    tq = np.arange(SEQ)
    cur = (tq // 64)[:, None]
    bid = np.arange(32)[None, :]
    forced = (bid == 0) | (bid == cur) | (bid == cur - 1)
    am = np.where(bid <= cur, np.where(forced, BIG, 0.0), -BIG).astype(np.float32)
    c["c_addmask"] = np.ascontiguousarray(am.reshape(16, 128, 32).transpose(1, 0, 2)).astype(bf)
    pl = np.zeros((4, 6, 4, 68), np.float32)
    for h in range(4):
        pl[h, 0, h, 64] = 1.0
        pl[h, 1, h, 65] = 1.0
        pl[0, 2, h, 66] = 1.0
        pl[0, 2, h, 67] = 1.0
        pl[h, 3, h, 66] = 1.0
        pl[h, 4, h, 67] = 1.0
        pl[0, 5, h, 64] = 1.0
        pl[0, 5, h, 65] = 1.0
    c["c_place"] = pl.reshape(4, 6 * 4 * 68).astype(bf)
    rot = np.zeros((128, 16), np.float32)
    for m in range(8):
        rot[m + 8, m] = -1.0
        rot[m, m + 8] = 1.0
    c["c_rot"] = rot.astype(bf)
    return c


_CONST_SPECS = None


class K:
    def __init__(self, nseq, nlayers=DEPTH, phases=("nsa", "sb", "fox", "mem", "ffn"), final=True):
        self.nseq, self.nlayers, self.phases, self.final = nseq, nlayers, phases, final
        nc = self.nc = bass.Bass("TRN2", target_bir_lowering=False)
        self.S = Sched(nc)
        self._n = 0
        self.din = {}
        self.declare_io()
        self.alloc()
        self.load_consts()
        self.prep_all()
        print("sbuf bytes remaining", nc.sbuf_bytes_remaining)
        for s in range(nseq):
            self.run_seq(s)
            self.S.barrier()
        self.finish()

    def dram_in(self, name, shape, dt=F32):
        self.din[name] = self.nc.dram_tensor(name, list(shape), dt, kind="ExternalInput").ap()
        return self.din[name]

    def declare_io(self):
        nc, L = self.nc, DEPTH
        self.dram_in("x", [self.nseq, SEQ, DM])
        self.dram_in("mem", [self.nseq, 256, DM])
        for nm, shp in [("norm_mix", [L, DM]), ("w_in", [L, DM, INC]), ("b_forget", [L, 4]),
                        ("cmp_pe_k", [L, 32, 64]), ("cmp_pe_v", [L, 32, 64]),
                        ("cmp_wk1", [L, 32, 64, 64]), ("cmp_wk2", [L, 64, 64]),
                        ("cmp_wv1", [L, 32, 64, 64]), ("cmp_wv2", [L, 64, 64]),
                        ("w_out", [L, DM, DM]), ("norm_cross", [L, DM]), ("norm_mem", [L, DM]),
                        ("w_mq", [L, DM, 256]), ("w_mk", [L, DM, 256]), ("w_mv", [L, DM, 256]),
                        ("w_mo", [L, 256, DM]), ("norm_ffn", [L, DM]), ("w_up", [L, DM, 2 * DFF]),
                        ("conv_w", [L, 3, 2 * DFF]), ("conv_b", [L, 2 * DFF]), ("w_down", [L, DFF, DM]),
                        ("norm_final", [DM])]:
            self.dram_in(nm, shp)
        for nm, arr in _consts().items():
            self.dram_in(nm, arr.shape, BF16 if arr.dtype == ml_dtypes.bfloat16 else F32)
        self.y = nc.dram_tensor("y", [self.nseq, SEQ, DM], F32, kind="ExternalOutput").ap()
        d = lambda nm, shp: nc.dram_tensor(nm, shp, BF16).ap()
        self.W = []
        for l in range(self.nlayers):
            self.W.append(dict(
                win=d("b_win%d" % l, [DM, INC]), rot=d("b_rot%d" % l, [DM, 224]),
                wout=d("b_wout%d" % l, [DM, DM]), mq=d("b_mq%d" % l, [DM, 256]),
                mk=d("b_mk%d" % l, [DM, 256]), mv=d("b_mv%d" % l, [DM, 256]),
                mo=d("b_mo%d" % l, [256, DM]), up=d("b_up%d" % l, [DM, 2 * DFF]),
                down=d("b_down%d" % l, [DFF, DM]),
                w1k=d("b_w1k%d" % l, [64, 32 * 64]), w1v=d("b_w1v%d" % l, [64, 32 * 64]),
                w2k=d("b_w2k%d" % l, [64, 64]), w2v=d("b_w2v%d" % l, [64, 64]),
                pek=d("b_pek%d" % l, [32, 64]), pev=d("b_pev%d" % l, [32, 64])))
        self.hT_d = nc.dram_tensor("b_hT", [KC, 128, SEQ], BF16).ap()
        self.hT_dbuf = Buf("hT_d", None)
        self.wbuf_d = Buf("wdram", None)

    def sb(self, name, shape, dt=F32):
        return Buf(name, self.nc.alloc_sbuf_tensor(name, list(shape), dt))

    def alloc(self):
        nc = self.nc
        self.xres_t = nc.alloc_sbuf_tensor("xres", [128, NT, DM], F32)
        self.xt = [Buf("x%d" % i, self.xres_t) for i in range(NT)]
        self.psbig = nc.alloc_psum_tensor("psbig", [128, 7 * 512], F32)
        self.ps = [Buf("ps%d" % i, self.psbig[:, i * 512:(i + 1) * 512], excl=True) for i in range(7)]
        self._pair = 0
        self.pst = Buf("pst", nc.alloc_psum_tensor("pst", [128, 1024], BF16), excl=True)
        self._rot = 0
        self._rotbanks = [0, 1, 2]
        self._pending = None
        self._deferred = []
        self.wbuf = [self.sb("wbuf%d" % i, [128, 4096], BF16) for i in range(4)]
        self._wrot = 0
        self.hTb = [self.sb("hTb%d" % i, [128, KC, TCH], BF16) for i in range(2)]
        self._hrot = 0
        self.ARENA = 22528
        self.arena = nc.alloc_sbuf_tensor("arena", [128, self.ARENA], BF16)
        self.PT = [self.sb("PT%d" % i, [128, 1024], BF16) for i in range(3)]
        self._prot = 0
        self.wf = [self.sb("wf%d" % i, [128, 514], F32) for i in range(5)]
        self._frot = 0
        self.h16 = [self.sb("h16_%d" % i, [128, DM], BF16) for i in range(2)]
        self._h16rot = 0
        self.small = [self.sb("sm%d" % i, [128, 64], F32) for i in range(4)]
        self._srot = 0
        self.o16 = [self.sb("o16_%d" % i, [128, 512], BF16) for i in range(2)]
        self._orot = 0
        self.oT = [self.sb("oT%d" % i, [128, 4, 128], BF16) for i in range(2)]
        self._otrot = 0
        self.oaccs = [self.sb("oacc%d" % i, [128, 8, 64], F32) for i in range(2)]
        self.oacc = self.oaccs[0]
        self.vcc = self.sb("vcc", [128, 2, 97], BF16)
        self.nmc = self.sb("nmc", [128, TCH], BF16)
        self.selpad = [self.sb("selpad%d" % g, [128, 96], BF16) for g in range(2)]

    def psum(self):
        rb = self._rotbanks
        self._rot = (self._rot + 1) % len(rb)
        return self.ps[rb[self._rot]]

    def pipe_unit(self, qk, pv):
        pt = qk()
        self.pipe_flush_pv()
        self._run_deferred(False)
        self._pending = (pv, pt)

    def _run_deferred(self, force):
        d, self._deferred = self._deferred, []
        keep = []
        for (f, n) in d:
            if n <= 0 or force:
                f()
            else:
                keep.append((f, n - 1))
        self._deferred = keep + self._deferred

    def pipe_flush_pv(self):
        if self._pending is not None:
            pv, pt = self._pending
            self._pending = None
            pv(pt)

    def pipe_defer(self, fn, delay=0):
        self._deferred.append((fn, delay))

    def pipe_drain(self):
        self.pipe_flush_pv()
        while self._deferred:
            self._run_deferred(True)

    def getw(self):
        b = self.wbuf[self._wrot]
        self._wrot = (self._wrot + 1) % 4
        return b

    def gethT(self):
        b = self.hTb[self._hrot]
        self._hrot = (self._hrot + 1) % 2
        return b

    def getPT(self):
        b = self.PT[self._prot]
        self._prot = (self._prot + 1) % 3
        return b

    def getf(self):
        b = self.wf[self._frot]
        self._frot = (self._frot + 1) % 5
        return b

    def getsm(self):
        b = self.small[self._srot]
        self._srot = (self._srot + 1) % 4
        return b

    def mm(self, ob, out, lb, lhsT, rb, rhs, start, stop, extra=()):
        nc = self.nc
        self.S.op("pe", lambda: nc.tensor.matmul(out, lhsT=lhsT, rhs=rhs, start=start, stop=stop,
                                                 skip_group_check=True),
                  reads=[lb, rb] + list(extra), writes=[ob])

    def act(self, ob, out, ib, in_, func, reads=(), **kw):
        nc = self.nc
        self.S.op("act", lambda: nc.scalar.activation(out=out, in_=in_, func=func, **kw),
                  reads=[ib] + list(reads), writes=[ob] if not isinstance(ob, (list, tuple)) else list(ob))

    def ts(self, eng, ob, out, ib, in0, s1, s2, op0, op1=None, reads=()):
        e = self.S.E[eng]
        if op1 is None:
            fn = lambda: e.tensor_scalar(out=out, in0=in0, scalar1=s1, scalar2=None, op0=op0)
        else:
            fn = lambda: e.tensor_scalar(out=out, in0=in0, scalar1=s1, scalar2=s2, op0=op0, op1=op1)
        self.S.op(eng, fn, reads=[ib] + list(reads), writes=[ob])

    def tt(self, eng, ob, out, ab, a, bb, b, op):
        e = self.S.E[eng]
        self.S.op(eng, lambda: e.tensor_tensor(out=out, in0=a, in1=b, op=op), reads=[ab, bb], writes=[ob])

    def stt(self, ob, out, ab, in0, scalar, bb, in1, op0, op1, reads=()):
        nc = self.nc
        self.S.op("dve", lambda: nc.vector.scalar_tensor_tensor(out=out, in0=in0, scalar=scalar, in1=in1,
                                                                op0=op0, op1=op1),
                  reads=[ab, bb] + list(reads), writes=[ob])

    def cp(self, eng, ob, out, ib, in_):
        if eng == "act":
            nc = self.nc
            self.S.op("act", lambda: nc.scalar.copy(out=out, in_=in_), reads=[ib], writes=[ob])
        else:
            e = self.S.E[eng]
            self.S.op(eng, lambda: e.tensor_copy(out=out, in_=in_), reads=[ib], writes=[ob])

    def memset(self, eng, ob, ap, val):
        e = self.S.E[eng]
        self.S.op(eng, lambda: e.memset(ap, val), writes=[ob])

    def load_consts(self):
        S = self.S
        self.C = {}
        for nm, arr in _consts().items():
            shp = list(arr.shape)
            if nm in ("c_blk1h", "c_nm_cmp"):
                continue
            b = self.sb("k_" + nm, shp, BF16 if arr.dtype == ml_dtypes.bfloat16 else F32)
            S.dma(b[:], self.din[nm], writes=[b])
            self.C[nm] = b
        L = self.nlayers
        self.gain = {}
        grow = Buf("grow", self.arena[0:8, 12288:12288 + 256].bitcast(F32))
        idf = self.C["c_identf"]
        nc = self.nc
        for nm in ("norm_mix", "norm_cross", "norm_mem", "norm_ffn"):
            g = self.sb("g_" + nm, [128, L, KC], F32)
            for l in range(L):
                S.dma(grow[:, :], self.din[nm][l].rearrange("(k p) -> k p", p=128), writes=[grow])
                ps = self.psum()
                S.op("pe", lambda ps=ps: nc.tensor.transpose(out=ps[:, 0:8], in_=grow[:, :], identity=idf[0:8, 0:8]),
                     reads=[grow, idf], writes=[ps])
                self.cp("dve", g, g[:, l, :], ps, ps[:, 0:8])
            self.gain[nm] = g
        g8 = self.sb("g8_mix", [128, L, KC], F32)
        self.ts("dve", g8, g8[:], self.gain["norm_mix"], self.gain["norm_mix"][:], 0.125, None, ALU.mult)
        self.gain["norm_mix8"] = g8
        g8c = self.sb("g8_cross", [128, L, KC], F32)
        self.ts("dve", g8c, g8c[:], self.gain["norm_cross"], self.gain["norm_cross"][:], 0.125, None, ALU.mult)
        self.gain["norm_cross8"] = g8c
        self.nbf = self.sb("nbf", [4, L], F32)
        S.dma(self.nbf[:], self.din["b_forget"][0:L].rearrange("l h -> h l"), writes=[self.nbf],
              allow_slow_non_contiguous=True)
        self.ts("dve", self.nbf, self.nbf[:], self.nbf, self.nbf[:], -1.0, None, ALU.mult)
        self.cw = self.sb("cw", [128, L, 44, 4], F32)
        crow = Buf("crow", self.arena[0:4, 0:4 * DFF].bitcast(F32))
        idf = self.C["c_identf"]
        for l in range(L):
            S.dma(crow[0:3, :], self.din["conv_w"][l], writes=[crow])
            S.dma(crow[3:4, :], self.din["conv_b"][l:l + 1, :], writes=[crow])
            for c0 in range(0, 44, 11):
                ps = self.psum()
                for cc in range(c0, c0 + 11):
                    nc = self.nc
                    S.op("pe", lambda cc=cc, ps=ps: nc.tensor.transpose(
                        out=ps[:, (cc - c0) * 4:(cc - c0) * 4 + 4], in_=crow[0:4, cc * 128:(cc + 1) * 128],
                        identity=idf[0:4, 0:4]), reads=[crow, idf], writes=[ps])
                self.cp("dve", self.cw, self.cw[:, l, c0:c0 + 11, :].rearrange("p c k -> p (c k)"), ps, ps[:, 0:44])
        for g in range(2):
            self.memset("pool", self.selpad[g], self.selpad[g][:], 0.0)
        for g in range(2):
            self.cp("pool", self.vcc, self.vcc[:, g, 64:97], self.C["c_vext"], self.C["c_vext"][:, :])
        self.ones4b = Buf("ones4b", self.C["c_ones"][0:4, :])
        S.barrier()

    def prep_all(self):
        S, nc = self.S, self.nc
        ar = self.arena
        NS = 3
        pin = [Buf("pin%d" % i, ar[:, i * 4096:(i + 1) * 4096].bitcast(F32)) for i in range(NS)]
        pout = [Buf("pout%d" % i, ar[:, 12288 + i * 2048: 12288 + (i + 1) * 2048]) for i in range(NS)]
        rott = Buf("rott", ar[:, 18432:18432 + 224])
        self._pk = 0
        engs = ["act", "dve"]

        def piece(src, dst, rows, cols, scale, after=None):
            i = self._pk % NS
            eng = engs[self._pk % 2]
            self._pk += 1
            S.dma(pin[i][0:rows, 0:cols], src, writes=[pin[i]])
            o, a = pout[i][0:rows, 0:cols], pin[i][0:rows, 0:cols]
            if isinstance(scale, tuple):
                sbuf, sap = scale
                if eng == "act":
                    self.act(pout[i], o, pin[i], a, AF.Copy, reads=[sbuf], scale=sap)
                else:
                    self.ts(eng, pout[i], o, pin[i], a, sap, None, ALU.mult, reads=[sbuf])
            else:
                if eng == "act":
                    self.act(pout[i], o, pin[i], a, AF.Copy, scale=float(scale))
                else:
                    self.ts(eng, pout[i], o, pin[i], a, float(scale), None, ALU.mult)
            if after is not None:
                after(pout[i])
            S.dma(dst, pout[i][0:rows, 0:cols], reads=[pout[i]], q="pool")

        def mat(src, dst, R, Ccols, gain=None, l=0, scale=1.0):
            for r0 in range(0, R, 128):
                rows = min(128, R - r0)
                for c0 in range(0, Ccols, 2048):
                    cols = min(2048, Ccols - c0)
                    sc = (gain, gain[0:rows, l, r0 // 128:r0 // 128 + 1]) if gain is not None else scale
                    piece(src[r0:r0 + rows, c0:c0 + cols], dst[r0:r0 + rows, c0:c0 + cols], rows, cols, sc)

        for l in range(self.nlayers):
            W, D = self.W[l], self.din
            g, g8 = self.gain["norm_mix"], self.gain["norm_mix8"]
            for rc in range(KC):
                r0 = rc * 128
                gs, g8s = (g, g[:, l, rc:rc + 1]), (g8, g8[:, l, rc:rc + 1])

                def rot_ops(src_off, nh, roff):
                    def f(pb):
                        v = pb[:, src_off:src_off + 64 * nh].rearrange("p (h d) -> p h d", d=64)
                        rt = rott[:, :].rearrange("p (h d) -> p h d", d=16)
                        self.ts("dve", rott, rt[:, roff:roff + nh, 0:8], pb, v[:, :, 8:16], -1.0, None, ALU.mult)
                        self.cp("dve", rott, rt[:, roff:roff + nh, 8:16], pb, v[:, :, 0:8])
                    return f

                def rot2(pb):
                    rot_ops(0, 2, 8)(pb)
                    rot_ops(256, 2, 10)(pb)
                    rot_ops(512, 2, 12)(pb)

                segs = [(0, 512, g8s, rot_ops(0, 8, 0)), (512, 1304, gs, rot2), (1304, 1560, g8s, None),
                        (1560, 2072, gs, None), (2072, 2328, g8s, None), (2328, 2844, gs, None)]
                for (a, b, sc, aft) in segs:
                    piece(D["w_in"][l, r0:r0 + 128, a:b], W["win"][r0:r0 + 128, a:b], 128, b - a, sc, aft)
                S.dma(W["rot"][r0:r0 + 128, :], rott[:, :], reads=[rott])
            mat(D["w_out"][l], W["wout"], DM, DM)
            mat(D["w_mq"][l], W["mq"], DM, 256, gain=self.gain["norm_cross8"], l=l)
            mat(D["w_mk"][l], W["mk"], DM, 256, gain=self.gain["norm_mem"], l=l)
            mat(D["w_mv"][l], W["mv"], DM, 256, gain=self.gain["norm_mem"], l=l)
            mat(D["w_mo"][l], W["mo"], 256, DM)
            mat(D["w_up"][l], W["up"], DM, 2 * DFF, gain=self.gain["norm_ffn"], l=l)
            mat(D["w_down"][l], W["down"], DFF, DM)
            for (sn, dn) in (("cmp_wk1", "w1k"), ("cmp_wv1", "w1v")):
                for l0 in range(0, 32, 16):
                    i = self._pk % NS
                    self._pk += 1
                    S.dma(pin[i][0:64, 0:1024].rearrange("p (l c) -> p l c", c=64),
                          D[sn][l, l0:l0 + 16].rearrange("l d c -> d l c"), writes=[pin[i]])
                    self.cp("dve", pout[i], pout[i][0:64, 0:1024], pin[i], pin[i][0:64, 0:1024])
                    S.dma(W[dn][:, l0 * 64:(l0 + 16) * 64], pout[i][0:64, 0:1024], reads=[pout[i]])
            mat(D["cmp_wk2"][l], W["w2k"], 64, 64)
            mat(D["cmp_wv2"][l], W["w2v"], 64, 64)
            mat(D["cmp_pe_k"][l], W["pek"], 32, 64)
            mat(D["cmp_pe_v"][l], W["pev"], 32, 64)
        S.barrier()

    def load_w(self, src, shape_view, nbytes_cols):
        b = self.getw()
        v = b[:, 0:nbytes_cols]
        if shape_view is not None:
            v = v.rearrange(shape_view[0], **shape_view[1])
        self.S.dma(v, src, reads=[self.wbuf_d], writes=[b])
        return b, v

    def load_wcols(self, wd, c0, ncols, rows=DM):
        nk = rows // 128
        return self.load_w(wd[:, c0:c0 + ncols].rearrange("(k p) n -> p k n", p=128),
                           ("p (k n) -> p k n", dict(k=nk)), nk * ncols)

    def rmsnorm_T(self, tiles, hb, hview, col0):
        nc, S = self.nc, self.S
        for i, t in enumerate(tiles):
            xb = self.xt[t]
            xa = self.xres_t[:, t, :]
            sm = self.getsm()
            h = self.h16[self._h16rot]
            self._h16rot ^= 1
            self.act([h, sm], h[:], xb, xa, AF.Square, accum_out=sm[:, 0:1])
            self.act(sm, sm[:, 1:2], sm, sm[:, 0:1], AF.Ln, scale=1.0 / DM, bias=EPS)
            self.act(sm, sm[:, 2:3], sm, sm[:, 1:2], AF.Exp, scale=-0.5)
            self.ts("dve", h, h[:], xb, xa, sm[:, 2:3], None, ALU.mult, reads=[sm])
            for kc in range(KC):
                S.op("pe", lambda kc=kc, h=h: nc.tensor.transpose(
                    out=self.pst[:, kc * 128:(kc + 1) * 128], in_=h[:, kc * 128:(kc + 1) * 128],
                    identity=self.C["c_ident"][:]), reads=[h, self.C["c_ident"]], writes=[self.pst])
            self.cp("act" if i % 2 else "dve", hb, hview[:, :, col0 + i * 128: col0 + (i + 1) * 128],
                    self.pst, self.pst[:, :].rearrange("p (k t) -> p k t", k=KC))

    def projT(self, dstb, dst, wb, wv, c0, M, hb, hv, ncols, evac="act", scale=None):
        ps = self.psum()
        for kc in range(KC):
            self.mm(ps, ps[0:M, 0:ncols], wb, wv[:, kc, c0:c0 + M], hb, hv[:, kc, 0:ncols], kc == 0, kc == KC - 1)
        self.rp_flush()
        if dst is not None:
            if scale is not None:
                self.act(dstb, dst, ps, ps[0:M, 0:ncols], AF.Copy, scale=scale)
            else:
                self.cp(evac, dstb, dst, ps, ps[0:M, 0:ncols])
        return ps

    _rp = None

    def rp_flush(self):
        if self._rp is not None:
            f, self._rp = self._rp, None
            f()

    def rope_rows(self, dstb, dst16, psA, src, K, t0, ncols):
        psB = self.ps[6]
        rot = self.C["c_rot"]
        self.mm(psB, psB[0:16, 0:ncols], rot, rot[0:K, :], dstb, src, True, True)
        cs, sn = self.C["c_cos"], self.C["c_sin"]
        f1, f2 = self.getf(), self.getf()
        self.tt("dve", f1, f1[0:16, 0:ncols], psA, psA[0:16, 0:ncols], cs, cs[:, t0:t0 + ncols], ALU.mult)
        self.tt("dve", f2, f2[0:16, 0:ncols], psB, psB[0:16, 0:ncols], sn, sn[:, t0:t0 + ncols], ALU.mult)
        self.tt("pool", dstb, dst16, f1, f1[0:16, 0:ncols], f2, f2[0:16, 0:ncols], ALU.add)

    def out_proj(self, o16b, ncol_o, wob, wov, tile):
        nc, S = self.nc, self.S
        nk = ncol_o // 128
        for k in range(nk):
            S.op("pe", lambda k=k: nc.tensor.transpose(
                out=self.pst[:, k * 128:(k + 1) * 128], in_=o16b[:, k * 128:(k + 1) * 128],
                identity=self.C["c_ident"][:]), reads=[o16b, self.C["c_ident"]], writes=[self.pst])
        oT = self.oT[self._otrot]
        self._otrot ^= 1
        self.cp("act", oT, oT[:, 0:nk, :], self.pst, self.pst[:, 0:nk * 128].rearrange("p (k t) -> p k t", k=nk))
        xb, xa = self.xt[tile], self.xres_t[:, tile, :]
        for n in range(2):
            ps = self.psum()
            for k in range(nk):
                self.mm(ps, ps[:, :], oT, oT[:, k, :], wob, wov[:, k, n * 512:(n + 1) * 512], k == 0, k == nk - 1)
            self.tt("dve", xb, xa[:, n * 512:(n + 1) * 512], xb, xa[:, n * 512:(n + 1) * 512], ps, ps[:, :], ALU.add)

    def load_hT(self, c):
        hb = self.gethT()
        self.S.dma(hb[:], self.hT_d[:, :, c * TCH:(c + 1) * TCH].rearrange("k p t -> p k t"),
                   reads=[self.hT_dbuf], writes=[hb])
        return hb

    def pvs(self, st, acc, out, ptb, lhsT, vb, rhs):
        self.mm(acc, out, ptb, lhsT, vb, rhs, st["first"], True)
        st["first"] = False

    def run_seq(self, s):
        S = self.S
        for t in range(NT):
            S.dma(self.xres_t[:, t, :], self.din["x"][s, t * 128:(t + 1) * 128, :], writes=[self.xt[t]])
        for l in range(self.nlayers):
            self.l = l
            self.s = s
            if any(p in self.phases for p in ("nsa", "sb", "fox")):
                self.phase_mix()
            if "mem" in self.phases:
                self.phase_mem()
            if "ffn" in self.phases:
                self.phase_ffn()
        if self.final:
            self.gfin = Buf("gfin", self.arena[:, 0:2048].bitcast(F32))
            S.dma(self.gfin[:, :], self.din["norm_final"].partition_broadcast(128), writes=[self.gfin])
        for t in range(NT):
            xb, xa = self.xt[t], self.xres_t[:, t, :]
            if self.final:
                sm = self.getsm()
                h = self.h16[self._h16rot]
                self._h16rot ^= 1
                self.act([h, sm], h[:], xb, xa, AF.Square, accum_out=sm[:, 0:1])
                self.act(sm, sm[:, 1:2], sm, sm[:, 0:1], AF.Ln, scale=1.0 / DM, bias=EPS)
                self.act(sm, sm[:, 2:3], sm, sm[:, 1:2], AF.Exp, scale=-0.5)
                self.stt(xb, xa, xb, xa, sm[:, 2:3], self.gfin, self.gfin[:, :], ALU.mult, ALU.mult, reads=[sm])
            self.S.dma(self.y[s, t * 128:(t + 1) * 128, :], xa, reads=[xb])

    ybuf = Buf("y", None)

    def finish(self):
        S = self.S
        deps = {k: v for k, v in S.cnt.items() if isinstance(k, tuple) and v > 0}
        S._wait("sp", deps)

    def phase_mix(self):
        S, nc, l = self.S, self.nc, self.l
        W = self.W[l]
        ar = self.arena
        A = lambda name, p, a, n: Buf(name, ar[0:p, a:a + n])
        kvc = A("kvc", 64, 0, 8192)
        ksT = A("ksT", 128, 8192, 4096)
        kwT = A("kwT", 128, 12288, 4096)
        vsw = A("vsw", 128, 16384, 4160)
        kcc = A("kcc", 64, 20544, 256)
        gat = Buf("gat", ar[:, 20800:20800 + 768].bitcast(F32))
        kvc_v = kvc[:, :].rearrange("p (j t) -> p j t", j=4)
        ksT_v = ksT[:, :].rearrange("p (g t) -> p g t", g=2)
        kwT_v = kwT[:, :].rearrange("p (g t) -> p g t", g=2)
        vsw_v = vsw[:, :].rearrange("p (t j d) -> p t j d", t=NT, j=4)
        kcc_v = kcc[:, :].rearrange("p (g n) -> p g n", g=2)
        do_nsa = "nsa" in self.phases
        if do_nsa:
            self.memset("pool", ksT, ksT[96:128, :], 0.0)
            self.memset("pool", kwT, kwT[64:128, :], 0.0)
            for g in range(2):
                S.dma(ksT_v[64:96, g, :], self.din["c_blk1h"], writes=[ksT])
            self.memset("pool", vsw, vsw_v[:, :, :, 64:65], 1.0)
        for c in range(NCH):
            hb = self.gethT()
            self.rmsnorm_T(range(4 * c, 4 * c + 4), hb, hb[:], 0)
            S.dma(self.hT_d[:, :, c * TCH:(c + 1) * TCH].rearrange("k p t -> p k t"), hb[:], reads=[hb],
                  writes=[self.hT_dbuf])
            if not do_nsa:
                continue
            wb, wv = self.load_wcols(W["win"], C_KC, 256)
            t0 = c * TCH

            def rp(dstb, dv, g, ps, K):
                self._rp = lambda: self.rope_rows(dstb, dv[0:16, g, t0:t0 + TCH], ps, dv[0:K, g, t0:t0 + TCH], K, t0, TCH)

            for g in range(2):
                ps = self.projT(kvc, kvc_v[0:64, g, t0:t0 + TCH], wb, wv, g * 64, 64, hb, hb[:], TCH)
                rp(kvc, kvc_v, g, ps, 64)
                self.projT(kvc, kvc_v[0:64, 2 + g, t0:t0 + TCH], wb, wv, 128 + g * 64, 64, hb, hb[:], TCH, evac="dve")
            wb, wv = self.load_wcols(W["win"], C_KS, 512)
            for g in range(2):
                ps = self.projT(ksT, ksT_v[0:64, g, t0:t0 + TCH], wb, wv, g * 64, 64, hb, hb[:], TCH)
                rp(ksT, ksT_v, g, ps, 128)
                ps = self.projT(kwT, kwT_v[0:64, g, t0:t0 + TCH], wb, wv, 256 + g * 64, 64, hb, hb[:], TCH)
                rp(kwT, kwT_v, g, ps, 128)
            for tl in range(4):
                t = 4 * c + tl
                ps = self.psum()
                if tl == 1:
                    self.rp_flush()
                for kc in range(KC):
                    self.mm(ps, ps[:, 0:128], hb, hb[:, kc, tl * 128:(tl + 1) * 128], wb, wv[:, kc, 128:256], kc == 0, kc == KC - 1)
                for kc in range(KC):
                    self.mm(ps, ps[:, 128:256], hb, hb[:, kc, tl * 128:(tl + 1) * 128], wb, wv[:, kc, 384:512], kc == 0, kc == KC - 1)
                self.cp("act", vsw, vsw_v[:, t, :, 0:64], ps, ps[:, 0:256].rearrange("p (j d) -> p j d", j=4))
        if do_nsa:
            self.nsa_compress(kvc, kvc_v, kcc, kcc_v)
            S.barrier()
            self.nsa_q(kvc, ksT, ksT_v, kwT, kwT_v, vsw, vsw_v, kcc, kcc_v, gat)
            S.barrier()
        if "sb" in self.phases:
            self.phase_sb()
            S.barrier()
        if "fox" in self.phases:
            self.phase_fox()
            S.barrier()

    def nsa_compress(self, kvc, kvc_v, kcc, kcc_v):
        S, nc, l = self.S, self.nc, self.l
        W = self.W[l]
        wb = self.getw()
        w1 = wb[0:64, 0:4096].rearrange("p (j l c) -> p j l c", j=2, l=32)
        S.dma(w1[:, 0, :, :], W["w1k"].rearrange("p (l c) -> p l c", c=64), reads=[self.wbuf_d], writes=[wb])
        S.dma(w1[:, 1, :, :], W["w1v"].rearrange("p (l c) -> p l c", c=64), reads=[self.wbuf_d], writes=[wb])
        wb2 = self.getw()
        w2 = wb2[0:64, 0:128].rearrange("p (j e) -> p j e", j=2)
        S.dma(w2[:, 0, :], W["w2k"], reads=[self.wbuf_d], writes=[wb2])
        S.dma(w2[:, 1, :], W["w2v"], reads=[self.wbuf_d], writes=[wb2])
        pen = wb2[0:32, 256:384].rearrange("p (j d) -> p j d", j=2)
        S.dma(pen[:, 0, :], W["pek"], reads=[self.wbuf_d], writes=[wb2])
        S.dma(pen[:, 1, :], W["pev"], reads=[self.wbuf_d], writes=[wb2])
        idn = self.C["c_ident"]
        for j in range(2):
            S.op("pe", lambda j=j: nc.tensor.transpose(out=self.pst[0:64, j * 32:(j + 1) * 32], in_=pen[:, j, :],
                                                       identity=idn[0:32, 0:32]), reads=[wb2, idn], writes=[self.pst])
        pet = self.getPT()
        peT = pet[0:64, 0:64].rearrange("p (j l) -> p j l", j=2)
        self.cp("dve", pet, pet[0:64, 0:64], self.pst, self.pst[0:64, 0:64])
        sm = self.getsm()
        for j in range(2):
            ps = self.psum()
            for li in range(32):
                self.mm(ps, ps[0:64, 0:1], wb, w1[:, j, li, :], pet, peT[:, j, li:li + 1], li == 0, li == 31)
            self.cp("dve", sm, sm[0:64, j:j + 1], ps, ps[0:64, 0:1])
        for j in range(2):
            for g in range(2):
                ps = self.psum()
                for li in range(32):
                    self.mm(ps, ps[0:64, 0:127], wb, w1[:, j, li, :], kvc, kvc_v[0:64, 2 * j + g, li:li + 16 * 126 + 1:16],
                            li == 0, li == 31)
                hid = self.getPT()
                self.act(hid, hid[0:64, 0:127], ps, ps[0:64, 0:127], AF.Silu, reads=[sm], bias=sm[0:64, j:j + 1], scale=1.0)
                ps2 = self.psum()
                if j == 0:
                    self.mm(ps2, ps2[0:64, 0:127], wb2, w2[:, 0, :], hid, hid[0:64, 0:127], True, True)
                    self.cp("dve", kcc, kcc_v[0:64, g, 0:127], ps2, ps2[0:64, 0:127])
                else:
                    self.mm(ps2, ps2[0:127, 0:64], hid, hid[0:64, 0:127], wb2, w2[:, 1, :], True, True)
                    self.cp("dve", self.vcc, self.vcc[0:127, g, 0:64], ps2, ps2[0:127, 0:64])

    def nsa_q(self, kvc, ksT, ksT_v, kwT, kwT_v, vsw, vsw_v, kcc, kcc_v, gat):
        S, nc, l = self.S, self.nc, self.l
        W = self.W[l]
        QnT = Buf("QnT", self.arena[0:128, 0:4096])
        Q = QnT[:, :].rearrange("p (g b h q) -> p g b h q", g=2, b=4, h=4)
        self.Qsel = Buf("Qsel", None)
        self.memset("pool", QnT, QnT[64:128, :], 0.0)
        self._rotbanks = [3, 4, 5]
        gat_v = gat[:, :].rearrange("p (t c) -> p t c", c=24)
        idn = self.C["c_ident"]
        wob, wov = None, None
        for c in range(NCH):
            hb = self.load_hT(c)
            wb, wv = self.load_wcols(W["win"], C_QN, 512)
            wgb, wgv = self.load_wcols(W["win"], C_G, 24)
            S.dma(self.nmc[:, :], self.din["c_nm_cmp"][:, c * TCH:(c + 1) * TCH], writes=[self.nmc])
            if wob is None or True:
                wob, wov = self.load_w(W["wout"][0:512, :].rearrange("(k p) n -> p k n", p=128),
                                       ("p (k n) -> p k n", dict(k=4)), 4096)
            t0 = c * TCH
            for hh in range(8):
                g, h = hh // 4, hh % 4
                ps = self.psum()
                for kc in range(KC):
                    self.mm(ps, ps[0:64, :], wb, wv[:, kc, hh * 64:(hh + 1) * 64], hb, hb[:, kc, :], kc == 0, kc == KC - 1)
                self.rp_flush()
                self.cp("act", QnT, Q[0:64, g, :, h, :], ps, ps[0:64, :].rearrange("p (b q) -> p b q", b=4))

                def rq(g=g, h=h, ps=ps):
                    psB = self.ps[6]
                    rot = self.C["c_rot"]
                    self.mm(psB, psB[0:16, :].rearrange("p (b q) -> p b q", b=4), rot, rot[:, :], QnT, Q[0:128, g, :, h, :], True, True)
                    cs, sn = self.C["c_cos"], self.C["c_sin"]
                    f1, f2 = self.getf(), self.getf()
                    self.tt("dve", f1, f1[0:16, 0:TCH], ps, ps[0:16, :], cs, cs[:, t0:t0 + TCH], ALU.mult)
                    self.tt("dve", f2, f2[0:16, 0:TCH], psB, psB[0:16, :], sn, sn[:, t0:t0 + TCH], ALU.mult)
                    self.tt("pool", QnT, Q[0:16, g, :, h, :], f1, f1[0:16, 0:TCH].rearrange("p (b q) -> p b q", b=4),
                            f2, f2[0:16, 0:TCH].rearrange("p (b q) -> p b q", b=4), ALU.add)
                self._rp = rq
            for tl in range(4):
                t = 4 * c + tl
                ps = self.psum()
                if tl == 1:
                    self.rp_flush()
                for kc in range(KC):
                    self.mm(ps, ps[:, 0:24], hb, hb[:, kc, tl * 128:(tl + 1) * 128], wgb, wgv[:, kc, 0:24], kc == 0, kc == KC - 1)
                self.act(gat, gat_v[:, t, :], ps, ps[:, 0:24], AF.Exp, scale=-1.0)
                self.ts("dve", gat, gat_v[:, t, :], gat, gat_v[:, t, :], 1.0, None, ALU.add)
                S.op("dve", lambda t=t: nc.vector.reciprocal(out=gat_v[:, t, :], in_=gat_v[:, t, :]), reads=[gat], writes=[gat])
            for bl in range(4):
                qb = 4 * c + bl
                self.nsa_qblock(qb, bl, QnT, Q, ksT, ksT_v, kwT, kwT_v, vsw, vsw_v, kcc, kcc_v, gat, gat_v)

                def fin(qb=qb, wob=wob, wov=wov, oacc=self.oaccs[qb % 2]):
                    o16 = self.o16[self._orot]
                    self._orot ^= 1
                    self.cp("dve", o16, o16[:, :], oacc, oacc[:, :, :].rearrange("p h d -> p (h d)"))
                    self.out_proj(o16, 512, wob, wov, qb)
                self.pipe_defer(fin, delay=4)
            self.pipe_drain()
        self._rotbanks = [0, 1, 2]

    def nsa_qblock(self, qb, bl, QnT, Q, ksT, ksT_v, kwT, kwT_v, vsw, vsw_v, kcc, kcc_v, gat, gat_v):
        S, nc = self.S, self.nc
        idn = self.C["c_ident"]
        oacc = self.oaccs[qb % 2]
        gvs = [gat_v[:, qb, g * 12:(g + 1) * 12].rearrange("p (h k) -> p h k", k=3) for g in range(2)]
        q64s = [Q[0:64, g, bl, :, :].rearrange("p h q -> p (h q)") for g in range(2)]
        q128s = [Q[0:128, g, bl, :, :].rearrange("p h q -> p (h q)") for g in range(2)]
        for g in range(2):
            acc = self.ps[0]
            st = {"first": True}

            def qk(g=g):
                ps = self.psum()
                self.mm(ps, ps[0:127, :], kcc, kcc_v[0:64, g, 0:127], QnT, q64s[g], True, False)
                nmc = self.nmc
                self.mm(ps, ps[0:127, :].rearrange("p (h q) -> p h q", h=4), idn, idn[0:127, 0:127], nmc,
                        nmc[0:127, bl * 128:(bl + 1) * 128].unsqueeze(1).broadcast_to([127, 4, 128]), False, True)
                pt = self.getPT()
                self.act(pt, pt[0:127, 0:512], ps, ps[0:127, :], AF.Exp)
                return pt

            def pv(pt, g=g, acc=acc, st=st):
                for h in range(4):
                    self.pvs(st, acc, acc[:, h * 97:(h + 1) * 97], pt, pt[0:127, h * 128:(h + 1) * 128],
                             self.vcc, self.vcc[0:127, g, :])

            def epi(g=g, acc=acc):
                gv = gvs[g]
                accv = acc[:, 0:388].rearrange("p (h d) -> p h d", h=4)
                sm = self.getsm()
                self.ts("dve", sm, sm[:, 0:4], acc, accv[:, :, 64], 1e-30, None, ALU.max)
                S.op("dve", lambda sm=sm: nc.vector.reciprocal(out=sm[:, 4:8], in_=sm[:, 0:4]), reads=[sm], writes=[sm])
                self.tt("dve", sm, sm[:, 8:12], sm, sm[:, 4:8], gat, gv[:, :, 0], ALU.mult)
                f = self.getf()
                fv = f[:, 0:128].rearrange("p (h j) -> p h j", h=4)
                self.tt("dve", f, fv, acc, accv[:, :, 65:97], sm, sm[:, 4:8].unsqueeze(2).broadcast_to([128, 4, 32]), ALU.mult)
                ov = oacc[:, g * 4:(g + 1) * 4, :]
                self.tt("dve", oacc, ov, acc, accv[:, :, 0:64], sm, sm[:, 8:12].unsqueeze(2).broadcast_to([128, 4, 64]), ALU.mult)
                imp = f[:, 128:160]
                self.tt("dve", f, imp, f, fv[:, 0, :], f, fv[:, 1, :], ALU.add)
                self.tt("dve", f, f[:, 160:192], f, fv[:, 2, :], f, fv[:, 3, :], ALU.add)
                self.tt("dve", f, imp, f, imp, f, f[:, 160:192], ALU.add)
                am = self.C["c_addmask"]
                self.tt("dve", f, imp, f, imp, am, am[:, qb, :], ALU.add)
                S.op("dve", lambda f=f: nc.vector.max(out=f[:, 192:200], in_=f[:, 128:160]), reads=[f], writes=[f])
                S.op("dve", lambda f=f: nc.vector.match_replace(out=f[:, 200:232], in_to_replace=f[:, 192:200],
                                                               in_values=f[:, 128:160], imm_value=-3.0e38), reads=[f], writes=[f])
                S.op("dve", lambda f=f: nc.vector.max(out=f[:, 232:240], in_=f[:, 200:232]), reads=[f], writes=[f])
                sp_ = self.selpad[g]
                self.ts("dve", sp_, sp_[:, 64:96], f, imp, f[:, 239:240], NEG, ALU.is_lt, ALU.mult)

            self.pipe_unit(qk, pv)
            self.pipe_defer(epi)

        def epi_b(g):
            sp_ = self.selpad[g]
            ps = self.psum()
            self.mm(ps, ps[0:96, 0:128], sp_, sp_[:, :], idn, idn[:, :], True, True)
            self.cp("act", self.Qsel, Q[64:96, g, bl, :, :], ps, ps[64:96, 0:128].unsqueeze(1).broadcast_to([32, 4, 128]))
        def branch(g, kbs, acc, kTb, kT_v, vcol, maskfn, extra):
            st = {"first": True}
            groups = [kbs[i:i + 2] for i in range(0, len(kbs), 2)]
            for grp in groups:
                def qk(grp=grp):
                    base = 3 + 2 * self._pair
                    self._pair ^= 1
                    banks = [self.ps[base], self.ps[base + 1]]
                    for j, kb in enumerate(grp):
                        ps = banks[j]
                        nm = maskfn(kb)
                        self.mm(ps, ps[:, :], kTb, kT_v[0:128, g, kb * 128:(kb + 1) * 128], QnT, q128s[g], True, nm is None,
                                extra=extra)
                        if nm is not None:
                            self.mm(ps, ps[:, :], idn, idn[:, :], nm, nm[:, :], False, True)
                    pt = self.getPT()
                    w = 512 * len(grp)
                    self.act(pt, pt[:, 0:w], banks[0], self.psbig[:, base * 512:base * 512 + w], AF.Exp,
                             reads=[banks[1]] if len(grp) == 2 else [])
                    return pt

                def pv(pt, grp=grp):
                    for j, kb in enumerate(grp):
                        for h in range(4):
                            self.pvs(st, acc, acc[:, h * 65:(h + 1) * 65], pt, pt[:, j * 512 + h * 128:j * 512 + (h + 1) * 128],
                                     vsw, vsw_v[:, kb, vcol, :])

                self.pipe_unit(qk, pv)

        for g in range(2):
            def wmask(kb):
                d = qb - kb
                return self.C["c_nm_incl"] if d == 0 else (self.C["c_nm_win"] if d == 4 else None)
            branch(g, list(range(max(0, qb - 4), qb + 1)), self.ps[1], kwT, kwT_v, 2 + g, wmask, [])
            self.pipe_defer(lambda g=g: self.nsa_accum(self.ps[1], gat, gvs[g], 2, g, oacc))
        for g in range(2):
            epi_b(g)
        for g in range(2):
            def smask(kb):
                return self.C["c_nm_incl"] if kb == qb else None
            branch(g, list(range(qb + 1)), self.ps[2], ksT, ksT_v, g, smask, [self.Qsel])
            self.pipe_defer(lambda g=g: self.nsa_accum(self.ps[2], gat, gvs[g], 1, g, oacc))

    def nsa_accum(self, acc, gat, gv, k, g, oacc):
        S, nc = self.S, self.nc
        accv = acc[:, 0:260].rearrange("p (h d) -> p h d", h=4)
        sm = self.getsm()
        S.op("dve", lambda: nc.vector.reciprocal(out=sm[:, 0:4], in_=accv[:, :, 64]), reads=[acc], writes=[sm])
        self.tt("dve", sm, sm[:, 4:8], sm, sm[:, 0:4], gat, gv[:, :, k], ALU.mult)
        f = self.getf()
        fv = f[:, 0:256].rearrange("p (h d) -> p h d", h=4)
        self.tt("dve", f, fv, acc, accv[:, :, 0:64], sm, sm[:, 4:8].unsqueeze(2).broadcast_to([128, 4, 64]), ALU.mult)
        ov = oacc[:, g * 4:(g + 1) * 4, :]
        self.tt("dve", oacc, ov, oacc, ov, f, fv, ALU.add)

    def phase_sb(self):
        S, nc, l = self.S, self.nc, self.l
        W = self.W[l]
        ar = self.arena
        kT = Buf("sbk", ar[0:128, 0:8192])
        kT_v = kT[:, :].rearrange("p (h t) -> p h t", h=4)
        vb = Buf("sbv", ar[:, 8192:8192 + 4096])
        v_v = vb[:, :].rearrange("p (t c) -> p t c", t=NT)
        qT = Buf("sbq", ar[0:128, 12288:12288 + 2048])
        q_v = qT[:, :].rearrange("p (h t) -> p h t", h=4)
        self.memset("pool", kT, kT[64:128, :], 0.0)
        self.memset("pool", qT, qT[64:128, :], 0.0)
        lacc = Buf("lacc", ar[:, 14336:14336 + 1024].bitcast(F32))
        lacc16 = [Buf("lacc16_%d" % i, ar[:, 15360 + i * 512:15360 + (i + 1) * 512]) for i in range(3)]
        l16 = [Buf("l16_%d" % i, ar[:, 16896 + i * 512:16896 + (i + 1) * 512]) for i in range(2)]
        idn, tri, ones = self.C["c_ident"], self.C["c_tri"], self.C["c_ones"]
        self._rotbanks = [0, 1, 2, 5, 6]
        for c in range(NCH):
            hb = self.load_hT(c)
            wb, wv = self.load_wcols(W["win"], C_KSB, 512)
            for h in range(4):
                self.projT(kT, kT_v[0:64, h, c * TCH:(c + 1) * TCH], wb, wv, h * 64, 64, hb, hb[:], TCH,
                           evac="act" if h % 2 else "dve")
            for tl in range(4):
                ps = self.psum()
                for kc in range(KC):
                    self.mm(ps, ps[:, 0:256], hb, hb[:, kc, tl * 128:(tl + 1) * 128], wb, wv[:, kc, 256:512], kc == 0, kc == KC - 1)
                self.cp("act", vb, v_v[:, 4 * c + tl, :], ps, ps[:, 0:256])
        for c in range(NCH):
            hb = self.load_hT(c)
            wb, wv = self.load_wcols(W["win"], C_QS, 256)
            wob, wov = self.load_w(W["wout"][512:768, :].rearrange("(k p) n -> p k n", p=128),
                                   ("p (k n) -> p k n", dict(k=2)), 2048)
            for h in range(4):
                self.projT(qT, q_v[0:64, h, :], wb, wv, h * 64, 64, hb, hb[:], TCH, evac="act" if h % 2 else "dve")
            o16s = [self.sb_dummy(i) for i in range(4)]
            units = []
            for h in range(4):
                kbs = list(range(4 * c + 3, -1, -1))
                for j, kb in enumerate(kbs):
                    units.append(dict(h=h, kb=kb, first=j == 0, last=j == len(kbs) - 1,
                                      off=max(0, (kb - 4 * c) * 128), diag=kb >= 4 * c,
                                      acc=self.ps[3 + (h % 2)], st=None))
            sts = {}
            n = len(units)

            def stageA(i):
                u = units[i]
                h, kb, off = u["h"], u["kb"], u["off"]
                if u["first"]:
                    self.memset("pool", lacc, lacc[:, :], 0.0)
                    sts[h] = {"first": True}
                ks = kT_v[0:128, h, kb * 128:(kb + 1) * 128]
                ps1 = self.psum()
                self.mm(ps1, ps1[:, off:512], kT, ks, qT, q_v[0:128, h, off:512], True, True)
                sp = self.getf()
                self.act(sp, sp[:, off:512], ps1, ps1[:, off:512], AF.Exp, scale=-1.0)
                self.act(sp, sp[:, off:512], sp, sp[:, off:512], AF.Ln, bias=1.0, scale=1.0)
                lb = l16[i % 2]
                self.stt(lb, lb[:, off:512], ps1, ps1[:, off:512], -1.0, sp, sp[:, off:512], ALU.mult, ALU.subtract)
                if u["diag"]:
                    m01 = self.C["c_m01_strict"]
                    self.tt("pool", lb, lb[:, off:off + 128], lb, lb[:, off:off + 128], m01, m01[:, :], ALU.mult)
                if not u["last"]:
                    self.tt("dve", lacc, lacc[:, off:512], lacc, lacc[:, off:512], lb, lb[:, off:512], ALU.add)
                    la = lacc16[i % 3]
                    self.cp("dve", la, la[:, :], lacc, lacc[:, :])

            def stageB(i):
                u = units[i]
                h, kb, off = u["h"], u["kb"], u["off"]
                ks = kT_v[0:128, h, kb * 128:(kb + 1) * 128]
                lb = l16[i % 2]
                ps2 = self.psum()
                grp = [(ps2[:, off:512], kT, ks, qT, q_v[0:128, h, off:512]),
                       (ps2[:, off:512], tri, tri[:, :], lb, lb[:, off:512])]
                if not u["first"]:
                    la = lacc16[(i - 1) % 3]
                    grp.append((ps2[:, off:512], ones, ones[:, 0:128], la, la[:, off:512]))
                if u["diag"]:
                    nm = self.C["c_nm_strict"]
                    grp.append((ps2[:, off:off + 128], idn, idn[:, :], nm, nm[:, 0:128]))
                for gi, (o_, lb_, l_, rb_, r_) in enumerate(grp):
                    self.mm(ps2, o_, lb_, l_, rb_, r_, gi == 0, gi == len(grp) - 1)
                pt = self.getPT()
                self.act(pt, pt[:, off:512], ps2, ps2[:, off:512], AF.Exp)
                u["pt"] = pt

            def stageC(i):
                u = units[i]
                h, kb, off, acc, pt = u["h"], u["kb"], u["off"], u["acc"], u["pt"]
                for qbl in range(off // 128, 4):
                    self.pvs(sts[h], acc, acc[:, qbl * 64:(qbl + 1) * 64], pt, pt[:, qbl * 128:(qbl + 1) * 128],
                             vb, v_v[:, kb, h * 64:(h + 1) * 64])
                if u["last"]:
                    for qbl in range(4):
                        self.cp("dve", o16s[qbl], o16s[qbl][:, h * 64:(h + 1) * 64], acc, acc[:, qbl * 64:(qbl + 1) * 64])

            for i in range(n + 2):
                if i < n:
                    stageA(i)
                if 0 <= i - 1 < n:
                    stageB(i - 1)
                if 0 <= i - 2 < n:
                    stageC(i - 2)
            for qbl in range(4):
                self.out_proj(o16s[qbl], 256, wob, wov, 4 * c + qbl)
        self._rotbanks = [0, 1, 2]

    def sb_dummy(self, i):
        if not hasattr(self, "_o4"):
            self._o4 = [Buf("o4_%d" % j, self.o16[j // 2][:, (j % 2) * 256:(j % 2 + 1) * 256]) for j in range(4)]
        return self._o4[i]

    def phase_fox(self):
        S, nc, l = self.S, self.nc, self.l
        W = self.W[l]
        ar = self.arena
        kT = Buf("fxk", ar[0:128, 0:8192])
        kT_v = kT[:, :].rearrange("p (h t) -> p h t", h=4)
        self.memset("pool", kT, kT[64:128, :], 0.0)
        vb = Buf("fxv", ar[:, 8192:8192 + 4160])
        v_v = vb[:, :].rearrange("p (t h d) -> p t h d", t=NT, h=4)
        qT = Buf("fxq", ar[0:128, 12352:12352 + 2048])
        q_v = qT[:, :].rearrange("p (h t) -> p h t", h=4)
        self.memset("pool", qT, qT[64:128, :], 0.0)
        csp = Buf("csp", ar[0:4, 14400:14400 + 4096].bitcast(F32))
        hi = Buf("hi", ar[0:4, 18496:18496 + 2048])
        nhi = Buf("nhi", ar[0:4, 22016:22016 + 512])
        idn = self.C["c_ident"]
        place = self.C["c_place"][:, :].rearrange("p (k h m) -> p k h m", k=6, h=4)
        plb = self.C["c_place"]
        self.memset("pool", vb, v_v[:, :, :, 64:65], 1.0)
        self._rotbanks = [0, 1, 2, 5, 6]
        for c in range(NCH):
            hb = self.load_hT(c)
            wb, wv = self.load_wcols(W["win"], C_KF, 512)
            wfb, wfv = self.load_wcols(W["win"], C_FL, 4)
            t0 = c * TCH
            for h in range(4):
                self.projT(kT, kT_v[0:64, h, t0:t0 + TCH], wb, wv, h * 64, 64, hb, hb[:], TCH,
                           evac="act" if h % 2 else "dve")
            for tl in range(4):
                ps = self.psum()
                for kc in range(KC):
                    self.mm(ps, ps[:, 0:256], hb, hb[:, kc, tl * 128:(tl + 1) * 128], wb, wv[:, kc, 256:512], kc == 0, kc == KC - 1)
                self.cp("act", vb, v_v[:, 4 * c + tl, :, 0:64], ps, ps[:, 0:256].rearrange("p (h d) -> p h d", h=4))
            ps = self.psum()
            for kc in range(KC):
                self.mm(ps, ps[0:4, :], wfb, wfv[:, kc, 0:4], hb, hb[:, kc, :], kc == 0, kc == KC - 1)
            e, sp = self.getf(), self.getf()
            self.act(e, e[0:4, 0:TCH], ps, ps[0:4, :], AF.Exp, reads=[self.nbf], scale=-1.0, bias=self.nbf[:, self.l:self.l + 1])
            self.act(sp, sp[0:4, 0:TCH], e, e[0:4, 0:TCH], AF.Ln, bias=1.0, scale=1.0)
            init = 0.0 if c == 0 else csp[:, t0 - 1:t0]
            S.op("dve", lambda init=init, sp=sp, t0=t0: nc.vector.tensor_tensor_scan(
                out=csp[:, t0:t0 + TCH], data0=self.ones4b[:, :], data1=sp[0:4, 0:TCH], initial=init,
                op0=ALU.mult, op1=ALU.add), reads=[self.ones4b, sp, csp], writes=[csp])
            self.cp("dve", hi, hi[:, t0:t0 + TCH], csp, csp[:, t0:t0 + TCH])
            f = self.getf()
            self.tt("dve", f, f[0:4, 0:TCH], csp, csp[:, t0:t0 + TCH], hi, hi[:, t0:t0 + TCH], ALU.subtract)
            lo16 = self.getPT()
            self.cp("dve", lo16, lo16[0:4, 0:TCH], f, f[0:4, 0:TCH])
            for h in range(4):
                ps = self.psum()
                self.mm(ps, ps[0:68, :], plb, place[:, 3, h, :], hi, hi[:, t0:t0 + TCH], True, False)
                self.mm(ps, ps[0:68, :], plb, place[:, 4, h, :], lo16, lo16[0:4, 0:TCH], False, False)
                self.mm(ps, ps[0:68, :], plb, place[:, 5, h, :], self.ones4b, self.ones4b[:, :], False, True)
                self.cp("act", kT, kT_v[64:68, h, t0:t0 + TCH], ps, ps[64:68, :])
        for c in range(NCH):
            hb = self.load_hT(c)
            t0 = c * TCH
            wb, wv = self.load_wcols(W["win"], C_QF, 256)
            wob, wov = self.load_w(W["wout"][768:1024, :].rearrange("(k p) n -> p k n", p=128),
                                   ("p (k n) -> p k n", dict(k=2)), 2048)
            for h in range(4):
                self.projT(qT, q_v[0:64, h, :], wb, wv, h * 64, 64, hb, hb[:], TCH, evac="act" if h % 2 else "dve")
            self.ts("dve", nhi, nhi[:, 0:TCH], hi, hi[:, t0:t0 + TCH], -1.0, None, ALU.mult)
            f = self.getf()
            self.tt("dve", f, f[0:4, 0:TCH], hi, hi[:, t0:t0 + TCH], csp, csp[:, t0:t0 + TCH], ALU.subtract)
            nlo = self.getPT()
            self.cp("dve", nlo, nlo[0:4, 0:TCH], f, f[0:4, 0:TCH])
            for h in range(4):
                ps = self.psum()
                self.mm(ps, ps[0:68, :], plb, place[:, 0, h, :], nhi, nhi[:, 0:TCH], True, False)
                self.mm(ps, ps[0:68, :], plb, place[:, 1, h, :], nlo, nlo[0:4, 0:TCH], False, False)
                self.mm(ps, ps[0:68, :], plb, place[:, 2, h, :], self.ones4b, self.ones4b[:, :], False, True)
                self.cp("act", qT, q_v[64:68, h, :], ps, ps[64:68, :])
            o16s = [self.sb_dummy(i) for i in range(4)]
            for h in range(4):
                acc = self.ps[3 + (h % 2)]
                st = {"first": True}
                for kb in range(4 * c + 3, -1, -1):
                    off = max(0, (kb - 4 * c) * 128)
                    diag = kb >= 4 * c

                    def qk(h=h, kb=kb, off=off, diag=diag):
                        ps = self.psum()
                        self.mm(ps, ps[:, off:512], kT, kT_v[0:128, h, kb * 128:(kb + 1) * 128], qT, q_v[0:128, h, off:512], True, not diag)
                        if diag:
                            nm = self.C["c_nm_incl"]
                            self.mm(ps, ps[:, off:off + 128], idn, idn[:, :], nm, nm[:, 0:128], False, True)
                        pt = self.getPT()
                        self.act(pt, pt[:, off:512], ps, ps[:, off:512], AF.Exp)
                        return pt

                    def pv(pt, h=h, kb=kb, off=off, acc=acc, st=st):
                        for qbl in range(off // 128, 4):
                            self.pvs(st, acc, acc[:, qbl * 65:(qbl + 1) * 65], pt, pt[:, qbl * 128:(qbl + 1) * 128],
                                     vb, v_v[:, kb, h, :])

                    self.pipe_unit(qk, pv)

                def epi(h=h, acc=acc):
                    accv = acc[:, 0:260].rearrange("p (b d) -> p b d", b=4)
                    sm = self.getsm()
                    S.op("dve", lambda: nc.vector.reciprocal(out=sm[:, 0:4], in_=accv[:, :, 64]), reads=[acc], writes=[sm])
                    for qbl in range(4):
                        self.ts("dve", o16s[qbl], o16s[qbl][:, h * 64:(h + 1) * 64], acc, accv[:, qbl, 0:64],
                                sm[:, qbl:qbl + 1], None, ALU.mult, reads=[sm])
                self.pipe_defer(epi)
            self.pipe_drain()
            for qbl in range(4):
                self.out_proj(o16s[qbl], 256, wob, wov, 4 * c + qbl)
        self._rotbanks = [0, 1, 2]

    def phase_mem(self):
        S, nc, l = self.S, self.nc, self.l
        W = self.W[l]
        ar = self.arena
        mx = Buf("mx", ar[:, 0:4096].bitcast(F32))
        mx_v = mx[:, :].rearrange("p (t d) -> p t d", t=2)
        mT = Buf("mT", ar[:, 4096:4096 + 2048])
        mT_v = mT[:, :].rearrange("p (k t) -> p k t", k=KC)
        kT = Buf("mk", ar[0:128, 6144:6144 + 1024])
        kT_v = kT[:, :].rearrange("p (h t) -> p h t", h=4)
        self.memset("pool", kT, kT[64:128, :], 0.0)
        vb = Buf("mv", ar[:, 7168:7168 + 520])
        v_v = vb[:, :].rearrange("p (t h d) -> p t h d", t=2, h=4)
        qT = Buf("mq", ar[0:128, 7688:7688 + 2048])
        q_v = qT[:, :].rearrange("p (h t) -> p h t", h=4)
        self.memset("pool", qT, qT[64:128, :], 0.0)
        idn = self.C["c_ident"]
        for t in range(2):
            S.dma(mx_v[:, t, :], self.din["mem"][self.s, t * 128:(t + 1) * 128, :], writes=[mx])
        self.memset("pool", vb, v_v[:, :, :, 64:65], 1.0)
        for t in range(2):
            sm = self.getsm()
            h = self.h16[self._h16rot]
            self._h16rot ^= 1
            self.act([h, sm], h[:], mx, mx_v[:, t, :], AF.Square, accum_out=sm[:, 0:1])
            self.act(sm, sm[:, 1:2], sm, sm[:, 0:1], AF.Ln, scale=1.0 / DM, bias=EPS)
            self.act(sm, sm[:, 2:3], sm, sm[:, 1:2], AF.Exp, scale=-0.5)
            self.ts("dve", h, h[:], mx, mx_v[:, t, :], sm[:, 2:3], None, ALU.mult, reads=[sm])
            for kc in range(KC):
                S.op("pe", lambda kc=kc, h=h: nc.tensor.transpose(
                    out=self.pst[:, kc * 128:(kc + 1) * 128], in_=h[:, kc * 128:(kc + 1) * 128],
                    identity=idn[:]), reads=[h, idn], writes=[self.pst])
            self.cp("dve", mT, mT_v[:, :, t * 128:(t + 1) * 128], self.pst, self.pst[:, :].rearrange("p (k t) -> p k t", k=KC))
        wb, wv = self.load_wcols(W["mk"], 0, 256)
        for h in range(4):
            self.projT(kT, kT_v[0:64, h, :], wb, wv, h * 64, 64, mT, mT_v, 256)
        wb, wv = self.load_wcols(W["mv"], 0, 256)
        for t in range(2):
            ps = self.psum()
            for kc in range(KC):
                self.mm(ps, ps[:, 0:256], mT, mT_v[:, kc, t * 128:(t + 1) * 128], wb, wv[:, kc, :], kc == 0, kc == KC - 1)
            self.cp("act", vb, v_v[:, t, :, 0:64], ps, ps[:, 0:256].rearrange("p (h d) -> p h d", h=4))
        wqb, wqv = self.load_wcols(W["mq"], 0, 256)
        wob, wov = self.load_w(W["mo"].rearrange("(k p) n -> p k n", p=128), ("p (k n) -> p k n", dict(k=2)), 2048)
        for c in range(NCH):
            hb = self.gethT()
            self.rmsnorm_T(range(4 * c, 4 * c + 4), hb, hb[:], 0)
            for h in range(4):
                self.projT(qT, q_v[0:64, h, :], wqb, wqv, h * 64, 64, hb, hb[:], TCH, evac="act" if h % 2 else "dve")
            o16s = [self.sb_dummy(i) for i in range(4)]
            for h in range(4):
                acc = self.ps[3 + (h % 2)]
                st = {"first": True}
                for kb in range(2):
                    def qk(h=h, kb=kb):
                        ps = self.psum()
                        self.mm(ps, ps[:, :], kT, kT_v[0:128, h, kb * 128:(kb + 1) * 128], qT, q_v[0:128, h, :], True, True)
                        pt = self.getPT()
                        self.act(pt, pt[:, 0:512], ps, ps[:, :], AF.Exp)
                        return pt

                    def pv(pt, h=h, kb=kb, acc=acc, st=st):
                        for qbl in range(4):
                            self.pvs(st, acc, acc[:, qbl * 65:(qbl + 1) * 65], pt, pt[:, qbl * 128:(qbl + 1) * 128], vb, v_v[:, kb, h, :])

                    self.pipe_unit(qk, pv)

                def epi(h=h, acc=acc):
                    accv = acc[:, 0:260].rearrange("p (b d) -> p b d", b=4)
                    sm = self.getsm()
                    S.op("dve", lambda: nc.vector.reciprocal(out=sm[:, 0:4], in_=accv[:, :, 64]), reads=[acc], writes=[sm])
                    for qbl in range(4):
                        self.ts("dve", o16s[qbl], o16s[qbl][:, h * 64:(h + 1) * 64], acc, accv[:, qbl, 0:64],
                                sm[:, qbl:qbl + 1], None, ALU.mult, reads=[sm])
                self.pipe_defer(epi)
            self.pipe_drain()
            for qbl in range(4):
                self.out_proj(o16s[qbl], 256, wob, wov, 4 * c + qbl)
        S.barrier()

    def phase_ffn(self):
        S, nc, l = self.S, self.nc, self.l
        W = self.W[l]
        ar = self.arena
        gT = Buf("gT", ar[:, 0:11264])
        g_v = gT[:, :].rearrange("p (k t) -> p k t", k=22)
        halo = Buf("halo", ar[:, 11264:11264 + 176].bitcast(F32))
        halo_v = halo[:, :].rearrange("p (c k) -> p c k", k=2)
        self.memset("pool", halo, halo[:, :], 0.0)
        cw = self.cw
        for c in range(NCH):
            hb = self.gethT()
            self.rmsnorm_T(range(4 * c, 4 * c + 4), hb, hb[:], 0)
            wcur = {}
            uy = {}

            def stage1(cc):
                cg, ci = cc // 4, cc % 4
                if ci == 0:
                    wcur["w"] = self.load_wcols(W["up"], cg * 512, 512)
                wb, wv = wcur["w"]
                ps = self.psum()
                for kc in range(KC):
                    self.mm(ps, ps[:, :], wb, wv[:, kc, ci * 128:(ci + 1) * 128], hb, hb[:, kc, :], kc == 0, kc == KC - 1)
                u = self.getf()
                y = self.getf()
                self.cp("pool", u, u[:, 0:2], halo, halo_v[:, cc, :])
                self.cp("act", u, u[:, 2:514], ps, ps[:, :])
                self.act(y, y[:, 0:512], ps, ps[:, :], AF.Copy, reads=[cw], scale=cw[:, l, cc, 2:3])
                self.cp("pool", halo, halo_v[:, cc, :], u, u[:, 512:514])
                uy[cc] = [u, y]

            def stage2(cc):
                u, y = uy[cc]
                self.stt(y, y[:, 0:512], u, u[:, 1:513], cw[:, l, cc, 1:2], y, y[:, 0:512], ALU.mult, ALU.add, reads=[cw])
                self.stt(y, y[:, 0:512], u, u[:, 0:512], cw[:, l, cc, 0:1], y, y[:, 0:512], ALU.mult, ALU.add, reads=[cw])

            def stage3(cc):
                y = uy.pop(cc)[1]
                if cc < 22:
                    self.act(gT, g_v[:, cc, :], y, y[:, 0:512], AF.Silu, reads=[cw], bias=cw[:, l, cc, 3:4], scale=1.0)
                else:
                    self.stt(gT, g_v[:, cc - 22, :], y, y[:, 0:512], cw[:, l, cc, 3:4], gT, g_v[:, cc - 22, :],
                             ALU.add, ALU.mult, reads=[cw])

            for i in range(44 + 2):
                if i < 44:
                    stage1(i)
                if 0 <= i - 1 < 44:
                    stage2(i - 1)
                if 0 <= i - 2 < 44:
                    stage3(i - 2)
            for n in range(2):
                accs = [self.ps[3 + tl] for tl in range(4)]
                for pc in range(6):
                    k0 = pc * 4
                    nk = min(4, 22 - k0)
                    wdb, wdv = self.load_w(W["down"][k0 * 128:(k0 + nk) * 128, n * 512:(n + 1) * 512].rearrange("(k p) n -> p k n", p=128),
                                           ("p (k n) -> p k n", dict(k=nk)), nk * 512)
                    for tl in range(4):
                        for k in range(nk):
                            self.mm(accs[tl], accs[tl][:, :], gT, g_v[:, k0 + k, tl * 128:(tl + 1) * 128], wdb, wdv[:, k, :],
                                    k0 + k == 0, k0 + k == 21)
                for tl in range(4):
                    t = 4 * c + tl
                    xb, xa = self.xt[t], self.xres_t[:, t, :]
                    self.tt("dve", xb, xa[:, n * 512:(n + 1) * 512], xb, xa[:, n * 512:(n + 1) * 512], accs[tl], accs[tl][:, :], ALU.add)
        S.barrier()


_PROG = {}


def _get_prog(nseq):
    if nseq not in _PROG:
        _PROG[nseq] = K(nseq)
    return _PROG[nseq]


def kernel(**inputs):
    x = np.ascontiguousarray(np.asarray(inputs["x"], dtype=np.float32))
    mem = np.ascontiguousarray(np.asarray(inputs["mem"], dtype=np.float32))
    B = x.shape[0]
    ncores = 8
    per = B // ncores
    prog = _get_prog(per)
    consts = _consts()
    in_maps = []
    for i in range(ncores):
        m = {"x": x[i * per:(i + 1) * per], "mem": mem[i * per:(i + 1) * per]}
        for k, v in inputs.items():
            if k not in ("x", "mem"):
                m[k] = np.ascontiguousarray(np.asarray(v, dtype=np.float32))
        m.update(consts)
        in_maps.append(m)
    res = run_bass_kernel_spmd(prog.nc, in_maps, core_ids=list(range(ncores)))
    return np.concatenate([np.asarray(r["y"], dtype=np.float32) for r in res.results], axis=0)
```

```python
import numpy as np
import ml_dtypes
import concourse.bass as bass
import concourse.mybir as mybir
from concourse.bass_utils import run_bass_kernel_spmd

F32 = mybir.dt.float32
BF16 = mybir.dt.bfloat16
AF = mybir.ActivationFunctionType
ALU = mybir.AluOpType

SEQ, DM, KC, NT, TCH, NCH = 2048, 1024, 8, 16, 512, 4
DEPTH = 4
DFF = 2816
NEG = -30000.0
BIG = 1.0e30
EPS = 1e-6
C_QN, C_KC, C_VC, C_KS, C_VS, C_KW, C_VW, C_G = 0, 512, 640, 768, 896, 1024, 1152, 1280
C_QS, C_KSB, C_VSB, C_QF, C_KF, C_VF, C_FL = 1304, 1560, 1816, 2072, 2328, 2584, 2840
INC = 2844


class Buf:
    __slots__ = ("name", "t", "lw", "rd", "excl")

    def __init__(self, name, t, excl=False):
        self.name, self.t, self.lw, self.rd, self.excl = name, t, None, {}, excl

    def __getitem__(self, idx):
        return self.t[idx]


class Sched:
    NDSEM = 24

    def __init__(self, nc):
        self.nc = nc
        self.E = {"pe": nc.tensor, "act": nc.scalar, "dve": nc.vector, "pool": nc.gpsimd, "sp": nc.sync}
        self.sems, self.cnt = {}, {}
        for k in self.E:
            self.sems[k] = nc.alloc_semaphore("s_" + k)
            self.cnt[k] = 0
        for i in range(self.NDSEM):
            self.sems[("d", i)] = nc.alloc_semaphore("d%d" % i)
            self.cnt[("d", i)] = 0
        self.seen = {k: {} for k in self.E}
        self.dnext = 0
        self.ninstr = 0
        self.nwait = 0

    def _deps(self, reads, writes):
        deps = {}

        def add(kv):
            if kv is not None and deps.get(kv[0], 0) < kv[1]:
                deps[kv[0]] = kv[1]

        for b in reads:
            add(b.lw)
            if b.excl:
                for kv in b.rd.items():
                    add(kv)
        for b in writes:
            add(b.lw)
            for kv in b.rd.items():
                add(kv)
        return deps

    def _wait(self, eng, deps):
        seen, e = self.seen[eng], self.E[eng]
        for k, v in deps.items():
            if k == "pe" and eng == "pe":
                continue
            if seen.get(k, 0) >= v:
                continue
            e.wait_ge(self.sems[k], v)
            seen[k] = v
            self.nwait += 1

    def _mark(self, key, val, reads, writes):
        for b in reads:
            if b.excl:
                b.lw, b.rd = (key, val), {}
            else:
                b.rd[key] = val
        for b in writes:
            b.lw, b.rd = (key, val), {}

    def op(self, eng, fn, reads=(), writes=()):
        self._wait(eng, self._deps(reads, writes))
        ins = fn()
        self.cnt[eng] += 1
        ins.then_inc(self.sems[eng], 1)
        self._mark(eng, self.cnt[eng], reads, writes)
        self.ninstr += 1
        return ins

    def dma(self, out_ap, in_ap, reads=(), writes=(), q="sp", **kw):
        deps = self._deps(reads, writes)
        dk = ("d", self.dnext)
        self.dnext = (self.dnext + 1) % self.NDSEM
        if self.cnt[dk] > deps.get(dk, 0):
            deps[dk] = self.cnt[dk]
        self._wait(q, deps)
        ins = self.E[q].dma_start(out=out_ap, in_=in_ap, **kw)
        self.cnt[dk] += 16
        ins.then_inc(self.sems[dk], 16)
        self._mark(dk, self.cnt[dk], reads, writes)
        self.ninstr += 1
        return ins

    def barrier(self):
        deps = {k: v for k, v in self.cnt.items() if v > 0 and k != "sp"}
        for eng in self.E:
            self._wait(eng, dict(deps))


def _consts():
    bf = ml_dtypes.bfloat16
    c = {}
    half = 8
    inv = 500000.0 ** (-np.arange(half, dtype=np.float32) / half)
    ang = np.arange(SEQ, dtype=np.float32)[None, :] * inv[:, None]
    cs = np.concatenate([np.cos(ang), np.cos(ang)], 0).astype(np.float32)
    sn = np.concatenate([np.sin(ang), np.sin(ang)], 0).astype(np.float32)
    c["c_cos"], c["c_sin"] = cs.astype(bf), sn.astype(bf)
    j = np.arange(128)[:, None]
    t = np.arange(128)[None, :]
    c["c_ident"] = np.eye(128, dtype=np.float32).astype(bf)
    c["c_identf"] = np.eye(8, dtype=np.float32)
    c["c_ones"] = np.ones((128, 512), np.float32).astype(bf)
    c["c_tri"] = (j >= t).astype(np.float32).astype(bf)
    c["c_nm_incl"] = np.tile(np.where(j > t, NEG, 0.0), (1, 4)).astype(np.float32).astype(bf)
    c["c_nm_strict"] = np.tile(np.where(j >= t, NEG, 0.0), (1, 4)).astype(np.float32).astype(bf)
    c["c_nm_win"] = np.tile(np.where(j <= t, NEG, 0.0), (1, 4)).astype(np.float32).astype(bf)
    c["c_m01_strict"] = (j < t).astype(np.float32).astype(bf)
    n = np.arange(128)[:, None]
    tt = np.arange(SEQ)[None, :]
    c["c_nm_cmp"] = np.where(16 * n + 31 > tt, NEG, 0.0).astype(np.float32).astype(bf)
    starts = np.arange(127) * 16
    sel_starts = np.arange(32) * 64
    ovl = ((starts[:, None] < sel_starts[None, :] + 64) & (starts[:, None] + 32 > sel_starts[None, :]))
    vext = np.zeros((128, 33), np.float32)
    vext[:, 0] = 1.0
    vext[:127, 1:] = ovl
    c["c_vext"] = vext.astype(bf)
    onehot = (np.arange(SEQ)[None, :] // 64 == np.arange(32)[:, None]).astype(np.float32)
    c["c_blk1h"] = onehot.astype(bf)
    tq = np.arange(SEQ)
    cur = (tq // 64)[:, None]
    bid = np.arange(32)[None, :]
    forced = (bid == 0) | (bid == cur) | (bid == cur - 1)
    am = np.where(bid <= cur, np.where(forced, BIG, 0.0), -BIG).astype(np.float32)
    c["c_addmask"] = np.ascontiguousarray(am.reshape(16, 128, 32).transpose(1, 0, 2)).astype(bf)
    pl = np.zeros((4, 6, 4, 68), np.float32)
    for h in range(4):
        pl[h, 0, h, 64] = 1.0
        pl[h, 1, h, 65] = 1.0
        pl[0, 2, h, 66] = 1.0
        pl[0, 2, h, 67] = 1.0
        pl[h, 3, h, 66] = 1.0
        pl[h, 4, h, 67] = 1.0
        pl[0, 5, h, 64] = 1.0
        pl[0, 5, h, 65] = 1.0
    c["c_place"] = pl.reshape(4, 6 * 4 * 68).astype(bf)
    rot = np.zeros((128, 16), np.float32)
    for m in range(8):
        rot[m + 8, m] = -1.0
        rot[m, m + 8] = 1.0
    c["c_rot"] = rot.astype(bf)
    return c


_CONST_SPECS = None


class K:
    def __init__(self, nseq, nlayers=DEPTH, phases=("nsa", "sb", "fox", "mem", "ffn"), final=True):
        self.nseq, self.nlayers, self.phases, self.final = nseq, nlayers, phases, final
        nc = self.nc = bass.Bass("TRN2", target_bir_lowering=False)
        self.S = Sched(nc)
        self._n = 0
        self.din = {}
        self.declare_io()
        self.alloc()
        self.load_consts()
        self.prep_all()
        print("sbuf bytes remaining", nc.sbuf_bytes_remaining)
        for s in range(nseq):
            self.run_seq(s)
            self.S.barrier()
        self.finish()

    def dram_in(self, name, shape, dt=F32):
        self.din[name] = self.nc.dram_tensor(name, list(shape), dt, kind="ExternalInput").ap()
        return self.din[name]

    def declare_io(self):
        nc, L = self.nc, DEPTH
        self.dram_in("x", [self.nseq, SEQ, DM])
        self.dram_in("mem", [self.nseq, 256, DM])
        for nm, shp in [("norm_mix", [L, DM]), ("w_in", [L, DM, INC]), ("b_forget", [L, 4]),
                        ("cmp_pe_k", [L, 32, 64]), ("cmp_pe_v", [L, 32, 64]),
                        ("cmp_wk1", [L, 32, 64, 64]), ("cmp_wk2", [L, 64, 64]),
                        ("cmp_wv1", [L, 32, 64, 64]), ("cmp_wv2", [L, 64, 64]),
                        ("w_out", [L, DM, DM]), ("norm_cross", [L, DM]), ("norm_mem", [L, DM]),
                        ("w_mq", [L, DM, 256]), ("w_mk", [L, DM, 256]), ("w_mv", [L, DM, 256]),
                        ("w_mo", [L, 256, DM]), ("norm_ffn", [L, DM]), ("w_up", [L, DM, 2 * DFF]),
                        ("conv_w", [L, 3, 2 * DFF]), ("conv_b", [L, 2 * DFF]), ("w_down", [L, DFF, DM]),
                        ("norm_final", [DM])]:
            self.dram_in(nm, shp)
        for nm, arr in _consts().items():
            self.dram_in(nm, arr.shape, BF16 if arr.dtype == ml_dtypes.bfloat16 else F32)
        self.y = nc.dram_tensor("y", [self.nseq, SEQ, DM], F32, kind="ExternalOutput").ap()
        d = lambda nm, shp: nc.dram_tensor(nm, shp, BF16).ap()
        self.W = []
        for l in range(self.nlayers):
            self.W.append(dict(
                win=d("b_win%d" % l, [DM, INC]), rot=d("b_rot%d" % l, [DM, 224]),
                wout=d("b_wout%d" % l, [DM, DM]), mq=d("b_mq%d" % l, [DM, 256]),
                mk=d("b_mk%d" % l, [DM, 256]), mv=d("b_mv%d" % l, [DM, 256]),
                mo=d("b_mo%d" % l, [256, DM]), up=d("b_up%d" % l, [DM, 2 * DFF]),
                down=d("b_down%d" % l, [DFF, DM]),
                w1k=d("b_w1k%d" % l, [64, 32 * 64]), w1v=d("b_w1v%d" % l, [64, 32 * 64]),
                w2k=d("b_w2k%d" % l, [64, 64]), w2v=d("b_w2v%d" % l, [64, 64]),
                pek=d("b_pek%d" % l, [32, 64]), pev=d("b_pev%d" % l, [32, 64])))
        self.hT_d = nc.dram_tensor("b_hT", [KC, 128, SEQ], BF16).ap()
        self.hT_dbuf = Buf("hT_d", None)
        self.wbuf_d = Buf("wdram", None)

    def sb(self, name, shape, dt=F32):
        return Buf(name, self.nc.alloc_sbuf_tensor(name, list(shape), dt))

    def alloc(self):
        nc = self.nc
        self.xres_t = nc.alloc_sbuf_tensor("xres", [128, NT, DM], F32)
        self.xt = [Buf("x%d" % i, self.xres_t) for i in range(NT)]
        self.psbig = nc.alloc_psum_tensor("psbig", [128, 7 * 512], F32)
        self.ps = [Buf("ps%d" % i, self.psbig[:, i * 512:(i + 1) * 512], excl=True) for i in range(7)]
        self._pair = 0
        self.pst = Buf("pst", nc.alloc_psum_tensor("pst", [128, 1024], BF16), excl=True)
        self._rot = 0
        self._rotbanks = [0, 1, 2]
        self._pending = None
        self._deferred = []
        self.wbuf = [self.sb("wbuf%d" % i, [128, 4096], BF16) for i in range(4)]
        self._wrot = 0
        self.hTb = [self.sb("hTb%d" % i, [128, KC, TCH], BF16) for i in range(2)]
        self._hrot = 0
        self.ARENA = 22528
        self.arena = nc.alloc_sbuf_tensor("arena", [128, self.ARENA], BF16)
        self.PT = [self.sb("PT%d" % i, [128, 1024], BF16) for i in range(3)]
        self._prot = 0
        self.wf = [self.sb("wf%d" % i, [128, 514], F32) for i in range(5)]
        self._frot = 0
        self.h16 = [self.sb("h16_%d" % i, [128, DM], BF16) for i in range(2)]
        self._h16rot = 0
        self.small = [self.sb("sm%d" % i, [128, 64], F32) for i in range(4)]
        self._srot = 0
        self.o16 = [self.sb("o16_%d" % i, [128, 512], BF16) for i in range(2)]
        self._orot = 0
        self.oT = [self.sb("oT%d" % i, [128, 4, 128], BF16) for i in range(2)]
        self._otrot = 0
        self.oaccs = [self.sb("oacc%d" % i, [128, 8, 64], F32) for i in range(2)]
        self.oacc = self.oaccs[0]
        self.vcc = self.sb("vcc", [128, 2, 97], BF16)
        self.nmc = self.sb("nmc", [128, TCH], BF16)
        self.selpad = [self.sb("selpad%d" % g, [128, 96], BF16) for g in range(2)]

    def psum(self):
        rb = self._rotbanks
        self._rot = (self._rot + 1) % len(rb)
        return self.ps[rb[self._rot]]

    def pipe_unit(self, qk, pv):
        pt = qk()
        self.pipe_flush_pv()
        self._run_deferred(False)
        self._pending = (pv, pt)

    def _run_deferred(self, force):
        d, self._deferred = self._deferred, []
        keep = []
        for (f, n) in d:
            if n <= 0 or force:
                f()
            else:
                keep.append((f, n - 1))
        self._deferred = keep + self._deferred

    def pipe_flush_pv(self):
        if self._pending is not None:
            pv, pt = self._pending
            self._pending = None
            pv(pt)

    def pipe_defer(self, fn, delay=0):
        self._deferred.append((fn, delay))

    def pipe_drain(self):
        self.pipe_flush_pv()
        while self._deferred:
            self._run_deferred(True)

    def getw(self):
        b = self.wbuf[self._wrot]
        self._wrot = (self._wrot + 1) % 4
        return b

    def gethT(self):
        b = self.hTb[self._hrot]
        self._hrot = (self._hrot + 1) % 2
        return b

    def getPT(self):
        b = self.PT[self._prot]
        self._prot = (self._prot + 1) % 3
        return b

    def getf(self):
        b = self.wf[self._frot]
        self._frot = (self._frot + 1) % 5
        return b

    def getsm(self):
        b = self.small[self._srot]
        self._srot = (self._srot + 1) % 4
        return b

    def mm(self, ob, out, lb, lhsT, rb, rhs, start, stop, extra=()):
        nc = self.nc
        self.S.op("pe", lambda: nc.tensor.matmul(out, lhsT=lhsT, rhs=rhs, start=start, stop=stop,
                                                 skip_group_check=True),
                  reads=[lb, rb] + list(extra), writes=[ob])

    def act(self, ob, out, ib, in_, func, reads=(), **kw):
        nc = self.nc
        self.S.op("act", lambda: nc.scalar.activation(out=out, in_=in_, func=func, **kw),
                  reads=[ib] + list(reads), writes=[ob] if not isinstance(ob, (list, tuple)) else list(ob))

    def ts(self, eng, ob, out, ib, in0, s1, s2, op0, op1=None, reads=()):
        e = self.S.E[eng]
        if op1 is None:
            fn = lambda: e.tensor_scalar(out=out, in0=in0, scalar1=s1, scalar2=None, op0=op0)
        else:
            fn = lambda: e.tensor_scalar(out=out, in0=in0, scalar1=s1, scalar2=s2, op0=op0, op1=op1)
        self.S.op(eng, fn, reads=[ib] + list(reads), writes=[ob])

    def tt(self, eng, ob, out, ab, a, bb, b, op):
        e = self.S.E[eng]
        self.S.op(eng, lambda: e.tensor_tensor(out=out, in0=a, in1=b, op=op), reads=[ab, bb], writes=[ob])

    def stt(self, ob, out, ab, in0, scalar, bb, in1, op0, op1, reads=()):
        nc = self.nc
        self.S.op("dve", lambda: nc.vector.scalar_tensor_tensor(out=out, in0=in0, scalar=scalar, in1=in1,
                                                                op0=op0, op1=op1),
                  reads=[ab, bb] + list(reads), writes=[ob])

    def cp(self, eng, ob, out, ib, in_):
        if eng == "act":
            nc = self.nc
            self.S.op("act", lambda: nc.scalar.copy(out=out, in_=in_), reads=[ib], writes=[ob])
        else:
            e = self.S.E[eng]
            self.S.op(eng, lambda: e.tensor_copy(out=out, in_=in_), reads=[ib], writes=[ob])

    def memset(self, eng, ob, ap, val):
        e = self.S.E[eng]
        self.S.op(eng, lambda: e.memset(ap, val), writes=[ob])

    def load_consts(self):
        S = self.S
        self.C = {}
        for nm, arr in _consts().items():
            shp = list(arr.shape)
            if nm in ("c_blk1h", "c_nm_cmp"):
                continue
            b = self.sb("k_" + nm, shp, BF16 if arr.dtype == ml_dtypes.bfloat16 else F32)
            S.dma(b[:], self.din[nm], writes=[b])
            self.C[nm] = b
        L = self.nlayers
        self.gain = {}
        grow = Buf("grow", self.arena[0:8, 12288:12288 + 256].bitcast(F32))
        idf = self.C["c_identf"]
        nc = self.nc
        for nm in ("norm_mix", "norm_cross", "norm_mem", "norm_ffn"):
            g = self.sb("g_" + nm, [128, L, KC], F32)
            for l in range(L):
                S.dma(grow[:, :], self.din[nm][l].rearrange("(k p) -> k p", p=128), writes=[grow])
                ps = self.psum()
                S.op("pe", lambda ps=ps: nc.tensor.transpose(out=ps[:, 0:8], in_=grow[:, :], identity=idf[0:8, 0:8]),
                     reads=[grow, idf], writes=[ps])
                self.cp("dve", g, g[:, l, :], ps, ps[:, 0:8])
            self.gain[nm] = g
        g8 = self.sb("g8_mix", [128, L, KC], F32)
        self.ts("dve", g8, g8[:], self.gain["norm_mix"], self.gain["norm_mix"][:], 0.125, None, ALU.mult)
        self.gain["norm_mix8"] = g8
        g8c = self.sb("g8_cross", [128, L, KC], F32)
        self.ts("dve", g8c, g8c[:], self.gain["norm_cross"], self.gain["norm_cross"][:], 0.125, None, ALU.mult)
        self.gain["norm_cross8"] = g8c
        self.nbf = self.sb("nbf", [4, L], F32)
        S.dma(self.nbf[:], self.din["b_forget"][0:L].rearrange("l h -> h l"), writes=[self.nbf],
              allow_slow_non_contiguous=True)
        self.ts("dve", self.nbf, self.nbf[:], self.nbf, self.nbf[:], -1.0, None, ALU.mult)
        self.cw = self.sb("cw", [128, L, 44, 4], F32)
        crow = Buf("crow", self.arena[0:4, 0:4 * DFF].bitcast(F32))
        idf = self.C["c_identf"]
        for l in range(L):
            S.dma(crow[0:3, :], self.din["conv_w"][l], writes=[crow])
            S.dma(crow[3:4, :], self.din["conv_b"][l:l + 1, :], writes=[crow])
            for c0 in range(0, 44, 11):
                ps = self.psum()
                for cc in range(c0, c0 + 11):
                    nc = self.nc
                    S.op("pe", lambda cc=cc, ps=ps: nc.tensor.transpose(
                        out=ps[:, (cc - c0) * 4:(cc - c0) * 4 + 4], in_=crow[0:4, cc * 128:(cc + 1) * 128],
                        identity=idf[0:4, 0:4]), reads=[crow, idf], writes=[ps])
                self.cp("dve", self.cw, self.cw[:, l, c0:c0 + 11, :].rearrange("p c k -> p (c k)"), ps, ps[:, 0:44])
        for g in range(2):
            self.memset("pool", self.selpad[g], self.selpad[g][:], 0.0)
        for g in range(2):
            self.cp("pool", self.vcc, self.vcc[:, g, 64:97], self.C["c_vext"], self.C["c_vext"][:, :])
        self.ones4b = Buf("ones4b", self.C["c_ones"][0:4, :])
        S.barrier()

    def prep_all(self):
        S, nc = self.S, self.nc
        ar = self.arena
        NS = 3
        pin = [Buf("pin%d" % i, ar[:, i * 4096:(i + 1) * 4096].bitcast(F32)) for i in range(NS)]
        pout = [Buf("pout%d" % i, ar[:, 12288 + i * 2048: 12288 + (i + 1) * 2048]) for i in range(NS)]
        rott = Buf("rott", ar[:, 18432:18432 + 224])
        self._pk = 0
        engs = ["act", "dve"]

        def piece(src, dst, rows, cols, scale, after=None):
            i = self._pk % NS
            eng = engs[self._pk % 2]
            self._pk += 1
            S.dma(pin[i][0:rows, 0:cols], src, writes=[pin[i]])
            o, a = pout[i][0:rows, 0:cols], pin[i][0:rows, 0:cols]
            if isinstance(scale, tuple):
                sbuf, sap = scale
                if eng == "act":
                    self.act(pout[i], o, pin[i], a, AF.Copy, reads=[sbuf], scale=sap)
                else:
                    self.ts(eng, pout[i], o, pin[i], a, sap, None, ALU.mult, reads=[sbuf])
            else:
                if eng == "act":
                    self.act(pout[i], o, pin[i], a, AF.Copy, scale=float(scale))
                else:
                    self.ts(eng, pout[i], o, pin[i], a, float(scale), None, ALU.mult)
            if after is not None:
                after(pout[i])
            S.dma(dst, pout[i][0:rows, 0:cols], reads=[pout[i]], q="pool")

        def mat(src, dst, R, Ccols, gain=None, l=0, scale=1.0):
            for r0 in range(0, R, 128):
                rows = min(128, R - r0)
                for c0 in range(0, Ccols, 2048):
                    cols = min(2048, Ccols - c0)
                    sc = (gain, gain[0:rows, l, r0 // 128:r0 // 128 + 1]) if gain is not None else scale
                    piece(src[r0:r0 + rows, c0:c0 + cols], dst[r0:r0 + rows, c0:c0 + cols], rows, cols, sc)

        for l in range(self.nlayers):
            W, D = self.W[l], self.din
            g, g8 = self.gain["norm_mix"], self.gain["norm_mix8"]
            for rc in range(KC):
                r0 = rc * 128
                gs, g8s = (g, g[:, l, rc:rc + 1]), (g8, g8[:, l, rc:rc + 1])

                def rot_ops(src_off, nh, roff):
                    def f(pb):
                        v = pb[:, src_off:src_off + 64 * nh].rearrange("p (h d) -> p h d", d=64)
                        rt = rott[:, :].rearrange("p (h d) -> p h d", d=16)
                        self.ts("dve", rott, rt[:, roff:roff + nh, 0:8], pb, v[:, :, 8:16], -1.0, None, ALU.mult)
                        self.cp("dve", rott, rt[:, roff:roff + nh, 8:16], pb, v[:, :, 0:8])
                    return f

                def rot2(pb):
                    rot_ops(0, 2, 8)(pb)
                    rot_ops(256, 2, 10)(pb)
                    rot_ops(512, 2, 12)(pb)

                segs = [(0, 512, g8s, rot_ops(0, 8, 0)), (512, 1304, gs, rot2), (1304, 1560, g8s, None),
                        (1560, 2072, gs, None), (2072, 2328, g8s, None), (2328, 2844, gs, None)]
                for (a, b, sc, aft) in segs:
                    piece(D["w_in"][l, r0:r0 + 128, a:b], W["win"][r0:r0 + 128, a:b], 128, b - a, sc, aft)
                S.dma(W["rot"][r0:r0 + 128, :], rott[:, :], reads=[rott])
            mat(D["w_out"][l], W["wout"], DM, DM)
            mat(D["w_mq"][l], W["mq"], DM, 256, gain=self.gain["norm_cross8"], l=l)
            mat(D["w_mk"][l], W["mk"], DM, 256, gain=self.gain["norm_mem"], l=l)
            mat(D["w_mv"][l], W["mv"], DM, 256, gain=self.gain["norm_mem"], l=l)
            mat(D["w_mo"][l], W["mo"], 256, DM)
            mat(D["w_up"][l], W["up"], DM, 2 * DFF, gain=self.gain["norm_ffn"], l=l)
            mat(D["w_down"][l], W["down"], DFF, DM)
            for (sn, dn) in (("cmp_wk1", "w1k"), ("cmp_wv1", "w1v")):
                for l0 in range(0, 32, 16):
                    i = self._pk % NS
                    self._pk += 1
                    S.dma(pin[i][0:64, 0:1024].rearrange("p (l c) -> p l c", c=64),
                          D[sn][l, l0:l0 + 16].rearrange("l d c -> d l c"), writes=[pin[i]])
                    self.cp("dve", pout[i], pout[i][0:64, 0:1024], pin[i], pin[i][0:64, 0:1024])
                    S.dma(W[dn][:, l0 * 64:(l0 + 16) * 64], pout[i][0:64, 0:1024], reads=[pout[i]])
            mat(D["cmp_wk2"][l], W["w2k"], 64, 64)
            mat(D["cmp_wv2"][l], W["w2v"], 64, 64)
            mat(D["cmp_pe_k"][l], W["pek"], 32, 64)
            mat(D["cmp_pe_v"][l], W["pev"], 32, 64)
        S.barrier()

    def load_w(self, src, shape_view, nbytes_cols):
        b = self.getw()
        v = b[:, 0:nbytes_cols]
        if shape_view is not None:
            v = v.rearrange(shape_view[0], **shape_view[1])
        self.S.dma(v, src, reads=[self.wbuf_d], writes=[b])
        return b, v

    def load_wcols(self, wd, c0, ncols, rows=DM):
        nk = rows // 128
        return self.load_w(wd[:, c0:c0 + ncols].rearrange("(k p) n -> p k n", p=128),
                           ("p (k n) -> p k n", dict(k=nk)), nk * ncols)

    def rmsnorm_T(self, tiles, hb, hview, col0):
        nc, S = self.nc, self.S
        for i, t in enumerate(tiles):
            xb = self.xt[t]
            xa = self.xres_t[:, t, :]
            sm = self.getsm()
            h = self.h16[self._h16rot]
            self._h16rot ^= 1
            self.act([h, sm], h[:], xb, xa, AF.Square, accum_out=sm[:, 0:1])
            self.act(sm, sm[:, 1:2], sm, sm[:, 0:1], AF.Ln, scale=1.0 / DM, bias=EPS)
            self.act(sm, sm[:, 2:3], sm, sm[:, 1:2], AF.Exp, scale=-0.5)
            self.ts("dve", h, h[:], xb, xa, sm[:, 2:3], None, ALU.mult, reads=[sm])
            for kc in range(KC):
                S.op("pe", lambda kc=kc, h=h: nc.tensor.transpose(
                    out=self.pst[:, kc * 128:(kc + 1) * 128], in_=h[:, kc * 128:(kc + 1) * 128],
                    identity=self.C["c_ident"][:]), reads=[h, self.C["c_ident"]], writes=[self.pst])
            self.cp("act" if i % 2 else "dve", hb, hview[:, :, col0 + i * 128: col0 + (i + 1) * 128],
                    self.pst, self.pst[:, :].rearrange("p (k t) -> p k t", k=KC))

    def projT(self, dstb, dst, wb, wv, c0, M, hb, hv, ncols, evac="act", scale=None):
        ps = self.psum()
        for kc in range(KC):
            self.mm(ps, ps[0:M, 0:ncols], wb, wv[:, kc, c0:c0 + M], hb, hv[:, kc, 0:ncols], kc == 0, kc == KC - 1)
        self.rp_flush()
        if dst is not None:
            if scale is not None:
                self.act(dstb, dst, ps, ps[0:M, 0:ncols], AF.Copy, scale=scale)
            else:
                self.cp(evac, dstb, dst, ps, ps[0:M, 0:ncols])
        return ps

    _rp = None

    def rp_flush(self):
        if self._rp is not None:
            f, self._rp = self._rp, None
            f()

    def rope_rows(self, dstb, dst16, psA, src, K, t0, ncols):
        psB = self.ps[6]
        rot = self.C["c_rot"]
        self.mm(psB, psB[0:16, 0:ncols], rot, rot[0:K, :], dstb, src, True, True)
        cs, sn = self.C["c_cos"], self.C["c_sin"]
        f1, f2 = self.getf(), self.getf()
        self.tt("dve", f1, f1[0:16, 0:ncols], psA, psA[0:16, 0:ncols], cs, cs[:, t0:t0 + ncols], ALU.mult)
        self.tt("dve", f2, f2[0:16, 0:ncols], psB, psB[0:16, 0:ncols], sn, sn[:, t0:t0 + ncols], ALU.mult)
        self.tt("pool", dstb, dst16, f1, f1[0:16, 0:ncols], f2, f2[0:16, 0:ncols], ALU.add)

    def out_proj(self, o16b, ncol_o, wob, wov, tile):
        nc, S = self.nc, self.S
        nk = ncol_o // 128
        for k in range(nk):
            S.op("pe", lambda k=k: nc.tensor.transpose(
                out=self.pst[:, k * 128:(k + 1) * 128], in_=o16b[:, k * 128:(k + 1) * 128],
                identity=self.C["c_ident"][:]), reads=[o16b, self.C["c_ident"]], writes=[self.pst])
        oT = self.oT[self._otrot]
        self._otrot ^= 1
        self.cp("act", oT, oT[:, 0:nk, :], self.pst, self.pst[:, 0:nk * 128].rearrange("p (k t) -> p k t", k=nk))
        xb, xa = self.xt[tile], self.xres_t[:, tile, :]
        for n in range(2):
            ps = self.psum()
            for k in range(nk):
                self.mm(ps, ps[:, :], oT, oT[:, k, :], wob, wov[:, k, n * 512:(n + 1) * 512], k == 0, k == nk - 1)
            self.tt("dve", xb, xa[:, n * 512:(n + 1) * 512], xb, xa[:, n * 512:(n + 1) * 512], ps, ps[:, :], ALU.add)

    def load_hT(self, c):
        hb = self.gethT()
        self.S.dma(hb[:], self.hT_d[:, :, c * TCH:(c + 1) * TCH].rearrange("k p t -> p k t"),
                   reads=[self.hT_dbuf], writes=[hb])
        return hb

    def pvs(self, st, acc, out, ptb, lhsT, vb, rhs):
        self.mm(acc, out, ptb, lhsT, vb, rhs, st["first"], True)
        st["first"] = False

    def run_seq(self, s):
        S = self.S
        for t in range(NT):
            S.dma(self.xres_t[:, t, :], self.din["x"][s, t * 128:(t + 1) * 128, :], writes=[self.xt[t]])
        for l in range(self.nlayers):
            self.l = l
            self.s = s
            if any(p in self.phases for p in ("nsa", "sb", "fox")):
                self.phase_mix()
            if "mem" in self.phases:
                self.phase_mem()
            if "ffn" in self.phases:
                self.phase_ffn()
        if self.final:
            self.gfin = Buf("gfin", self.arena[:, 0:2048].bitcast(F32))
            S.dma(self.gfin[:, :], self.din["norm_final"].partition_broadcast(128), writes=[self.gfin])
        for t in range(NT):
            xb, xa = self.xt[t], self.xres_t[:, t, :]
            if self.final:
                sm = self.getsm()
                h = self.h16[self._h16rot]
                self._h16rot ^= 1
                self.act([h, sm], h[:], xb, xa, AF.Square, accum_out=sm[:, 0:1])
                self.act(sm, sm[:, 1:2], sm, sm[:, 0:1], AF.Ln, scale=1.0 / DM, bias=EPS)
                self.act(sm, sm[:, 2:3], sm, sm[:, 1:2], AF.Exp, scale=-0.5)
                self.stt(xb, xa, xb, xa, sm[:, 2:3], self.gfin, self.gfin[:, :], ALU.mult, ALU.mult, reads=[sm])
            self.S.dma(self.y[s, t * 128:(t + 1) * 128, :], xa, reads=[xb])

    ybuf = Buf("y", None)

    def finish(self):
        S = self.S
        deps = {k: v for k, v in S.cnt.items() if isinstance(k, tuple) and v > 0}
        S._wait("sp", deps)

    def phase_mix(self):
        S, nc, l = self.S, self.nc, self.l
        W = self.W[l]
        ar = self.arena
        A = lambda name, p, a, n: Buf(name, ar[0:p, a:a + n])
        kvc = A("kvc", 64, 0, 8192)
        ksT = A("ksT", 128, 8192, 4096)
        kwT = A("kwT", 128, 12288, 4096)
        vsw = A("vsw", 128, 16384, 4160)
        kcc = A("kcc", 64, 20544, 256)
        gat = Buf("gat", ar[:, 20800:20800 + 768].bitcast(F32))
        kvc_v = kvc[:, :].rearrange("p (j t) -> p j t", j=4)
        ksT_v = ksT[:, :].rearrange("p (g t) -> p g t", g=2)
        kwT_v = kwT[:, :].rearrange("p (g t) -> p g t", g=2)
        vsw_v = vsw[:, :].rearrange("p (t j d) -> p t j d", t=NT, j=4)
        kcc_v = kcc[:, :].rearrange("p (g n) -> p g n", g=2)
        do_nsa = "nsa" in self.phases
        if do_nsa:
            self.memset("pool", ksT, ksT[96:128, :], 0.0)
            self.memset("pool", kwT, kwT[64:128, :], 0.0)
            for g in range(2):
                S.dma(ksT_v[64:96, g, :], self.din["c_blk1h"], writes=[ksT])
            self.memset("pool", vsw, vsw_v[:, :, :, 64:65], 1.0)
        for c in range(NCH):
            hb = self.gethT()
            self.rmsnorm_T(range(4 * c, 4 * c + 4), hb, hb[:], 0)
            S.dma(self.hT_d[:, :, c * TCH:(c + 1) * TCH].rearrange("k p t -> p k t"), hb[:], reads=[hb],
                  writes=[self.hT_dbuf])
            if not do_nsa:
                continue
            wb, wv = self.load_wcols(W["win"], C_KC, 256)
            t0 = c * TCH

            def rp(dstb, dv, g, ps, K):
                self._rp = lambda: self.rope_rows(dstb, dv[0:16, g, t0:t0 + TCH], ps, dv[0:K, g, t0:t0 + TCH], K, t0, TCH)

            for g in range(2):
                ps = self.projT(kvc, kvc_v[0:64, g, t0:t0 + TCH], wb, wv, g * 64, 64, hb, hb[:], TCH)
                rp(kvc, kvc_v, g, ps, 64)
                self.projT(kvc, kvc_v[0:64, 2 + g, t0:t0 + TCH], wb, wv, 128 + g * 64, 64, hb, hb[:], TCH, evac="dve")
            wb, wv = self.load_wcols(W["win"], C_KS, 512)
            for g in range(2):
                ps = self.projT(ksT, ksT_v[0:64, g, t0:t0 + TCH], wb, wv, g * 64, 64, hb, hb[:], TCH)
                rp(ksT, ksT_v, g, ps, 128)
                ps = self.projT(kwT, kwT_v[0:64, g, t0:t0 + TCH], wb, wv, 256 + g * 64, 64, hb, hb[:], TCH)
                rp(kwT, kwT_v, g, ps, 128)
            for tl in range(4):
                t = 4 * c + tl
                ps = self.psum()
                if tl == 1:
                    self.rp_flush()
                for kc in range(KC):
                    self.mm(ps, ps[:, 0:128], hb, hb[:, kc, tl * 128:(tl + 1) * 128], wb, wv[:, kc, 128:256], kc == 0, kc == KC - 1)
                for kc in range(KC):
                    self.mm(ps, ps[:, 128:256], hb, hb[:, kc, tl * 128:(tl + 1) * 128], wb, wv[:, kc, 384:512], kc == 0, kc == KC - 1)
                self.cp("act", vsw, vsw_v[:, t, :, 0:64], ps, ps[:, 0:256].rearrange("p (j d) -> p j d", j=4))
        if do_nsa:
            self.nsa_compress(kvc, kvc_v, kcc, kcc_v)
            S.barrier()
            self.nsa_q(kvc, ksT, ksT_v, kwT, kwT_v, vsw, vsw_v, kcc, kcc_v, gat)
            S.barrier()
        if "sb" in self.phases:
            self.phase_sb()
            S.barrier()
        if "fox" in self.phases:
            self.phase_fox()
            S.barrier()

    def nsa_compress(self, kvc, kvc_v, kcc, kcc_v):
        S, nc, l = self.S, self.nc, self.l
        W = self.W[l]
        wb = self.getw()
        w1 = wb[0:64, 0:4096].rearrange("p (j l c) -> p j l c", j=2, l=32)
        S.dma(w1[:, 0, :, :], W["w1k"].rearrange("p (l c) -> p l c", c=64), reads=[self.wbuf_d], writes=[wb])
        S.dma(w1[:, 1, :, :], W["w1v"].rearrange("p (l c) -> p l c", c=64), reads=[self.wbuf_d], writes=[wb])
        wb2 = self.getw()
        w2 = wb2[0:64, 0:128].rearrange("p (j e) -> p j e", j=2)
        S.dma(w2[:, 0, :], W["w2k"], reads=[self.wbuf_d], writes=[wb2])
        S.dma(w2[:, 1, :], W["w2v"], reads=[self.wbuf_d], writes=[wb2])
        pen = wb2[0:32, 256:384].rearrange("p (j d) -> p j d", j=2)
        S.dma(pen[:, 0, :], W["pek"], reads=[self.wbuf_d], writes=[wb2])
        S.dma(pen[:, 1, :], W["pev"], reads=[self.wbuf_d], writes=[wb2])
        idn = self.C["c_ident"]
        for j in range(2):
            S.op("pe", lambda j=j: nc.tensor.transpose(out=self.pst[0:64, j * 32:(j + 1) * 32], in_=pen[:, j, :],
                                                       identity=idn[0:32, 0:32]), reads=[wb2, idn], writes=[self.pst])
        pet = self.getPT()
        peT = pet[0:64, 0:64].rearrange("p (j l) -> p j l", j=2)
        self.cp("dve", pet, pet[0:64, 0:64], self.pst, self.pst[0:64, 0:64])
        sm = self.getsm()
        for j in range(2):
            ps = self.psum()
            for li in range(32):
                self.mm(ps, ps[0:64, 0:1], wb, w1[:, j, li, :], pet, peT[:, j, li:li + 1], li == 0, li == 31)
            self.cp("dve", sm, sm[0:64, j:j + 1], ps, ps[0:64, 0:1])
        for j in range(2):
            for g in range(2):
                ps = self.psum()
                for li in range(32):
                    self.mm(ps, ps[0:64, 0:127], wb, w1[:, j, li, :], kvc, kvc_v[0:64, 2 * j + g, li:li + 16 * 126 + 1:16],
                            li == 0, li == 31)
                hid = self.getPT()
                self.act(hid, hid[0:64, 0:127], ps, ps[0:64, 0:127], AF.Silu, reads=[sm], bias=sm[0:64, j:j + 1], scale=1.0)
                ps2 = self.psum()
                if j == 0:
                    self.mm(ps2, ps2[0:64, 0:127], wb2, w2[:, 0, :], hid, hid[0:64, 0:127], True, True)
                    self.cp("dve", kcc, kcc_v[0:64, g, 0:127], ps2, ps2[0:64, 0:127])
                else:
                    self.mm(ps2, ps2[0:127, 0:64], hid, hid[0:64, 0:127], wb2, w2[:, 1, :], True, True)
                    self.cp("dve", self.vcc, self.vcc[0:127, g, 0:64], ps2, ps2[0:127, 0:64])

    def nsa_q(self, kvc, ksT, ksT_v, kwT, kwT_v, vsw, vsw_v, kcc, kcc_v, gat):
        S, nc, l = self.S, self.nc, self.l
        W = self.W[l]
        QnT = Buf("QnT", self.arena[0:128, 0:4096])
        Q = QnT[:, :].rearrange("p (g b h q) -> p g b h q", g=2, b=4, h=4)
        self.Qsel = Buf("Qsel", None)
        self.memset("pool", QnT, QnT[64:128, :], 0.0)
        self._rotbanks = [3, 4, 5]
        gat_v = gat[:, :].rearrange("p (t c) -> p t c", c=24)
        idn = self.C["c_ident"]
        wob, wov = None, None
        for c in range(NCH):
            hb = self.load_hT(c)
            wb, wv = self.load_wcols(W["win"], C_QN, 512)
            wgb, wgv = self.load_wcols(W["win"], C_G, 24)
            S.dma(self.nmc[:, :], self.din["c_nm_cmp"][:, c * TCH:(c + 1) * TCH], writes=[self.nmc])
            if wob is None or True:
                wob, wov = self.load_w(W["wout"][0:512, :].rearrange("(k p) n -> p k n", p=128),
                                       ("p (k n) -> p k n", dict(k=4)), 4096)
            t0 = c * TCH
            for hh in range(8):
                g, h = hh // 4, hh % 4
                ps = self.psum()
                for kc in range(KC):
                    self.mm(ps, ps[0:64, :], wb, wv[:, kc, hh * 64:(hh + 1) * 64], hb, hb[:, kc, :], kc == 0, kc == KC - 1)
                self.rp_flush()
                self.cp("act", QnT, Q[0:64, g, :, h, :], ps, ps[0:64, :].rearrange("p (b q) -> p b q", b=4))

                def rq(g=g, h=h, ps=ps):
                    psB = self.ps[6]
                    rot = self.C["c_rot"]
                    self.mm(psB, psB[0:16, :].rearrange("p (b q) -> p b q", b=4), rot, rot[:, :], QnT, Q[0:128, g, :, h, :], True, True)
                    cs, sn = self.C["c_cos"], self.C["c_sin"]
                    f1, f2 = self.getf(), self.getf()
                    self.tt("dve", f1, f1[0:16, 0:TCH], ps, ps[0:16, :], cs, cs[:, t0:t0 + TCH], ALU.mult)
                    self.tt("dve", f2, f2[0:16, 0:TCH], psB, psB[0:16, :], sn, sn[:, t0:t0 + TCH], ALU.mult)
                    self.tt("pool", QnT, Q[0:16, g, :, h, :], f1, f1[0:16, 0:TCH].rearrange("p (b q) -> p b q", b=4),
                            f2, f2[0:16, 0:TCH].rearrange("p (b q) -> p b q", b=4), ALU.add)
                self._rp = rq
            for tl in range(4):
                t = 4 * c + tl
                ps = self.psum()
                if tl == 1:
                    self.rp_flush()
                for kc in range(KC):
                    self.mm(ps, ps[:, 0:24], hb, hb[:, kc, tl * 128:(tl + 1) * 128], wgb, wgv[:, kc, 0:24], kc == 0, kc == KC - 1)
                self.act(gat, gat_v[:, t, :], ps, ps[:, 0:24], AF.Exp, scale=-1.0)
                self.ts("dve", gat, gat_v[:, t, :], gat, gat_v[:, t, :], 1.0, None, ALU.add)
                S.op("dve", lambda t=t: nc.vector.reciprocal(out=gat_v[:, t, :], in_=gat_v[:, t, :]), reads=[gat], writes=[gat])
            for bl in range(4):
                qb = 4 * c + bl
                self.nsa_qblock(qb, bl, QnT, Q, ksT, ksT_v, kwT, kwT_v, vsw, vsw_v, kcc, kcc_v, gat, gat_v)

                def fin(qb=qb, wob=wob, wov=wov, oacc=self.oaccs[qb % 2]):
                    o16 = self.o16[self._orot]
                    self._orot ^= 1
                    self.cp("dve", o16, o16[:, :], oacc, oacc[:, :, :].rearrange("p h d -> p (h d)"))
                    self.out_proj(o16, 512, wob, wov, qb)
                self.pipe_defer(fin, delay=4)
            self.pipe_drain()
        self._rotbanks = [0, 1, 2]

    def nsa_qblock(self, qb, bl, QnT, Q, ksT, ksT_v, kwT, kwT_v, vsw, vsw_v, kcc, kcc_v, gat, gat_v):
        S, nc = self.S, self.nc
        idn = self.C["c_ident"]
        oacc = self.oaccs[qb % 2]
        gvs = [gat_v[:, qb, g * 12:(g + 1) * 12].rearrange("p (h k) -> p h k", k=3) for g in range(2)]
        q64s = [Q[0:64, g, bl, :, :].rearrange("p h q -> p (h q)") for g in range(2)]
        q128s = [Q[0:128, g, bl, :, :].rearrange("p h q -> p (h q)") for g in range(2)]
        for g in range(2):
            acc = self.ps[0]
            st = {"first": True}

            def qk(g=g):
                ps = self.psum()
                self.mm(ps, ps[0:127, :], kcc, kcc_v[0:64, g, 0:127], QnT, q64s[g], True, False)
                nmc = self.nmc
                self.mm(ps, ps[0:127, :].rearrange("p (h q) -> p h q", h=4), idn, idn[0:127, 0:127], nmc,
                        nmc[0:127, bl * 128:(bl + 1) * 128].unsqueeze(1).broadcast_to([127, 4, 128]), False, True)
                pt = self.getPT()
                self.act(pt, pt[0:127, 0:512], ps, ps[0:127, :], AF.Exp)
                return pt

            def pv(pt, g=g, acc=acc, st=st):
                for h in range(4):
                    self.pvs(st, acc, acc[:, h * 97:(h + 1) * 97], pt, pt[0:127, h * 128:(h + 1) * 128],
                             self.vcc, self.vcc[0:127, g, :])

            def epi(g=g, acc=acc):
                gv = gvs[g]
                accv = acc[:, 0:388].rearrange("p (h d) -> p h d", h=4)
                sm = self.getsm()
                self.ts("dve", sm, sm[:, 0:4], acc, accv[:, :, 64], 1e-30, None, ALU.max)
                S.op("dve", lambda sm=sm: nc.vector.reciprocal(out=sm[:, 4:8], in_=sm[:, 0:4]), reads=[sm], writes=[sm])
                self.tt("dve", sm, sm[:, 8:12], sm, sm[:, 4:8], gat, gv[:, :, 0], ALU.mult)
                f = self.getf()
                fv = f[:, 0:128].rearrange("p (h j) -> p h j", h=4)
                self.tt("dve", f, fv, acc, accv[:, :, 65:97], sm, sm[:, 4:8].unsqueeze(2).broadcast_to([128, 4, 32]), ALU.mult)
                ov = oacc[:, g * 4:(g + 1) * 4, :]
                self.tt("dve", oacc, ov, acc, accv[:, :, 0:64], sm, sm[:, 8:12].unsqueeze(2).broadcast_to([128, 4, 64]), ALU.mult)
                imp = f[:, 128:160]
                self.tt("dve", f, imp, f, fv[:, 0, :], f, fv[:, 1, :], ALU.add)
                self.tt("dve", f, f[:, 160:192], f, fv[:, 2, :], f, fv[:, 3, :], ALU.add)
                self.tt("dve", f, imp, f, imp, f, f[:, 160:192], ALU.add)
                am = self.C["c_addmask"]
                self.tt("dve", f, imp, f, imp, am, am[:, qb, :], ALU.add)
                S.op("dve", lambda f=f: nc.vector.max(out=f[:, 192:200], in_=f[:, 128:160]), reads=[f], writes=[f])
                S.op("dve", lambda f=f: nc.vector.match_replace(out=f[:, 200:232], in_to_replace=f[:, 192:200],
                                                               in_values=f[:, 128:160], imm_value=-3.0e38), reads=[f], writes=[f])
                S.op("dve", lambda f=f: nc.vector.max(out=f[:, 232:240], in_=f[:, 200:232]), reads=[f], writes=[f])
                sp_ = self.selpad[g]
                self.ts("dve", sp_, sp_[:, 64:96], f, imp, f[:, 239:240], NEG, ALU.is_lt, ALU.mult)

            self.pipe_unit(qk, pv)
            self.pipe_defer(epi)

        def epi_b(g):
            sp_ = self.selpad[g]
            ps = self.psum()
            self.mm(ps, ps[0:96, 0:128], sp_, sp_[:, :], idn, idn[:, :], True, True)
            self.cp("act", self.Qsel, Q[64:96, g, bl, :, :], ps, ps[64:96, 0:128].unsqueeze(1).broadcast_to([32, 4, 128]))
        def branch(g, kbs, acc, kTb, kT_v, vcol, maskfn, extra):
            st = {"first": True}
            groups = [kbs[i:i + 2] for i in range(0, len(kbs), 2)]
            for grp in groups:
                def qk(grp=grp):
                    base = 3 + 2 * self._pair
                    self._pair ^= 1
                    banks = [self.ps[base], self.ps[base + 1]]
                    for j, kb in enumerate(grp):
                        ps = banks[j]
                        nm = maskfn(kb)
                        self.mm(ps, ps[:, :], kTb, kT_v[0:128, g, kb * 128:(kb + 1) * 128], QnT, q128s[g], True, nm is None,
                                extra=extra)
                        if nm is not None:
                            self.mm(ps, ps[:, :], idn, idn[:, :], nm, nm[:, :], False, True)
                    pt = self.getPT()
                    w = 512 * len(grp)
                    self.act(pt, pt[:, 0:w], banks[0], self.psbig[:, base * 512:base * 512 + w], AF.Exp,
                             reads=[banks[1]] if len(grp) == 2 else [])
                    return pt

                def pv(pt, grp=grp):
                    for j, kb in enumerate(grp):
                        for h in range(4):
                            self.pvs(st, acc, acc[:, h * 65:(h + 1) * 65], pt, pt[:, j * 512 + h * 128:j * 512 + (h + 1) * 128],
                                     vsw, vsw_v[:, kb, vcol, :])

                self.pipe_unit(qk, pv)

        for g in range(2):
            def wmask(kb):
                d = qb - kb
                return self.C["c_nm_incl"] if d == 0 else (self.C["c_nm_win"] if d == 4 else None)
            branch(g, list(range(max(0, qb - 4), qb + 1)), self.ps[1], kwT, kwT_v, 2 + g, wmask, [])
            self.pipe_defer(lambda g=g: self.nsa_accum(self.ps[1], gat, gvs[g], 2, g, oacc))
        for g in range(2):
            epi_b(g)
        for g in range(2):
            def smask(kb):
                return self.C["c_nm_incl"] if kb == qb else None
            branch(g, list(range(qb + 1)), self.ps[2], ksT, ksT_v, g, smask, [self.Qsel])
            self.pipe_defer(lambda g=g: self.nsa_accum(self.ps[2], gat, gvs[g], 1, g, oacc))

    def nsa_accum(self, acc, gat, gv, k, g, oacc):
        S, nc = self.S, self.nc
        accv = acc[:, 0:260].rearrange("p (h d) -> p h d", h=4)
        sm = self.getsm()
        S.op("dve", lambda: nc.vector.reciprocal(out=sm[:, 0:4], in_=accv[:, :, 64]), reads=[acc], writes=[sm])
        self.tt("dve", sm, sm[:, 4:8], sm, sm[:, 0:4], gat, gv[:, :, k], ALU.mult)
        f = self.getf()
        fv = f[:, 0:256].rearrange("p (h d) -> p h d", h=4)
        self.tt("dve", f, fv, acc, accv[:, :, 0:64], sm, sm[:, 4:8].unsqueeze(2).broadcast_to([128, 4, 64]), ALU.mult)
        ov = oacc[:, g * 4:(g + 1) * 4, :]
        self.tt("dve", oacc, ov, oacc, ov, f, fv, ALU.add)

    def phase_sb(self):
        S, nc, l = self.S, self.nc, self.l
        W = self.W[l]
        ar = self.arena
        kT = Buf("sbk", ar[0:128, 0:4096])
        kT_v = kT[:, :].rearrange("p (h t) -> p h t", h=2)
        vb = Buf("sbv", ar[:, 8192:8192 + 4096])
        v_v = vb[:, :].rearrange("p (t c) -> p t c", t=NT)
        qT = Buf("sbq", ar[0:128, 12288:12288 + 2048])
        q_v = qT[:, :].rearrange("p (h t) -> p h t", h=4)
        self.memset("pool", qT, qT[:, :], 0.0)
        lacc = Buf("lacc", ar[:, 14336:14336 + 1024].bitcast(F32))
        lacc16 = [Buf("lacc16_%d" % i, ar[:, 15360 + i * 512:15360 + (i + 1) * 512]) for i in range(3)]
        l16 = [Buf("l16_%d" % i, ar[:, 16896 + i * 512:16896 + (i + 1) * 512]) for i in range(2)]
        idn, tri, ones = self.C["c_ident"], self.C["c_tri"], self.C["c_ones"]
        self._rotbanks = [0, 1, 2, 5, 6]
        for c in range(NCH):
            hb = self.load_hT(c)
            wb, wv = self.load_wcols(W["win"], C_KSB, 512)
            for p in range(2):
                self.projT(kT, kT_v[0:128, p, c * TCH:(c + 1) * TCH], wb, wv, p * 128, 128, hb, hb[:], TCH,
                           evac="act" if p % 2 else "dve")
            for tl in range(4):
                ps = self.psum()
                for kc in range(KC):
                    self.mm(ps, ps[:, 0:256], hb, hb[:, kc, tl * 128:(tl + 1) * 128], wb, wv[:, kc, 256:512], kc == 0, kc == KC - 1)
                self.cp("act", vb, v_v[:, 4 * c + tl, :], ps, ps[:, 0:256])
        for c in range(NCH):
            hb = self.load_hT(c)
            wb, wv = self.load_wcols(W["win"], C_QS, 256)
            wob, wov = self.load_w(W["wout"][512:768, :].rearrange("(k p) n -> p k n", p=128),
                                   ("p (k n) -> p k n", dict(k=2)), 2048)
            for p in range(2):
                ps = self.projT(None, None, wb, wv, p * 128, 128, hb, hb[:], TCH)
                self.cp("act", qT, q_v[0:64, 2 * p, :], ps, ps[0:64, :])
                self.cp("dve", qT, q_v[64:128, 2 * p + 1, :], ps, ps[64:128, :])
            o16s = [self.sb_dummy(i) for i in range(4)]
            units = []
            for h in range(4):
                kbs = list(range(4 * c + 3, -1, -1))
                for j, kb in enumerate(kbs):
                    units.append(dict(h=h, kb=kb, first=j == 0, last=j == len(kbs) - 1,
                                      off=max(0, (kb - 4 * c) * 128), diag=kb >= 4 * c,
                                      acc=self.ps[3 + (h % 2)], st=None))
            sts = {}
            n = len(units)

            def stageA(i):
                u = units[i]
                h, kb, off = u["h"], u["kb"], u["off"]
                if u["first"]:
                    self.memset("pool", lacc, lacc[:, :], 0.0)
                    sts[h] = {"first": True}
                ks = kT_v[0:128, h // 2, kb * 128:(kb + 1) * 128]
                ps1 = self.psum()
                self.mm(ps1, ps1[:, off:512], kT, ks, qT, q_v[0:128, h, off:512], True, True)
                sp = self.getf()
                self.act(sp, sp[:, off:512], ps1, ps1[:, off:512], AF.Exp, scale=-1.0)
                self.act(sp, sp[:, off:512], sp, sp[:, off:512], AF.Ln, bias=1.0, scale=1.0)
                lb = l16[i % 2]
                self.stt(lb, lb[:, off:512], ps1, ps1[:, off:512], -1.0, sp, sp[:, off:512], ALU.mult, ALU.subtract)
                if u["diag"]:
                    m01 = self.C["c_m01_strict"]
                    self.tt("pool", lb, lb[:, off:off + 128], lb, lb[:, off:off + 128], m01, m01[:, :], ALU.mult)
                if not u["last"]:
                    self.tt("dve", lacc, lacc[:, off:512], lacc, lacc[:, off:512], lb, lb[:, off:512], ALU.add)
                    la = lacc16[i % 3]
                    self.cp("dve", la, la[:, :], lacc, lacc[:, :])

            def stageB(i):
                u = units[i]
                h, kb, off = u["h"], u["kb"], u["off"]
                ks = kT_v[0:128, h // 2, kb * 128:(kb + 1) * 128]
                lb = l16[i % 2]
                ps2 = self.psum()
                grp = [(ps2[:, off:512], kT, ks, qT, q_v[0:128, h, off:512]),
                       (ps2[:, off:512], tri, tri[:, :], lb, lb[:, off:512])]
                if not u["first"]:
                    la = lacc16[(i - 1) % 3]
                    grp.append((ps2[:, off:512], ones, ones[:, 0:128], la, la[:, off:512]))
                if u["diag"]:
                    nm = self.C["c_nm_strict"]
                    grp.append((ps2[:, off:off + 128], idn, idn[:, :], nm, nm[:, 0:128]))
                for gi, (o_, lb_, l_, rb_, r_) in enumerate(grp):
                    self.mm(ps2, o_, lb_, l_, rb_, r_, gi == 0, gi == len(grp) - 1)
                pt = self.getPT()
                self.act(pt, pt[:, off:512], ps2, ps2[:, off:512], AF.Exp)
                u["pt"] = pt

            def stageC(i):
                u = units[i]
                h, kb, off, acc, pt = u["h"], u["kb"], u["off"], u["acc"], u["pt"]
                for qbl in range(off // 128, 4):
                    self.pvs(sts[h], acc, acc[:, qbl * 64:(qbl + 1) * 64], pt, pt[:, qbl * 128:(qbl + 1) * 128],
                             vb, v_v[:, kb, h * 64:(h + 1) * 64])
                if u["last"]:
                    for qbl in range(4):
                        self.cp("dve", o16s[qbl], o16s[qbl][:, h * 64:(h + 1) * 64], acc, acc[:, qbl * 64:(qbl + 1) * 64])

            for i in range(n + 2):
                if i < n:
                    stageA(i)
                if 0 <= i - 1 < n:
                    stageB(i - 1)
                if 0 <= i - 2 < n:
                    stageC(i - 2)
            for qbl in range(4):
                self.out_proj(o16s[qbl], 256, wob, wov, 4 * c + qbl)
        self._rotbanks = [0, 1, 2]

    def sb_dummy(self, i):
        if not hasattr(self, "_o4"):
            self._o4 = [Buf("o4_%d" % j, self.o16[j // 2][:, (j % 2) * 256:(j % 2 + 1) * 256]) for j in range(4)]
        return self._o4[i]

    def phase_fox(self):
        S, nc, l = self.S, self.nc, self.l
        W = self.W[l]
        ar = self.arena
        kT = Buf("fxk", ar[0:128, 0:8192])
        kT_v = kT[:, :].rearrange("p (h t) -> p h t", h=4)
        self.memset("pool", kT, kT[64:128, :], 0.0)
        vb = Buf("fxv", ar[:, 8192:8192 + 4160])
        v_v = vb[:, :].rearrange("p (t h d) -> p t h d", t=NT, h=4)
        qT = Buf("fxq", ar[0:128, 12352:12352 + 2048])
        q_v = qT[:, :].rearrange("p (h t) -> p h t", h=4)
        self.memset("pool", qT, qT[64:128, :], 0.0)
        csp = Buf("csp", ar[0:4, 14400:14400 + 4096].bitcast(F32))
        hi = Buf("hi", ar[0:4, 18496:18496 + 2048])
        nhi = Buf("nhi", ar[0:4, 22016:22016 + 512])
        idn = self.C["c_ident"]
        place = self.C["c_place"][:, :].rearrange("p (k h m) -> p k h m", k=6, h=4)
        plb = self.C["c_place"]
        self.memset("pool", vb, v_v[:, :, :, 64:65], 1.0)
        self._rotbanks = [0, 1, 2, 5, 6]
        for c in range(NCH):
            hb = self.load_hT(c)
            wb, wv = self.load_wcols(W["win"], C_KF, 512)
            wfb, wfv = self.load_wcols(W["win"], C_FL, 4)
            t0 = c * TCH
            for h in range(4):
                self.projT(kT, kT_v[0:64, h, t0:t0 + TCH], wb, wv, h * 64, 64, hb, hb[:], TCH,
                           evac="act" if h % 2 else "dve")
            for tl in range(4):
                ps = self.psum()
                for kc in range(KC):
                    self.mm(ps, ps[:, 0:256], hb, hb[:, kc, tl * 128:(tl + 1) * 128], wb, wv[:, kc, 256:512], kc == 0, kc == KC - 1)
                self.cp("act", vb, v_v[:, 4 * c + tl, :, 0:64], ps, ps[:, 0:256].rearrange("p (h d) -> p h d", h=4))
            ps = self.psum()
            for kc in range(KC):
                self.mm(ps, ps[0:4, :], wfb, wfv[:, kc, 0:4], hb, hb[:, kc, :], kc == 0, kc == KC - 1)
            e, sp = self.getf(), self.getf()
            self.act(e, e[0:4, 0:TCH], ps, ps[0:4, :], AF.Exp, reads=[self.nbf], scale=-1.0, bias=self.nbf[:, self.l:self.l + 1])
            self.act(sp, sp[0:4, 0:TCH], e, e[0:4, 0:TCH], AF.Ln, bias=1.0, scale=1.0)
            init = 0.0 if c == 0 else csp[:, t0 - 1:t0]
            S.op("dve", lambda init=init, sp=sp, t0=t0: nc.vector.tensor_tensor_scan(
                out=csp[:, t0:t0 + TCH], data0=self.ones4b[:, :], data1=sp[0:4, 0:TCH], initial=init,
                op0=ALU.mult, op1=ALU.add), reads=[self.ones4b, sp, csp], writes=[csp])
            self.cp("dve", hi, hi[:, t0:t0 + TCH], csp, csp[:, t0:t0 + TCH])
            f = self.getf()
            self.tt("dve", f, f[0:4, 0:TCH], csp, csp[:, t0:t0 + TCH], hi, hi[:, t0:t0 + TCH], ALU.subtract)
            lo16 = self.getPT()
            self.cp("dve", lo16, lo16[0:4, 0:TCH], f, f[0:4, 0:TCH])
            for h in range(4):
                ps = self.psum()
                self.mm(ps, ps[0:68, :], plb, place[:, 3, h, :], hi, hi[:, t0:t0 + TCH], True, False)
                self.mm(ps, ps[0:68, :], plb, place[:, 4, h, :], lo16, lo16[0:4, 0:TCH], False, False)
                self.mm(ps, ps[0:68, :], plb, place[:, 5, h, :], self.ones4b, self.ones4b[:, :], False, True)
                self.cp("act", kT, kT_v[64:68, h, t0:t0 + TCH], ps, ps[64:68, :])
        for c in range(NCH):
            hb = self.load_hT(c)
            t0 = c * TCH
            wb, wv = self.load_wcols(W["win"], C_QF, 256)
            wob, wov = self.load_w(W["wout"][768:1024, :].rearrange("(k p) n -> p k n", p=128),
                                   ("p (k n) -> p k n", dict(k=2)), 2048)
            for h in range(4):
                self.projT(qT, q_v[0:64, h, :], wb, wv, h * 64, 64, hb, hb[:], TCH, evac="act" if h % 2 else "dve")
            self.ts("dve", nhi, nhi[:, 0:TCH], hi, hi[:, t0:t0 + TCH], -1.0, None, ALU.mult)
            f = self.getf()
            self.tt("dve", f, f[0:4, 0:TCH], hi, hi[:, t0:t0 + TCH], csp, csp[:, t0:t0 + TCH], ALU.subtract)
            nlo = self.getPT()
            self.cp("dve", nlo, nlo[0:4, 0:TCH], f, f[0:4, 0:TCH])
            for h in range(4):
                ps = self.psum()
                self.mm(ps, ps[0:68, :], plb, place[:, 0, h, :], nhi, nhi[:, 0:TCH], True, False)
                self.mm(ps, ps[0:68, :], plb, place[:, 1, h, :], nlo, nlo[0:4, 0:TCH], False, False)
                self.mm(ps, ps[0:68, :], plb, place[:, 2, h, :], self.ones4b, self.ones4b[:, :], False, True)
                self.cp("act", qT, q_v[64:68, h, :], ps, ps[64:68, :])
            o16s = [self.sb_dummy(i) for i in range(4)]
            for h in range(4):
                acc = self.ps[3 + (h % 2)]
                st = {"first": True}
                for kb in range(4 * c + 3, -1, -1):
                    off = max(0, (kb - 4 * c) * 128)
                    diag = kb >= 4 * c

                    def qk(h=h, kb=kb, off=off, diag=diag):
                        ps = self.psum()
                        self.mm(ps, ps[:, off:512], kT, kT_v[0:128, h, kb * 128:(kb + 1) * 128], qT, q_v[0:128, h, off:512], True, not diag)
                        if diag:
                            nm = self.C["c_nm_incl"]
                            self.mm(ps, ps[:, off:off + 128], idn, idn[:, :], nm, nm[:, 0:128], False, True)
                        pt = self.getPT()
                        self.act(pt, pt[:, off:512], ps, ps[:, off:512], AF.Exp)
                        return pt

                    def pv(pt, h=h, kb=kb, off=off, acc=acc, st=st):
                        for qbl in range(off // 128, 4):
                            self.pvs(st, acc, acc[:, qbl * 65:(qbl + 1) * 65], pt, pt[:, qbl * 128:(qbl + 1) * 128],
                                     vb, v_v[:, kb, h, :])

                    self.pipe_unit(qk, pv)

                def epi(h=h, acc=acc):
                    accv = acc[:, 0:260].rearrange("p (b d) -> p b d", b=4)
                    sm = self.getsm()
                    S.op("dve", lambda: nc.vector.reciprocal(out=sm[:, 0:4], in_=accv[:, :, 64]), reads=[acc], writes=[sm])
                    for qbl in range(4):
                        self.ts("dve", o16s[qbl], o16s[qbl][:, h * 64:(h + 1) * 64], acc, accv[:, qbl, 0:64],
                                sm[:, qbl:qbl + 1], None, ALU.mult, reads=[sm])
                self.pipe_defer(epi)
            self.pipe_drain()
            for qbl in range(4):
                self.out_proj(o16s[qbl], 256, wob, wov, 4 * c + qbl)
        self._rotbanks = [0, 1, 2]

    def phase_mem(self):
        S, nc, l = self.S, self.nc, self.l
        W = self.W[l]
        ar = self.arena
        mx = Buf("mx", ar[:, 0:4096].bitcast(F32))
        mx_v = mx[:, :].rearrange("p (t d) -> p t d", t=2)
        mT = Buf("mT", ar[:, 4096:4096 + 2048])
        mT_v = mT[:, :].rearrange("p (k t) -> p k t", k=KC)
        kT = Buf("mk", ar[0:128, 6144:6144 + 512])
        kT_v = kT[:, :].rearrange("p (h t) -> p h t", h=2)
        vb = Buf("mv", ar[:, 7168:7168 + 520])
        v_v = vb[:, :].rearrange("p (t h d) -> p t h d", t=2, h=4)
        qT = Buf("mq", ar[0:128, 7688:7688 + 2048])
        q_v = qT[:, :].rearrange("p (h t) -> p h t", h=4)
        self.memset("pool", qT, qT[:, :], 0.0)
        idn = self.C["c_ident"]
        for t in range(2):
            S.dma(mx_v[:, t, :], self.din["mem"][self.s, t * 128:(t + 1) * 128, :], writes=[mx])
        self.memset("pool", vb, v_v[:, :, :, 64:65], 1.0)
        for t in range(2):
            sm = self.getsm()
            h = self.h16[self._h16rot]
            self._h16rot ^= 1
            self.act([h, sm], h[:], mx, mx_v[:, t, :], AF.Square, accum_out=sm[:, 0:1])
            self.act(sm, sm[:, 1:2], sm, sm[:, 0:1], AF.Ln, scale=1.0 / DM, bias=EPS)
            self.act(sm, sm[:, 2:3], sm, sm[:, 1:2], AF.Exp, scale=-0.5)
            self.ts("dve", h, h[:], mx, mx_v[:, t, :], sm[:, 2:3], None, ALU.mult, reads=[sm])
            for kc in range(KC):
                S.op("pe", lambda kc=kc, h=h: nc.tensor.transpose(
                    out=self.pst[:, kc * 128:(kc + 1) * 128], in_=h[:, kc * 128:(kc + 1) * 128],
                    identity=idn[:]), reads=[h, idn], writes=[self.pst])
            self.cp("dve", mT, mT_v[:, :, t * 128:(t + 1) * 128], self.pst, self.pst[:, :].rearrange("p (k t) -> p k t", k=KC))
        wb, wv = self.load_wcols(W["mk"], 0, 256)
        for p in range(2):
            self.projT(kT, kT_v[0:128, p, :], wb, wv, p * 128, 128, mT, mT_v, 256)
        wb, wv = self.load_wcols(W["mv"], 0, 256)
        for t in range(2):
            ps = self.psum()
            for kc in range(KC):
                self.mm(ps, ps[:, 0:256], mT, mT_v[:, kc, t * 128:(t + 1) * 128], wb, wv[:, kc, :], kc == 0, kc == KC - 1)
            self.cp("act", vb, v_v[:, t, :, 0:64], ps, ps[:, 0:256].rearrange("p (h d) -> p h d", h=4))
        wqb, wqv = self.load_wcols(W["mq"], 0, 256)
        wob, wov = self.load_w(W["mo"].rearrange("(k p) n -> p k n", p=128), ("p (k n) -> p k n", dict(k=2)), 2048)
        for c in range(NCH):
            hb = self.gethT()
            self.rmsnorm_T(range(4 * c, 4 * c + 4), hb, hb[:], 0)
            for p in range(2):
                ps = self.projT(None, None, wqb, wqv, p * 128, 128, hb, hb[:], TCH)
                self.cp("act", qT, q_v[0:64, 2 * p, :], ps, ps[0:64, :])
                self.cp("dve", qT, q_v[64:128, 2 * p + 1, :], ps, ps[64:128, :])
            o16s = [self.sb_dummy(i) for i in range(4)]
            for h in range(4):
                acc = self.ps[3 + (h % 2)]
                st = {"first": True}
                for kb in range(2):
                    def qk(h=h, kb=kb):
                        ps = self.psum()
                        self.mm(ps, ps[:, :], kT, kT_v[0:128, h // 2, kb * 128:(kb + 1) * 128], qT, q_v[0:128, h, :], True, True)
                        pt = self.getPT()
                        self.act(pt, pt[:, 0:512], ps, ps[:, :], AF.Exp)
                        return pt

                    def pv(pt, h=h, kb=kb, acc=acc, st=st):
                        for qbl in range(4):
                            self.pvs(st, acc, acc[:, qbl * 65:(qbl + 1) * 65], pt, pt[:, qbl * 128:(qbl + 1) * 128], vb, v_v[:, kb, h, :])

                    self.pipe_unit(qk, pv)

                def epi(h=h, acc=acc):
                    accv = acc[:, 0:260].rearrange("p (b d) -> p b d", b=4)
                    sm = self.getsm()
                    S.op("dve", lambda: nc.vector.reciprocal(out=sm[:, 0:4], in_=accv[:, :, 64]), reads=[acc], writes=[sm])
                    for qbl in range(4):
                        self.ts("dve", o16s[qbl], o16s[qbl][:, h * 64:(h + 1) * 64], acc, accv[:, qbl, 0:64],
                                sm[:, qbl:qbl + 1], None, ALU.mult, reads=[sm])
                self.pipe_defer(epi)
            self.pipe_drain()
            for qbl in range(4):
                self.out_proj(o16s[qbl], 256, wob, wov, 4 * c + qbl)
        S.barrier()

    def phase_ffn(self):
        S, nc, l = self.S, self.nc, self.l
        W = self.W[l]
        ar = self.arena
        gT = Buf("gT", ar[:, 0:11264])
        g_v = gT[:, :].rearrange("p (k t) -> p k t", k=22)
        halo = Buf("halo", ar[:, 11264:11264 + 176].bitcast(F32))
        halo_v = halo[:, :].rearrange("p (c k) -> p c k", k=2)
        self.memset("pool", halo, halo[:, :], 0.0)
        cw = self.cw
        for c in range(NCH):
            hb = self.gethT()
            self.rmsnorm_T(range(4 * c, 4 * c + 4), hb, hb[:], 0)
            wcur = {}
            uy = {}

            def stage1(cc):
                cg, ci = cc // 4, cc % 4
                if ci == 0:
                    wcur["w"] = self.load_wcols(W["up"], cg * 512, 512)
                wb, wv = wcur["w"]
                ps = self.psum()
                for kc in range(KC):
                    self.mm(ps, ps[:, :], wb, wv[:, kc, ci * 128:(ci + 1) * 128], hb, hb[:, kc, :], kc == 0, kc == KC - 1)
                u = self.getf()
                y = self.getf()
                self.cp("pool", u, u[:, 0:2], halo, halo_v[:, cc, :])
                self.cp("act", u, u[:, 2:514], ps, ps[:, :])
                self.act(y, y[:, 0:512], ps, ps[:, :], AF.Copy, reads=[cw], scale=cw[:, l, cc, 2:3])
                self.cp("pool", halo, halo_v[:, cc, :], u, u[:, 512:514])
                uy[cc] = [u, y]

            def stage2(cc):
                u, y = uy[cc]
                self.stt(y, y[:, 0:512], u, u[:, 1:513], cw[:, l, cc, 1:2], y, y[:, 0:512], ALU.mult, ALU.add, reads=[cw])
                self.stt(y, y[:, 0:512], u, u[:, 0:512], cw[:, l, cc, 0:1], y, y[:, 0:512], ALU.mult, ALU.add, reads=[cw])

            def stage3(cc):
                y = uy.pop(cc)[1]
                if cc < 22:
                    self.act(gT, g_v[:, cc, :], y, y[:, 0:512], AF.Silu, reads=[cw], bias=cw[:, l, cc, 3:4], scale=1.0)
                else:
                    self.stt(gT, g_v[:, cc - 22, :], y, y[:, 0:512], cw[:, l, cc, 3:4], gT, g_v[:, cc - 22, :],
                             ALU.add, ALU.mult, reads=[cw])

            for i in range(44 + 2):
                if i < 44:
                    stage1(i)
                if 0 <= i - 1 < 44:
                    stage2(i - 1)
                if 0 <= i - 2 < 44:
                    stage3(i - 2)
            for n in range(2):
                accs = [self.ps[3 + tl] for tl in range(4)]
                for pc in range(6):
                    k0 = pc * 4
                    nk = min(4, 22 - k0)
                    wdb, wdv = self.load_w(W["down"][k0 * 128:(k0 + nk) * 128, n * 512:(n + 1) * 512].rearrange("(k p) n -> p k n", p=128),
                                           ("p (k n) -> p k n", dict(k=nk)), nk * 512)
                    for tl in range(4):
                        for k in range(nk):
                            self.mm(accs[tl], accs[tl][:, :], gT, g_v[:, k0 + k, tl * 128:(tl + 1) * 128], wdb, wdv[:, k, :],
                                    k0 + k == 0, k0 + k == 21)
                for tl in range(4):
                    t = 4 * c + tl
                    xb, xa = self.xt[t], self.xres_t[:, t, :]
                    self.tt("dve", xb, xa[:, n * 512:(n + 1) * 512], xb, xa[:, n * 512:(n + 1) * 512], accs[tl], accs[tl][:, :], ALU.add)
        S.barrier()


_PROG = {}


def _get_prog(nseq):
    if nseq not in _PROG:
        _PROG[nseq] = K(nseq)
    return _PROG[nseq]


def kernel(**inputs):
    x = np.ascontiguousarray(np.asarray(inputs["x"], dtype=np.float32))
    mem = np.ascontiguousarray(np.asarray(inputs["mem"], dtype=np.float32))
    B = x.shape[0]
    ncores = 8
    per = B // ncores
    prog = _get_prog(per)
    consts = _consts()
    in_maps = []
    for i in range(ncores):
        m = {"x": x[i * per:(i + 1) * per], "mem": mem[i * per:(i + 1) * per]}
        for k, v in inputs.items():
            if k not in ("x", "mem"):
                m[k] = np.ascontiguousarray(np.asarray(v, dtype=np.float32))
        m.update(consts)
        in_maps.append(m)
    res = run_bass_kernel_spmd(prog.nc, in_maps, core_ids=list(range(ncores)))
    return np.concatenate([np.asarray(r["y"], dtype=np.float32) for r in res.results], axis=0)
```
